# Optimizing a Trainium2 kernel written in Bass

```python
import math
import jax, jax.numpy as jnp
from jax import lax
import numpy as np

D_MODEL = 1024
BATCH = 2
SEQ = 8192
DEPTH = 1

A_HEADS = 8
A_HEAD_DIM = 64
A_WIDTH = A_HEADS * A_HEAD_DIM
DILATED_PATTERNS = ((128, 1), (512, 4), (2048, 16))
BAND_BLOCK = 128
REL_BUCKETS = 32
REL_MAX_DIST = 2048
M_HEADS = 8
M_NOPE = 64
M_ROPE = 32
M_V = 64
M_Q_LORA = 768
M_KV_LORA = 256
M_WIDTH = M_HEADS * M_V
ROPE_THETA = 10000.0
Q_BLOCK = 128
D_FF = -(-8 * D_MODEL // (3 * 256)) * 256
N_MOD = 6
EPS = 1e-6
NEG_INF = -1e30
IN_SIZES = (3 * A_WIDTH, M_Q_LORA, M_KV_LORA, M_ROPE, D_MODEL, D_MODEL)
IN_WIDTH = 3 * A_WIDTH + M_Q_LORA + M_KV_LORA + M_ROPE + 2 * D_MODEL
IN_SPLIT_POINTS = (3 * A_WIDTH,
                   3 * A_WIDTH + M_Q_LORA,
                   3 * A_WIDTH + M_Q_LORA + M_KV_LORA,
                   3 * A_WIDTH + M_Q_LORA + M_KV_LORA + M_ROPE,
                   3 * A_WIDTH + M_Q_LORA + M_KV_LORA + M_ROPE + D_MODEL)

kernel_name = "hybrid_dilated_mla_gated_block"


def _rmsnorm(x, g):
    xf = x.astype(jnp.float32)
    inv = lax.rsqrt(jnp.mean(xf * xf, axis=-1, keepdims=True) + EPS)
    return (xf * inv).astype(x.dtype) * g


def _t5_bucket(dist):
    max_exact = REL_BUCKETS // 2
    d = jnp.maximum(dist, 1).astype(jnp.float32)
    log_b = max_exact + (jnp.log(d / max_exact) / math.log(REL_MAX_DIST / max_exact)
                         * (REL_BUCKETS - max_exact)).astype(jnp.int32)
    log_b = jnp.minimum(log_b, REL_BUCKETS - 1)
    return jnp.where(dist < max_exact, dist, log_b)


def _dilated_pattern(q, k, v, rel_bias, window, dilation):
    B, H, S, Dh = q.shape
    n_back = window // dilation
    blk = BAND_BLOCK
    L = S // dilation
    pad_end = (-L) % blk
    Lp = L + pad_end
    nb = Lp // blk

    def to_sub(t):
        t = t.reshape(B, H, L, dilation, Dh).transpose(0, 1, 3, 2, 4)
        return jnp.pad(t, ((0, 0), (0, 0), (0, 0), (0, pad_end), (0, 0)))

    def band(t):
        tp = jnp.pad(t, ((0, 0), (0, 0), (0, 0), (blk, 0), (0, 0)))
        prev = tp[:, :, :, :Lp].reshape(B, H, dilation, nb, blk, Dh)
        cur = t.reshape(B, H, dilation, nb, blk, Dh)
        return jnp.concatenate([prev, cur], axis=4)

    qb = to_sub(q).reshape(B, H, dilation, nb, blk, Dh)
    kb = band(to_sub(k))
    vb = band(to_sub(v))
    s = jnp.einsum('bhrnqd,bhrnkd->bhrnqk', qb, kb).astype(jnp.float32) * (Dh ** -0.5)
    sub_dist = (jnp.arange(blk)[:, None] + blk) - jnp.arange(2 * blk)[None, :]
    in_band = (sub_dist >= 0) & (sub_dist <= n_back)
    bucket = _t5_bucket(jnp.clip(sub_dist, 0, n_back) * dilation)
    bias = rel_bias[:, bucket].astype(jnp.float32)
    key_idx = jnp.arange(nb)[:, None] * blk + jnp.arange(2 * blk)[None, :] - blk
    mask = in_band[None] & (key_idx >= 0)[:, None, :]
    s = jnp.where(mask, s + bias[None, :, None, None], NEG_INF)
    lse = jax.nn.logsumexp(s, axis=-1)
    p = jnp.exp(s - lse[..., None])
    o = jnp.einsum('bhrnqk,bhrnkd->bhrnqd', p.astype(v.dtype), vb)
    o = o.reshape(B, H, dilation, Lp, Dh)[:, :, :, :L].transpose(0, 1, 3, 2, 4).reshape(B, H, S, Dh)
    lse = lse.reshape(B, H, dilation, Lp)[:, :, :, :L].transpose(0, 1, 3, 2).reshape(B, H, S)
    return o, lse


def _dilated_attention(a_qkv, rel_bias):
    B, S, _ = a_qkv.shape
    qkv = a_qkv.reshape(B, S, 3, A_HEADS, A_HEAD_DIM).transpose(2, 0, 3, 1, 4)
    q, k, v = qkv[0], qkv[1], qkv[2]
    outs, lses = [], []
    for window, dilation in DILATED_PATTERNS:
        o, l = _dilated_pattern(q, k, v, rel_bias, window, dilation)
        outs.append(o)
        lses.append(l)
    w = jax.nn.softmax(jnp.stack(lses), axis=0)
    o = jnp.einsum('gbhs,gbhsd->bhsd', w.astype(v.dtype), jnp.stack(outs))
    return o.transpose(0, 2, 1, 3).reshape(B, S, A_WIDTH)


def _rope(x, pos):
    half = x.shape[-1] // 2
    freqs = ROPE_THETA ** (-jnp.arange(half, dtype=jnp.float32) / half)
    ang = pos.astype(jnp.float32)[..., None] * freqs
    ang = ang.reshape(ang.shape[:2] + (1,) * (x.ndim - 3) + (half,))
    cos, sin = jnp.cos(ang), jnp.sin(ang)
    x1 = x[..., :half].astype(jnp.float32)
    x2 = x[..., half:].astype(jnp.float32)
    return jnp.concatenate([x1 * cos - x2 * sin, x1 * sin + x2 * cos], axis=-1).astype(x.dtype)


def _mla(c_q, c_kv, k_r, positions, g_q_lora, w_uq, g_kv_lora, w_ukv):
    B, S, _ = c_q.shape
    q = jnp.einsum('bsl,lhd->bshd', _rmsnorm(c_q, g_q_lora), w_uq)
    kv = jnp.einsum('bsl,lhd->bshd', _rmsnorm(c_kv, g_kv_lora), w_ukv)
    q_nope, q_rope = q[..., :M_NOPE], _rope(q[..., M_NOPE:], positions)
    k_nope, v = kv[..., :M_NOPE], kv[..., M_NOPE:]
    k_rope = _rope(k_r, positions)
    scale = (M_NOPE + M_ROPE) ** -0.5
    nq = S // Q_BLOCK
    qn_b = jnp.moveaxis(q_nope.reshape(B, nq, Q_BLOCK, M_HEADS, M_NOPE), 1, 0)
    qr_b = jnp.moveaxis(q_rope.reshape(B, nq, Q_BLOCK, M_HEADS, M_ROPE), 1, 0)
    kpos = jnp.arange(S)

    def block(args):
        qn, qr, i = args
        s = (jnp.einsum('bqhd,bkhd->bhqk', qn, k_nope)
             + jnp.einsum('bqhr,bkr->bhqk', qr, k_rope)).astype(jnp.float32) * scale
        qpos = i * Q_BLOCK + jnp.arange(Q_BLOCK)
        s = jnp.where(kpos[None, :] <= qpos[:, None], s, NEG_INF)
        p = jax.nn.softmax(s, axis=-1)
        return jnp.einsum('bhqk,bkhd->bqhd', p.astype(v.dtype), v)

    o = lax.map(block, (qn_b, qr_b, jnp.arange(nq)))
    return jnp.moveaxis(o, 0, 1).reshape(B, S, M_WIDTH)


def _hybrid_mixer(h, positions, rel_bias, w_in, g_q_lora, w_uq, g_kv_lora, w_ukv,
                  w_up_a, w_up_b, w_o):
    proj = h @ w_in
    a_qkv, c_q, c_kv, k_r, gate_a, gate_b = jnp.split(proj, IN_SPLIT_POINTS, axis=-1)
    y_a = _dilated_attention(a_qkv, rel_bias) @ w_up_a
    y_b = _mla(c_q, c_kv, k_r, positions, g_q_lora, w_uq, g_kv_lora, w_ukv) @ w_up_b
    merged = jax.nn.sigmoid(gate_a) * y_a + jax.nn.sigmoid(gate_b) * y_b
    return merged @ w_o


def _swiglu(h, w_gate, w_up, w_down):
    return (jax.nn.silu(h @ w_gate) * (h @ w_up)) @ w_down


def setup_inputs(seed: int = 0) -> dict:
    key = jax.random.key(seed)
    ks = jax.random.split(key, 20)
    f32 = jnp.float32

    def nrm(k, shape, fan_in, mult=1.0):
        return jax.random.normal(k, shape, f32) * (mult * fan_in ** -0.5)

    def gain(k, shape):
        return 1.0 + 0.05 * jax.random.normal(k, shape, f32)

    x = jax.random.normal(ks[0], (BATCH, SEQ, D_MODEL), f32)
    c = jax.random.normal(ks[1], (BATCH, D_MODEL), f32)
    offsets = jax.random.randint(ks[2], (BATCH, 1), 0, 4096, dtype=jnp.int32)
    positions = offsets + jnp.arange(SEQ, dtype=jnp.int32)[None, :]
    return {
        "x": x,
        "c": c,
        "positions": positions,
        "rel_bias": 0.5 * jax.random.normal(ks[3], (A_HEADS, REL_BUCKETS), f32),
        "w_ada": nrm(ks[4], (DEPTH, D_MODEL, N_MOD * D_MODEL), D_MODEL, 0.5),
        "b_ada": 0.02 * jax.random.normal(ks[5], (DEPTH, N_MOD * D_MODEL), f32),
        "g_mix": gain(ks[6], (DEPTH, D_MODEL)),
        "w_in": nrm(ks[7], (DEPTH, D_MODEL, IN_WIDTH), D_MODEL),
        "g_q_lora": gain(ks[8], (DEPTH, M_Q_LORA)),
        "w_uq": nrm(ks[9], (DEPTH, M_Q_LORA, M_HEADS, M_NOPE + M_ROPE), M_Q_LORA),
        "g_kv_lora": gain(ks[10], (DEPTH, M_KV_LORA)),
        "w_ukv": nrm(ks[11], (DEPTH, M_KV_LORA, M_HEADS, M_NOPE + M_V), M_KV_LORA),
        "w_up_a": nrm(ks[12], (DEPTH, A_WIDTH, D_MODEL), A_WIDTH),
        "w_up_b": nrm(ks[13], (DEPTH, M_WIDTH, D_MODEL), M_WIDTH),
        "w_o": nrm(ks[14], (DEPTH, D_MODEL, D_MODEL), D_MODEL),
        "g_ffn": gain(ks[15], (DEPTH, D_MODEL)),
        "w_gate": nrm(ks[16], (DEPTH, D_MODEL, D_FF), D_MODEL),
        "w_up": nrm(ks[17], (DEPTH, D_MODEL, D_FF), D_MODEL),
        "w_down": nrm(ks[18], (DEPTH, D_FF, D_MODEL), D_FF),
        "g_final": gain(ks[19], (D_MODEL,)),
    }


def reference(x, c, positions, rel_bias, w_ada, b_ada, g_mix, w_in, g_q_lora, w_uq,
              g_kv_lora, w_ukv, w_up_a, w_up_b, w_o, g_ffn, w_gate, w_up, w_down, g_final):
    cond = jax.nn.silu(c)
    for layer in range(DEPTH):
        mod = (cond @ w_ada[layer] + b_ada[layer])[:, None, :]
        sh1, sc1, gt1, sh2, sc2, gt2 = jnp.split(mod, N_MOD, axis=-1)
        h = _rmsnorm(x, g_mix[layer]) * (1.0 + sc1) + sh1
        x = x + gt1 * _hybrid_mixer(h, positions, rel_bias, w_in[layer], g_q_lora[layer],
                                    w_uq[layer], g_kv_lora[layer], w_ukv[layer],
                                    w_up_a[layer], w_up_b[layer], w_o[layer])
        h = _rmsnorm(x, g_ffn[layer]) * (1.0 + sc2) + sh2
        x = x + gt2 * _swiglu(h, w_gate[layer], w_up[layer], w_down[layer])
    return _rmsnorm(x, g_final)
```

```python
import math
import numpy as np
import concourse.bass as bass
import concourse.mybir as mybir
from concourse.bass_utils import run_bass_kernel_spmd

F32 = mybir.dt.float32
BF16 = mybir.dt.bfloat16
I32 = mybir.dt.int32
ALU = mybir.AluOpType
AF = mybir.ActivationFunctionType

ENGS = ("pe", "act", "dve", "pool", "sp")
NEG = -30000.0
MAXW = 1
S = 8192
D = 1024
TQ = 2048
NT = 512
DFF = 2816
EPS = 1e-6
C_AQ, C_AK, C_AV, C_CQ, C_CKV, C_KR, C_GA, C_GB = 0, 512, 1024, 1536, 2304, 2560, 2592, 3616
INW = 4640
TW = 2944
ROWW = 3072


class Buf:
    __slots__ = ("name", "t", "lastw", "readers", "dsem", "dcount", "off")

    def __init__(self, name, t=None, off=None):
        self.name = name
        self.t = t
        self.off = off
        self.lastw = None
        self.readers = {}
        self.dsem = None
        self.dcount = 0

    def __getitem__(self, idx):
        if self.off is None:
            return self.t[idx]
        if not isinstance(idx, tuple):
            idx = (idx, slice(None))
        p, c = idx
        if isinstance(c, slice):
            a = 0 if c.start is None else c.start
            b = 512 if c.stop is None else c.stop
            c = slice(a + self.off, b + self.off)
        else:
            c = c + self.off
        return self.t[p, c]


class Op:
    __slots__ = ("eng", "fn", "deps", "signal", "ms", "is_dma", "sem", "val")

    def __init__(self, eng, fn):
        self.eng = eng
        self.fn = fn
        self.deps = []
        self.signal = False
        self.ms = 0
        self.is_dma = False
        self.sem = None
        self.val = 0


class Prog:
    def __init__(self, nc, base=18432, top=229376):
        self.nc = nc
        self.q = {e: [] for e in ENGS}
        self.esem = {}
        self.dma_bufs = []
        self.bar = []
        self.ptr = base
        self.top = top
        self.nalloc = 0

    def sb(self, name, shape, dtype):
        per = int(np.prod(shape[1:])) * mybir.dt.size(dtype)
        per = (per + 63) // 64 * 64
        assert self.ptr + per <= self.top, f"SBUF arena overflow at {name}: {self.ptr}+{per} > {self.top}"
        self.nalloc += 1
        t = self.nc.alloc_sbuf_tensor_at(f"{name}_{self.nalloc}", list(shape), dtype, offset=self.ptr)
        self.ptr += per
        return Buf(name, t)

    def sb_top(self, name, shape, dtype):
        per = int(np.prod(shape[1:])) * mybir.dt.size(dtype)
        per = (per + 63) // 64 * 64
        assert self.top - per >= self.ptr, f"SBUF arena overflow (top) at {name}"
        self.nalloc += 1
        self.top -= per
        t = self.nc.alloc_sbuf_tensor_at(f"{name}_{self.nalloc}", list(shape), dtype, offset=self.top)
        return Buf(name, t)

    def mark(self):
        return self.ptr

    def release(self, m):
        self.ptr = m
        self.barrier()

    def ps(self, name):
        return Buf(name, self.nc.alloc_psum_tensor(name, [128, 512], F32))

    def barrier(self):
        deps = []
        for e in ENGS:
            if self.q[e]:
                deps.append(self.q[e][-1])
        for b in self.dma_bufs:
            if b.lastw is not None and b.lastw.is_dma:
                deps.append(b.lastw)
            for r in b.readers.values():
                if r.is_dma:
                    deps.append(r)
        self.bar = deps

    def _track(self, op, reads, writes):
        deps = {}
        for d in self.bar:
            deps[id(d)] = d
        for b in reads:
            d = b.lastw
            if d is not None:
                deps[id(d)] = d
        for b in writes:
            d = b.lastw
            if d is not None:
                deps[id(d)] = d
            for r in b.readers.values():
                deps[id(r)] = r
        deps.pop(id(op), None)
        op.deps = list(deps.values())
        for b in reads:
            b.readers[op.eng] = op
        for b in writes:
            b.lastw = op
            b.readers = {}

    def add(self, eng, fn, reads=(), writes=()):
        op = Op(eng, fn)
        self._track(op, reads, writes)
        self.q[eng].append(op)
        return op

    def dma(self, queue, fns, sem_buf, reads=(), writes=()):
        op = Op(queue, fns)
        op.is_dma = True
        if sem_buf.dsem is None:
            sem_buf.dsem = self.nc.alloc_semaphore("d%d_%s" % (len(self.dma_bufs), sem_buf.name))
            self.dma_bufs.append(sem_buf)
        sem_buf.dcount += 16 * len(fns)
        op.sem = sem_buf.dsem
        op.val = sem_buf.dcount
        self._track(op, reads, writes)
        self.q[queue].append(op)
        return op

    def emit(self, final_wait_bufs=()):
        nc = self.nc
        for e in ENGS:
            self.esem[e] = nc.alloc_semaphore("e_" + e)
        for e in ENGS:
            for op in self.q[e]:
                for d in op.deps:
                    if d.is_dma:
                        continue
                    if d.eng == "pe" and e == "pe" and not op.is_dma:
                        continue
                    d.signal = True
        for e in ENGS:
            c = 0
            for op in self.q[e]:
                if op.signal and not op.is_dma:
                    c += 1
                    op.ms = c
        stats = {}

        def run(e, eng):
            seen = {}
            nw = 0
            for op in self.q[e]:
                waits = []
                for d in op.deps:
                    if d.is_dma:
                        s, v = d.sem, d.val
                    else:
                        if d.eng == "pe" and e == "pe" and not op.is_dma:
                            continue
                        s, v = self.esem[d.eng], d.ms
                    k = id(s)
                    if seen.get(k, 0) >= v:
                        continue
                    seen[k] = v
                    waits.append((s, v))
                    nw += 1
                emb = waits[-MAXW:] if MAXW > 0 else []
                for (s, v) in waits[:len(waits) - len(emb)]:
                    eng.wait_ge(s, v)
                if op.is_dma:
                    first = True
                    for f in op.fn:
                        ins = f(eng)
                        if first:
                            for (s, v) in emb:
                                ins._wait_ge(s, v)
                            first = False
                        ins.then_inc(op.sem, 16)
                else:
                    ins = op.fn(eng)
                    for (s, v) in emb:
                        ins._wait_ge(s, v)
                    if op.signal:
                        ins.then_inc(self.esem[e], 1)
            if e == "sp":
                for b in final_wait_bufs:
                    eng.wait_ge(b.dsem, b.dcount)
            stats[e] = (len(self.q[e]), nw)

        with nc.Block() as block:
            @block.tensor
            def _(eng):
                run("pe", eng)

            @block.scalar
            def _(eng):
                run("act", eng)

            @block.vector
            def _(eng):
                run("dve", eng)

            @block.gpsimd
            def _(eng):
                run("pool", eng)

            @block.sync
            def _(eng):
                run("sp", eng)
        return stats


def dap(t, off, dims):
    return bass.AP(t, off, [list(d) for d in dims])


def build(debug=False):
    nc = bass.Bass("TRN2", target_bir_lowering=False)
    P = Prog(nc)

    def din(name, shape, dt=F32):
        return nc.dram_tensor(name, list(shape), dt, kind="ExternalInput")

    xT = din("xT", [D, S])
    posr = din("posr", [1, S], I32)
    cT = din("cT", [128, 8])
    pm_d = din("pm", [128, 4])
    rbT_d = din("rbT", [128, 128])
    OH_d = din("OH", [128, ROWW])
    w_ada = din("w_ada", [D, 6 * D])
    b_adaT = din("b_adaT", [128, 48])
    g_mixT = din("g_mixT", [128, 8])
    w_in = din("w_in", [D, INW])
    w_kr = din("w_kr", [D, 2, 96])
    g_qT = din("g_qT", [128, 6])
    w_uq = din("w_uq", [768, 8 * 96])
    w_uqs = din("w_uqs", [768, 8 * 96])
    g_kvT = din("g_kvT", [128, 2])
    w_ukv = din("w_ukv", [256, 8 * 128])
    w_up_a = din("w_up_a", [512, D])
    w_up_b = din("w_up_b", [512, D])
    w_o = din("w_o", [D, D])
    g_ffnT = din("g_ffnT", [128, 8])
    w_gate = din("w_gate", [D, DFF])
    w_up = din("w_up", [D, DFF])
    w_down = din("w_down", [DFF, D])
    g_finT = din("g_finT", [128, 8])
    ident_d = din("ident", [128, 128])
    anti_d = din("anti", [128, 128])
    MT_d = din("MT", [128, 896])
    misc_d = din("misc", [128, 4])
    outT = nc.dram_tensor("outT", [D, TQ], F32, kind="ExternalOutput")
    vA_d = nc.dram_tensor("vA_scr", [8, ROWW], F32)
    dbg = {}

    def dbg_out(name, shape, dt=F32):
        dbg[name] = nc.dram_tensor("dbg_" + name, list(shape), dt, kind="ExternalOutput")
        return dbg[name]

    pmm = [P.ps("pmm0"), P.ps("pmm1")]
    scP = [nc.alloc_psum_tensor("scP0", [128, 1024], F32), nc.alloc_psum_tensor("scP1", [128, 1024], F32)]
    hb = [Buf("h0", scP[0], 0), Buf("h1", scP[0], 512), Buf("h2", scP[1], 0), Buf("h3", scP[1], 512)]
    psc = [hb[0], hb[1]]
    pss = hb[2]
    pmisc = hb[3]
    pov = [P.ps("pov0"), P.ps("pov1")]
    cnt = {"mm": 0, "sc": 0, "ov": 0, "ev": 0, "w": 0}

    ones_bf = P.sb("ones_bf", [128, 128], BF16)
    ones_f = P.sb("ones_f", [128, 128], F32)
    ident = P.sb("ident", [128, 128], BF16)
    anti = P.sb("anti", [128, 128], BF16)
    MT = P.sb("MT", [128, 896], BF16)
    misc = P.sb("misc", [128, 4], F32)
    pm = P.sb("pm", [128, 4], F32)
    modT = P.sb("modT", [128, 48], F32)
    gv1 = P.sb("gv1", [128, 8], F32)
    gv2 = P.sb("gv2", [128, 8], F32)
    gfin = P.sb("gfin", [128, 8], F32)
    gq = P.sb("gq", [128, 6], F32)
    gkv = P.sb("gkv", [128, 2], F32)
    attnAT = P.sb("attnAT", [128, 4, TQ], BF16)
    dump_stage = P.sb("dump_stage", [128, 512], F32) if debug else None

    P.add("dve", lambda e: e.memset(ones_bf[:], 1.0), writes=[ones_bf])
    P.add("dve", lambda e: e.memset(ones_f[:], 0.0), writes=[ones_f])
    P.add("dve", lambda e: e.memset(ones_f[64:65, :], 1.0), writes=[ones_f])
    P.dma("pool", [lambda e: e.dma_start(out=ident[:], in_=ident_d[:]),
                   lambda e: e.dma_start(out=anti[:], in_=anti_d[:]),
                   lambda e: e.dma_start(out=MT[:], in_=MT_d[:])], ident, writes=[ident, anti, MT])
    P.dma("sp", [lambda e: e.dma_start(out=misc[:], in_=misc_d[:]),
                 lambda e: e.dma_start(out=pm[:], in_=pm_d[:]),
                 lambda e: e.dma_start(out=gfin[:], in_=g_finT[:]),
                 lambda e: e.dma_start(out=gq[:], in_=g_qT[:]),
                 lambda e: e.dma_start(out=gkv[:], in_=g_kvT[:])], misc, writes=[misc, pm, gfin, gq, gkv])
    eps_ap = misc[:, 2:3]
    zero_ap = misc[:, 3:4]

    def evac(out_ap, in_ap, reads, writes, eng=None, scale=None):
        if eng is None:
            eng = "act" if cnt["ev"] % 2 == 0 else "dve"
            cnt["ev"] += 1
        if eng == "act":
            if scale is None:
                P.add("act", lambda e: e.activation(out=out_ap, in_=in_ap, func=AF.Copy), reads=reads, writes=writes)
            else:
                P.add("act", lambda e: e.activation(out=out_ap, in_=in_ap, func=AF.Copy, scale=scale), reads=reads, writes=writes)
        else:
            if scale is None:
                P.add(eng, lambda e: e.tensor_copy(out=out_ap, in_=in_ap), reads=reads, writes=writes)
            else:
                P.add(eng, lambda e: e.tensor_scalar(out=out_ap, in0=in_ap, scalar1=scale, scalar2=None, op0=ALU.mult), reads=reads, writes=writes)

    def mm(out_ap, lhsT, rhs, start, stop, reads, writes):
        P.add("pe", lambda e: e.matmul(out_ap, lhsT=lhsT, rhs=rhs, start=start, stop=stop), reads=reads, writes=writes)

    def next_mm():
        b = pmm[cnt["mm"] % 2]
        cnt["mm"] += 1
        return b

    def load_w(buf, out_ap, dram, off, dims):
        P.dma("pool", [lambda e: e.dma_start(out=out_ap, in_=dap(dram, off, dims))], buf, writes=[buf])

    def dump(name, src_ap, src_buf, shape, dt):
        if not debug:
            return
        o = dbg_out(name, shape, dt)
        P.dma("sp", [lambda e: e.dma_start(out=o[:], in_=src_ap)], src_buf, reads=[src_buf])
        dumped.append(src_buf)

    dumped = []

    tmp48 = P.sb("tmp48", [128, 48], F32)
    gtmp = P.sb("gtmp", [128, 16], F32)
    condT = P.sb("condT", [128, 8], F32)
    condB = P.sb("condB", [128, 8], BF16)
    m0 = P.mark()
    P.dma("sp", [lambda e: e.dma_start(out=condT[:], in_=cT[:]),
                 lambda e: e.dma_start(out=tmp48[:], in_=b_adaT[:]),
                 lambda e: e.dma_start(out=gtmp[:, 0:8], in_=g_mixT[:]),
                 lambda e: e.dma_start(out=gtmp[:, 8:16], in_=g_ffnT[:])], condT, writes=[condT, tmp48, gtmp])
    P.add("act", lambda e: e.activation(out=condB[:], in_=condT[:], func=AF.Silu), reads=[condT], writes=[condB])

    def ada_slab(sl, wb):
        load_w(wb, wb[:], w_ada, sl * 512, [[6 * D, 128], [128 * 6 * D, 8], [1, 512]])
        for cc in range(4):
            ci = sl * 4 + cc
            for kc in range(8):
                mm(pmisc[:, ci:ci + 1], wb[:, kc, cc * 128:(cc + 1) * 128], condB[:, kc:kc + 1], kc == 0, kc == 7,
                   [wb, condB], [pmisc])

    wst0 = [P.sb("wst0", [128, 8, 512], BF16), P.sb("wst1", [128, 8, 512], BF16)]
    for sl in range(4):
        ada_slab(sl, wst0[sl % 2])
    P.add("dve", lambda e: e.tensor_tensor(out=modT[:, 0:16], in0=pmisc[:, 0:16], in1=tmp48[:, 0:16], op=ALU.add), reads=[pmisc, tmp48], writes=[modT])
    P.add("dve", lambda e: e.scalar_tensor_tensor(out=gv1[:], in0=modT[:, 8:16], scalar=1.0, in1=gtmp[:, 0:8], op0=ALU.add, op1=ALU.mult),
          reads=[modT, gtmp], writes=[gv1])
    sh1, gt1, sh2, gt2 = modT[:, 0:8], modT[:, 16:24], modT[:, 24:32], modT[:, 40:48]

    rbT = P.sb("rbT", [128, 128], F32)
    OH = P.sb("OH", [128, ROWW], F32)
    vA_sb = P.sb("vA_sb", [8, ROWW], F32)
    P.dma("sp", [lambda e: e.dma_start(out=rbT[:], in_=rbT_d[:]), lambda e: e.dma_start(out=OH[:], in_=OH_d[:])], rbT, writes=[rbT, OH])
    for ch in range(6):
        pb_ = next_mm()
        mm(pb_[:, :], rbT[:, :], OH[:, ch * 512:(ch + 1) * 512], True, True, [rbT, OH], [pb_])
        evac(vA_sb[:, ch * 512:(ch + 1) * 512], pb_[0:8, :], [pb_], [vA_sb], eng="dve")
    vAd = Buf("vAd")
    P.dma("sp", [lambda e: e.dma_start(out=vA_d[:], in_=vA_sb[:])], vA_sb, reads=[vA_sb], writes=[vAd])
    P.release(m0)

    def alloc_tilework(double_h=True, double_x=True):
        w = {}
        x0 = P.sb("xbuf0", [128, 8, NT], F32)
        w["xbuf"] = [x0, P.sb("xbuf1", [128, 8, NT], F32) if double_x else x0]
        h0 = P.sb("hT0", [128, 8, NT], BF16)
        w["hT"] = [h0, P.sb("hT1", [128, 8, NT], BF16) if double_h else h0]
        w["sq"] = P.sb("sq", [128, 8, NT], BF16)
        w["rstd"] = P.sb("rstd", [128, NT], F32)
        w["tmpc"] = [P.sb("tmpc0", [128, NT], F32), P.sb("tmpc1", [128, NT], F32)]
        return w

    def load_x(w, i, t0):
        xb = w["xbuf"][i % 2]
        P.dma("sp", [lambda e: e.dma_start(out=xb[:], in_=dap(xT, t0, [[S, 128], [128 * S, 8], [1, NT]]))], xb, writes=[xb])
        return xb

    def rms_rstd(sqb, nch, rstd, nfeat, src_reads):
        for c in range(nch):
            mm(pss[:, :], ones_bf[:, :], sqb[:, c, :], c == 0, c == nch - 1, [ones_bf, sqb], [pss])
        P.add("act", lambda e: e.activation(out=rstd[:], in_=pss[:], func=AF.Ln, bias=eps_ap, scale=1.0 / nfeat), reads=[pss, misc], writes=[rstd])
        P.add("act", lambda e: e.activation(out=rstd[:], in_=rstd[:], func=AF.Exp, scale=-0.5), reads=[rstd], writes=[rstd])

    def make_h(w, xb, i, gv, sh, gvbuf):
        hT = w["hT"][i % 2]
        sqb, rstd = w["sq"], w["rstd"]
        P.add("act", lambda e: e.activation(out=sqb[:], in_=xb[:], func=AF.Square), reads=[xb], writes=[sqb])
        rms_rstd(sqb, 8, rstd, float(D), None)
        for c in range(8):
            tc_ = w["tmpc"][c % 2]
            P.add("dve", lambda e, c=c, tc_=tc_: e.scalar_tensor_tensor(out=tc_[:], in0=xb[:, c, :], scalar=gv[:, c:c + 1], in1=rstd[:], op0=ALU.mult, op1=ALU.mult),
                  reads=[xb, rstd, gvbuf], writes=[tc_])
            P.add("act", lambda e, c=c, tc_=tc_: e.activation(out=hT[:, c, :], in_=tc_[:], func=AF.Identity, bias=sh[:, c:c + 1], scale=1.0),
                  reads=[tc_, modT], writes=[hT])
        return hT

    def proj_fm(hT, wbuf, wap_fn, nk, out_fn, M=128, rhs_fn=None, ev_eng=None, scale=None, extra_reads=()):
        pb_ = next_mm()
        for k in range(nk):
            rhs = hT[:, k, :] if rhs_fn is None else rhs_fn(k)
            mm(pb_[0:M, :], wap_fn(k), rhs, k == 0, k == nk - 1, [hT, wbuf] + list(extra_reads), [pb_])
        out_fn(pb_)

    mA = P.mark()
    KT_A = P.sb("KT_A", [128, 4, 4096], BF16)
    V_A = P.sb("V_A", [128, 32, 8, 65], BF16)
    QT_A = P.sb("QT_A", [128, 4, TQ], BF16)
    QT_A1 = P.sb("QT_A1", [128, 4, TQ], BF16)
    P.add("pool", lambda e: e.memset(QT_A[64:128, :, :], 0.0), writes=[QT_A])
    P.add("pool", lambda e: e.memset(QT_A1[0:64, :, :], 0.0), writes=[QT_A1])
    mA1 = P.mark()
    wA = P.sb("wA", [128, 8, 1536], BF16)
    W = alloc_tilework(double_h=True, double_x=False)
    wst = [P.sb("wst0", [128, 8, 512], BF16), P.sb("wst1", [128, 8, 512], BF16)]
    for j3 in range(3):
        load_w(wA, wA[:, :, j3 * 512:(j3 + 1) * 512], w_in, j3 * 512, [[INW, 128], [128 * INW, 8], [1, 512]])
    P.add("pool", lambda e: e.memset(V_A[:, :, :, 64:65], 1.0), writes=[V_A])
    tiles = [(1, i) for i in range(4)] + [(0, i) for i in range(4)]
    for n, (s, i) in enumerate(tiles):
        xb = load_x(W, n, s * TQ + i * NT)
        hT = make_h(W, xb, n, gv1, sh1, gv1)
        if debug and n == 4:
            dump("hT0", hT[:, 0, :], hT, [128, 512], BF16)
        kp0 = (1 - s) * TQ + i * NT
        for hp in range(4):
            proj_fm(hT, wA, lambda k, hp=hp: wA[:, k, C_AK + hp * 128:C_AK + (hp + 1) * 128], 8,
                    lambda pb_, hp=hp: evac(KT_A[:, hp, kp0:kp0 + NT], pb_[:, :], [pb_], [KT_A]))
        for sub in range(4):
            pb_ = next_mm()
            for k in range(8):
                mm(pb_[:, :], hT[:, k, sub * 128:(sub + 1) * 128], wA[:, k, C_AV:C_AV + 512], k == 0, k == 7, [hT, wA], [pb_])
            kt = kp0 // 128 + sub
            evac(V_A[:, kt, :, 0:64], pb_[:, :].rearrange("p (h d) -> p h d", h=8), [pb_], [V_A])
        ada_slab(4 + n, wst[n % 2])
        if s == 0:
            for hp in range(4):
                proj_fm(hT, wA, lambda k, hp=hp: wA[:, k, C_AQ + hp * 128:C_AQ + (hp + 1) * 128], 8,
                        lambda pb_, hp=hp: (evac(QT_A[0:64, hp, i * NT:(i + 1) * NT], pb_[0:64, :], [pb_], [QT_A], scale=0.125),
                                            evac(QT_A1[64:128, hp, i * NT:(i + 1) * NT], pb_[64:128, :], [pb_], [QT_A1], scale=0.125)))
    P.add("dve", lambda e: e.tensor_tensor(out=modT[:, 16:48], in0=pmisc[:, 16:48], in1=tmp48[:, 16:48], op=ALU.add), reads=[pmisc, tmp48], writes=[modT])
    P.add("dve", lambda e: e.scalar_tensor_tensor(out=gv2[:], in0=modT[:, 32:40], scalar=1.0, in1=gtmp[:, 8:16], op0=ALU.add, op1=ALU.mult),
          reads=[modT, gtmp], writes=[gv2])
    dump("modT", modT[:], modT, [128, 48], F32)
    dump("KT_A", KT_A[:, 0, :], KT_A, [128, 4096], BF16)
    dump("QT_A", QT_A[:, 0, :], QT_A, [128, TQ], BF16)
    dump("V_A", V_A[:, :, 0, :], V_A, [128, 32, 65], BF16)

    P.release(mA1)
    TB = [P.sb("TB0", [128, TW], BF16), P.sb("TB1", [128, TW], BF16)]
    PT = [P.sb("PT0", [128, NT], BF16), P.sb("PT1", [128, NT], BF16), P.sb("PT2", [128, NT], BF16)]
    recs = P.sb("recs", [128, NT], F32)
    P.add("dve", lambda e: e.memset(recs[:], 0.0), writes=[recs])
    bcs = P.sb("bcs", [64, NT], F32)
    otmp = [P.sb("otmp0", [64, NT], BF16), P.sb("otmp1", [64, NT], BF16)]

    def finalize(ov, h, qb, dstT, n_odd, recs, bcs, otmp, pmisc=pmisc):
        hp, odd = h // 2, h % 2
        P.add("dve", lambda e: e.tensor_copy(out=recs[0:65, :], in_=ov[0:65, :]), reads=[ov], writes=[recs])
        mm(pmisc[:, :], ones_f[:, :], recs[:, :], True, True, [ones_f, recs], [pmisc])
        P.add("act", lambda e: e.activation(out=bcs[:], in_=pmisc[0:64, :], func=AF.Ln), reads=[pmisc], writes=[bcs])
        P.add("act", lambda e: e.activation(out=bcs[:], in_=bcs[:], func=AF.Exp, scale=-1.0), reads=[bcs], writes=[bcs])
        if not odd:
            P.add("dve", lambda e: e.tensor_tensor(out=dstT[0:64, hp, qb * NT:(qb + 1) * NT], in0=recs[0:64, :], in1=bcs[:], op=ALU.mult),
                  reads=[recs, bcs], writes=[dstT])
        else:
            ot = otmp[n_odd % 2]
            P.add("dve", lambda e: e.tensor_tensor(out=ot[:], in0=recs[0:64, :], in1=bcs[:], op=ALU.mult), reads=[recs, bcs], writes=[ot])
            P.dma("sp", [lambda e: e.dma_start(out=dstT[64:128, hp, qb * NT:(qb + 1) * NT], in_=ot[:])], ot, reads=[ot], writes=[dstT])


    class AttnPipe:
        def __init__(self, PTs, depth=2):
            self.scs = [psc[0], psc[1], pss]
            self.PTs = PTs
            self.depth = depth
            self.pend = []
            self.n = 0
            self.deferred = []

        def defer(self, fn, after=8):
            self.deferred.append([after, fn])

        def _tick(self):
            for d in self.deferred:
                d[0] -= 1
            while self.deferred and self.deferred[0][0] <= 0:
                self.deferred.pop(0)[1]()

        def tile(self, qk_fn, bias_ap, scale, pv_fn, bias_reads, mul_ap=None, mul_buf=None, PT2=None):
            self._tick()
            sc = self.scs[self.n % 3]
            pt = self.PTs[self.n % 3]
            qk_fn(sc)
            P.add("act", lambda e, sc=sc, pt=pt: e.activation(out=pt[:], in_=sc[:], func=AF.Exp, bias=bias_ap, scale=scale),
                  reads=[sc] + list(bias_reads), writes=[pt])
            if mul_ap is not None:
                pt2 = PT2[self.n % len(PT2)]
                P.add("dve", lambda e, pt=pt, pt2=pt2: e.tensor_tensor(out=pt2[:], in0=pt[:], in1=mul_ap, op=ALU.mult),
                      reads=[pt, mul_buf], writes=[pt2])
                pt = pt2
            self.n += 1
            self.pend.append((pv_fn, pt))
            assert self.depth < (len(PT2) if mul_ap is not None else len(self.PTs))
            if len(self.pend) > self.depth:
                f, p_ = self.pend.pop(0)
                f(p_)

        def flush(self):
            while self.pend:
                f, p_ = self.pend.pop(0)
                f(p_)
            while self.deferred:
                self.deferred.pop(0)[1]()

    n_odd = 0
    pipe = AttnPipe(PT, depth=3)
    PT2 = [P.sb("PTb%d" % i_, [128, NT], BF16) for i_ in range(4)]
    TE = [P.sb("TE0", [128, TW], BF16), P.sb("TE1", [128, TW], BF16)]
    for h in range(8):
        hp, pb0 = h // 2, 64 * (h % 2)
        tb = TB[h % 2]
        te = TE[h % 2]
        P.dma("pool", [lambda e, tb=tb, h=h: e.dma_start(out=tb[:], in_=dap(vA_d, h * ROWW, [[1, 128], [1, TW]]))], tb, reads=[vAd], writes=[tb])
        for c0 in range(0, TW, NT):
            cw = min(NT, TW - c0)
            pb_ = next_mm()
            mm(pb_[:, 0:cw], anti[:, :], tb[:, c0:c0 + cw], True, True, [anti, tb], [pb_])
            P.add("act", lambda e, pb_=pb_, te=te, c0=c0, cw=cw: e.activation(out=te[:, c0:c0 + cw], in_=pb_[:, 0:cw], func=AF.Exp),
                  reads=[pb_], writes=[te])
        for qb in range(4):
            ov = pov[cnt["ov"] % 2]
            cnt["ov"] += 1
            kts = list(range(4 * qb, 4 * qb + 20))
            for n_, kt in enumerate(kts):
                u = 16 + 4 * qb - kt
                qsel = QT_A if pb0 == 0 else QT_A1

                def qk(sc, kt=kt, qsel=qsel, hp=hp, qb=qb):
                    mm(sc[:, :], KT_A[:, hp, kt * 128:(kt + 1) * 128], qsel[:, hp, qb * NT:(qb + 1) * NT], True, True,
                       [KT_A, qsel], [sc])

                def pv(pt, kt=kt, h=h, qb=qb, ov=ov, first=(n_ == 0), last=(n_ == len(kts) - 1), n_odd=n_odd):
                    mm(ov[0:65, :], V_A[:, kt, h, 0:65], pt[:, :], first, last, [V_A, pt], [ov])
                    if last:
                        pipe.defer(lambda ov=ov, h=h, qb=qb, n_odd=n_odd: finalize(ov, h, qb, attnAT, n_odd, recs, bcs, otmp))

                bias_ap = pm[:, 1:2] if kt < 16 else zero_ap
                pipe.tile(qk, bias_ap, 1.0, pv, [pm, misc], mul_ap=te[:, 128 * (u + 3):128 * (u + 3) + NT], mul_buf=te, PT2=PT2)
            n_odd += h % 2
    pipe.flush()
    dump("attnAT", attnAT[:, 0, :], attnAT, [128, TQ], BF16)
    P.release(mA)

    mlaT = P.sb("mlaT", [128, 4, TQ], BF16)
    mB = P.mark()
    qT_all = P.sb("qT_all", [96, 8, TQ], BF16)

    def rope_tables(R, t0):
        posi, posf, ang, t1, ti, Ct, St = R["posi"], R["posf"], R["ang"], R["t1"], R["ti"], R["C"], R["S"]
        P.dma("sp", [lambda e: e.dma_start(out=posi[0:96, :], in_=dap(posr, t0, [[0, 96], [1, NT]]))], posi, writes=[posi])
        P.add("dve", lambda e: e.tensor_copy(out=posf[0:96, :], in_=posi[0:96, :]), reads=[posi], writes=[posf])
        P.add("dve", lambda e: e.tensor_scalar(out=ang[0:96, :], in0=posf[0:96, :], scalar1=misc[0:96, 0:1], scalar2=None, op0=ALU.mult),
              reads=[posf, misc], writes=[ang])
        for which, dst in ((0, St), (1, Ct)):
            if which == 1:
                P.add("dve", lambda e: e.tensor_scalar(out=ang[0:96, :], in0=ang[0:96, :], scalar1=math.pi / 2, scalar2=None, op0=ALU.add),
                      reads=[ang], writes=[ang])
            P.add("dve", lambda e: e.tensor_scalar(out=t1[0:96, :], in0=ang[0:96, :], scalar1=1.0 / (2 * math.pi), scalar2=None, op0=ALU.mult),
                  reads=[ang], writes=[t1])
            P.add("dve", lambda e: e.tensor_copy(out=ti[0:96, :], in_=t1[0:96, :]), reads=[t1], writes=[ti])
            P.add("dve", lambda e: e.tensor_copy(out=t1[0:96, :], in_=ti[0:96, :]), reads=[ti], writes=[t1])
            P.add("dve", lambda e: e.scalar_tensor_tensor(out=t1[0:96, :], in0=t1[0:96, :], scalar=-2 * math.pi, in1=ang[0:96, :], op0=ALU.mult, op1=ALU.add),
                  reads=[t1, ang], writes=[t1])
            P.add("dve", lambda e: e.tensor_scalar(out=t1[0:96, :], in0=t1[0:96, :], scalar1=-math.pi, scalar2=math.pi, op0=ALU.max, op1=ALU.min),
                  reads=[t1], writes=[t1])
            P.add("act", lambda e, dst=dst: e.activation(out=dst[0:96, :], in_=t1[0:96, :], func=AF.Sin), reads=[t1], writes=[dst])
        P.add("dve", lambda e: e.tensor_scalar(out=St[0:96, :], in0=St[0:96, :], scalar1=misc[0:96, 1:2], scalar2=None, op0=ALU.mult),
              reads=[St, misc], writes=[St])

    def alloc_rope():
        R = {}
        R["posi"] = P.sb("posi", [128, NT], I32)
        R["ti"] = R["posi"]
        for k in ("ang", "t1", "C", "S", "ra", "rb"):
            R[k] = P.sb(k, [128, NT], F32)
        R["posf"] = R["ra"]
        return R

    def apply_rope(R, pa, pb_, lo, hi, out_ap, out_buf):
        ra, rb = R["ra"], R["rb"]
        P.add("dve", lambda e: e.tensor_tensor(out=ra[lo:hi, :], in0=pa[lo:hi, :], in1=R["C"][lo:hi, :], op=ALU.mult), reads=[pa, R["C"]], writes=[ra])
        P.add("dve", lambda e: e.tensor_tensor(out=rb[lo:hi, :], in0=pb_[lo:hi, :], in1=R["S"][lo:hi, :], op=ALU.mult), reads=[pb_, R["S"]], writes=[rb])
        P.add("pool", lambda e: e.tensor_tensor(out=out_ap, in0=ra[lo:hi, :], in1=rb[lo:hi, :], op=ALU.add), reads=[ra, rb], writes=[out_buf])

    mB2 = P.mark()
    W = alloc_tilework(double_h=True, double_x=True)
    R = alloc_rope()
    wcq = P.sb("wcq", [128, 8, 768], BF16)
    wuq = P.sb("wuq", [128, 6, 768], BF16)
    wuqs = P.sb("wuqs", [128, 6, 768], BF16)
    cqf = P.sb("cqf", [128, 6, NT], F32)
    cqsq = P.sb("cqsq", [128, 6, NT], BF16)
    cqn = P.sb("cqn", [128, 6, NT], BF16)
    rstq = P.sb("rstq", [128, NT], F32)
    load_w(wcq, wcq[:, :, 0:384], w_in, C_CQ, [[INW, 128], [128 * INW, 8], [1, 384]])
    load_w(wcq, wcq[:, :, 384:768], w_in, C_CQ + 384, [[INW, 128], [128 * INW, 8], [1, 384]])
    load_w(wuq, wuq[:, :, :], w_uq, 0, [[768, 128], [128 * 768, 6], [1, 768]])
    load_w(wuqs, wuqs[:, :, :], w_uqs, 0, [[768, 128], [128 * 768, 6], [1, 768]])
    for i in range(4):
        xb = load_x(W, i, i * NT)
        hT = make_h(W, xb, i, gv1, sh1, gv1)
        rope_tables(R, i * NT)
        for m in range(6):
            def ev(pb_, m=m):
                P.add("act", lambda e: e.activation(out=cqf[:, m, :], in_=pb_[:, :], func=AF.Copy), reads=[pb_], writes=[cqf])
                P.add("act", lambda e: e.activation(out=cqsq[:, m, :], in_=pb_[:, :], func=AF.Square), reads=[pb_], writes=[cqsq])
            proj_fm(hT, wcq, lambda k, m=m: wcq[:, k, m * 128:(m + 1) * 128], 8, ev)
        rms_rstd(cqsq, 6, rstq, 768.0, None)
        for m in range(6):
            P.add("dve", lambda e, m=m: e.scalar_tensor_tensor(out=cqn[:, m, :], in0=cqf[:, m, :], scalar=gq[:, m:m + 1], in1=rstq[:], op0=ALU.mult, op1=ALU.mult),
                  reads=[cqf, rstq, gq], writes=[cqn])
        for h in range(8):
            pa, pb_ = pmm[0], pmm[1]
            for k in range(6):
                mm(pa[0:96, :], wuq[:, k, h * 96:(h + 1) * 96], cqn[:, k, :], k == 0, k == 5, [wuq, cqn], [pa])
            for k in range(6):
                mm(pb_[0:96, :], wuqs[:, k, h * 96:(h + 1) * 96], cqn[:, k, :], k == 0, k == 5, [wuqs, cqn], [pb_])
            apply_rope(R, pa, pb_, 0, 96, qT_all[0:96, h, i * NT:(i + 1) * NT], qT_all)
    dump("qT0", qT_all[:, 0, :], qT_all, [96, TQ], BF16)
    P.release(mB2)

    latT = P.sb("latT", [128, 2, S], BF16)
    KT = P.sb("KT", [96, S], BF16)
    mB1 = P.mark()
    W = alloc_tilework(double_h=False, double_x=True)
    R = alloc_rope()
    wkv = P.sb("wkv", [128, 8, 256], BF16)
    wkr = P.sb("wkr", [128, 8, 192], BF16)
    ckf = P.sb("ckf", [128, 2, NT], F32)
    cksq = P.sb("cksq", [128, 2, NT], BF16)
    rstk = P.sb("rstk", [128, NT], F32)
    load_w(wkv, wkv[:, :, :], w_in, C_CKV, [[INW, 128], [128 * INW, 8], [1, 256]])
    load_w(wkr, wkr[:, :, :], w_kr, 0, [[192, 128], [128 * 192, 8], [1, 192]])
    for n in range(16):
        t0 = n * NT
        xb = load_x(W, n, t0)
        hT = make_h(W, xb, n, gv1, sh1, gv1)
        rope_tables(R, t0)
        for m in range(2):
            def ev(pb_, m=m):
                P.add("act", lambda e: e.activation(out=ckf[:, m, :], in_=pb_[:, :], func=AF.Copy), reads=[pb_], writes=[ckf])
                P.add("act", lambda e: e.activation(out=cksq[:, m, :], in_=pb_[:, :], func=AF.Square), reads=[pb_], writes=[cksq])
            proj_fm(hT, wkv, lambda k, m=m: wkv[:, k, m * 128:(m + 1) * 128], 8, ev)
        rms_rstd(cksq, 2, rstk, 256.0, None)
        for m in range(2):
            P.add("dve", lambda e, m=m, t0=t0: e.scalar_tensor_tensor(out=latT[:, m, t0:t0 + NT], in0=ckf[:, m, :], scalar=gkv[:, m:m + 1], in1=rstk[:], op0=ALU.mult, op1=ALU.mult),
                  reads=[ckf, rstk, gkv], writes=[latT])
        pa, pb_ = pmm[0], pmm[1]
        for k in range(8):
            mm(pa[0:96, :], wkr[:, k, 0:96], hT[:, k, :], k == 0, k == 7, [wkr, hT], [pa])
        for k in range(8):
            mm(pb_[0:96, :], wkr[:, k, 96:192], hT[:, k, :], k == 0, k == 7, [wkr, hT], [pb_])
        apply_rope(R, pa, pb_, 64, 96, KT[64:96, t0:t0 + NT], KT)
    dump("latT", latT[:, 0, 0:TQ], latT, [128, TQ], BF16)
    P.release(mB1)

    mB3 = P.mark()
    wukv = P.sb("wukv", [128, 2, 1024], BF16)
    Vh = [P.sb("Vh0", [128, 64, 65], BF16), P.sb("Vh1", [128, 64, 65], BF16)]
    PT = [P.sb("PT0", [128, NT], BF16), P.sb("PT1", [128, NT], BF16), P.sb("PT2", [128, NT], BF16)]
    recs = P.sb("recs", [128, NT], F32)
    P.add("dve", lambda e: e.memset(recs[:], 0.0), writes=[recs])
    bcs = P.sb("bcs", [64, NT], F32)
    otmp = [P.sb("otmp0", [64, NT], BF16), P.sb("otmp1", [64, NT], BF16)]
    load_w(wukv, wukv[:, :, :], w_ukv, 0, [[1024, 128], [128 * 1024, 2], [1, 1024]])
    for vb in Vh:
        P.add("pool", lambda e, vb=vb: e.memset(vb[:, :, 64:65], 1.0), writes=[vb])
    n_odd = 0
    PTp = [P.sb("PTp%d" % i_, [128, 2 * NT], BF16) for i_ in range(3)]
    stages = [(hb[0], hb[1], scP[0]), (hb[2], hb[3], scP[1])]
    pend = []
    deferred = []
    npair = 0

    def tick():
        for d_ in deferred:
            d_[0] -= 1
        while deferred and deferred[0][0] <= 0:
            deferred.pop(0)[1]()

    for h in range(8):
        vb = Vh[h % 2]
        for n in range(16):
            pb_ = pmm[0]
            for k in range(2):
                mm(pb_[:, :], wukv[:, k, h * 128:h * 128 + 128], latT[:, k, n * NT:(n + 1) * NT], k == 0, k == 1, [wukv, latT], [pb_])
            evac(KT[0:64, n * NT:(n + 1) * NT], pb_[0:64, :], [pb_], [KT], eng="dve")
        for g in range(8):
            pb_ = pmm[0]
            for tt in range(8):
                kt = g * 8 + tt
                for k in range(2):
                    mm(pb_[:, tt * 64:(tt + 1) * 64], latT[:, k, kt * 128:(kt + 1) * 128], wukv[:, k, h * 128 + 64:h * 128 + 128], k == 0, k == 1,
                       [wukv, latT], [pb_])
            evac(vb[:, g * 8:(g + 1) * 8, 0:64], pb_[:, :].rearrange("p (t d) -> p t d", t=8), [pb_], [vb], eng="dve")
        if debug and h == 0:
            dump("KT0", KT[:, 0:TQ], KT, [96, TQ], BF16)
            dump("Vh0", vb[:, :, :], vb, [128, 64, 65], BF16)
        for qb in range(4):
            ov = pov[cnt["ov"] % 2]
            cnt["ov"] += 1
            klist = [(0, kt) for kt in range(4 * (qb + 1))] + [(s, kt) for s in (1, 2, 3) for kt in range(16)]
            assert len(klist) % 2 == 0
            for pi in range(len(klist) // 2):
                tick()
                two = klist[2 * pi:2 * pi + 2]
                assert two[0][0] == two[1][0]
                s_ = two[0][0]
                ha, hb_, scT = stages[npair % 2]
                ptp = PTp[npair % 3]
                npair += 1
                for j_, (s, kt) in enumerate(two):
                    g = s * 16 + kt
                    diag = (s == 0 and kt >= 4 * qb)
                    sc = (ha, hb_)[j_]
                    mm(sc[:, :], KT[0:96, g * 128:(g + 1) * 128], qT_all[0:96, h, qb * NT:(qb + 1) * NT], True, not diag, [KT, qT_all], [sc])
                    if diag:
                        wv = kt - 4 * qb
                        mm(sc[:, :], ident[:, :], MT[:, 128 * (3 - wv):128 * (3 - wv) + NT], False, True, [ident, MT], [sc])
                P.add("act", lambda e, scT=scT, ptp=ptp, s_=s_: e.activation(out=ptp[:, :], in_=scT[:, 0:2 * NT], func=AF.Exp, bias=pm[:, s_:s_ + 1], scale=96.0 ** -0.5),
                      reads=[ha, hb_, pm], writes=[ptp])

                def pv(two=two, h=h, qb=qb, ov=ov, vb=vb, ptp=ptp, pi=pi, npairs=len(klist) // 2, n_odd=n_odd):
                    for j_, (s, kt) in enumerate(two):
                        g = s * 16 + kt
                        first = (pi == 0 and j_ == 0)
                        last = (pi == npairs - 1 and j_ == 1)
                        mm(ov[0:65, :], vb[:, g, 0:65], ptp[:, j_ * NT:(j_ + 1) * NT], first, last, [vb, ptp], [ov])
                    if pi == npairs - 1:
                        deferred.append([5, lambda: finalize(ov, h, qb, mlaT, n_odd, recs, bcs, otmp, pmisc=pmm[1])])

                pend.append(pv)
                if len(pend) > 2:
                    pend.pop(0)()
            n_odd += h % 2
    while pend:
        pend.pop(0)()
    while deferred:
        deferred.pop(0)[1]()
    dump("mlaT", mlaT[:, 0, :], mlaT, [128, TQ], BF16)
    P.release(mB)

    top0 = P.top
    wbufs = [P.sb_top("wb0", [128, 4096], BF16), P.sb_top("wb1", [128, 4096], BF16), P.sb_top("wb2", [128, 4096], BF16)]
    mCm = P.mark()
    mergedT = P.sb("mergedT", [128, 8, TQ], BF16)
    mCh = P.mark()
    hTo = P.sb("hTo", [128, 8, TQ], BF16)
    mC0 = P.mark()
    W = alloc_tilework()
    for i in range(4):
        xb = load_x(W, i, i * NT)
        hT = make_h(W, xb, i, gv1, sh1, gv1)
        evac(hTo[:, :, i * NT:(i + 1) * NT], hT[:], [hT], [hTo], eng="pool")
    P.release(mC0)

    def next_w():
        b = wbufs[cnt["w"] % 3]
        cnt["w"] += 1
        return b

    mC1 = P.mark()
    sig = [P.sb("sig0", [128, NT], F32), P.sb("sig1", [128, NT], F32)]
    prod = [P.sb("prod0", [128, NT], F32), P.sb("prod1", [128, NT], F32)]
    for m in range(8):
        wg = next_w()
        wgv = wg[:, 0:2048].rearrange("p (k g c) -> p k g c", k=8, g=2)
        load_w(wg, wgv[:, :, 0, :], w_in, C_GA + m * 128, [[INW, 128], [128 * INW, 8], [1, 128]])
        load_w(wg, wgv[:, :, 1, :], w_in, C_GB + m * 128, [[INW, 128], [128 * INW, 8], [1, 128]])
        wu = next_w()
        wuv = wu[:, 0:1024].rearrange("p (k g c) -> p k g c", k=4, g=2)
        load_w(wu, wuv[:, :, 0, :], w_up_a, m * 128, [[D, 128], [128 * D, 4], [1, 128]])
        load_w(wu, wuv[:, :, 1, :], w_up_b, m * 128, [[D, 128], [128 * D, 4], [1, 128]])
        for i in range(4):
            tsl = slice(i * NT, (i + 1) * NT)
            for gidx, srcT in ((0, attnAT), (1, mlaT)):
                pg = next_mm()
                for k in range(8):
                    mm(pg[:, :], wgv[:, k, gidx, :], hTo[:, k, tsl], k == 0, k == 7, [wg, hTo], [pg])
                sg = sig[gidx]
                P.add("act", lambda e, sg=sg, pg=pg: e.activation(out=sg[:], in_=pg[:], func=AF.Sigmoid), reads=[pg], writes=[sg])
                py = next_mm()
                for k in range(4):
                    mm(py[:, :], wuv[:, k, gidx, :], srcT[:, k, tsl], k == 0, k == 3, [wu, srcT], [py])
                pr = prod[gidx]
                P.add("dve", lambda e, pr=pr, py=py, sg=sg: e.tensor_tensor(out=pr[:], in0=py[:], in1=sg[:], op=ALU.mult), reads=[py, sg], writes=[pr])
            P.add("pool", lambda e, m=m, tsl=tsl: e.tensor_tensor(out=mergedT[:, m, tsl], in0=prod[0][:], in1=prod[1][:], op=ALU.add),
                  reads=[prod[0], prod[1]], writes=[mergedT])
    dump("mergedT", mergedT[:, 0, :], mergedT, [128, TQ], BF16)
    P.release(mCh)

    x1T = P.sb_top("x1T", [128, 8, TQ], F32)
    mC2 = P.mark()
    xres = [P.sb("xres0", [128, NT], F32), P.sb("xres1", [128, NT], F32)]
    nx = 0
    for m in range(8):
        wo = next_w()
        wov = wo[:, 0:1024].rearrange("p (k c) -> p k c", k=8)
        load_w(wo, wov, w_o, m * 128, [[D, 128], [128 * D, 8], [1, 128]])
        for i in range(4):
            tsl = slice(i * NT, (i + 1) * NT)
            xr = xres[nx % 2]
            nx += 1
            P.dma("sp", [lambda e, xr=xr, m=m, i=i: e.dma_start(out=xr[:], in_=dap(xT, m * 128 * S + i * NT, [[S, 128], [1, NT]]))], xr, writes=[xr])
            po = next_mm()
            for k in range(8):
                mm(po[:, :], wov[:, k, :], mergedT[:, k, tsl], k == 0, k == 7, [wo, mergedT], [po])
            P.add("dve", lambda e, po=po, xr=xr, m=m, tsl=tsl: e.scalar_tensor_tensor(out=x1T[:, m, tsl], in0=po[:], scalar=gt1[:, m:m + 1], in1=xr[:], op0=ALU.mult, op1=ALU.add),
                  reads=[po, xr, modT], writes=[x1T])
    dump("x1T", x1T[:, 0, :], x1T, [128, TQ], F32)
    P.release(mA)

    h2T = P.sb_top("h2T", [128, 8, TQ], BF16)
    mC3 = P.mark()
    sqb = P.sb("sq", [128, 8, NT], BF16)
    rstd = P.sb("rstd", [128, NT], F32)
    tmpc = [P.sb("tmpc0", [128, NT], F32), P.sb("tmpc1", [128, NT], F32)]
    for i in range(4):
        tsl = slice(i * NT, (i + 1) * NT)
        P.add("act", lambda e, tsl=tsl, sqb=sqb: e.activation(out=sqb[:], in_=x1T[:, :, tsl], func=AF.Square), reads=[x1T], writes=[sqb])
        rms_rstd(sqb, 8, rstd, float(D), None)
        for c in range(8):
            tc_ = tmpc[c % 2]
            P.add("dve", lambda e, c=c, tc_=tc_, tsl=tsl, rstd=rstd: e.scalar_tensor_tensor(out=tc_[:], in0=x1T[:, c, tsl], scalar=gv2[:, c:c + 1], in1=rstd[:], op0=ALU.mult, op1=ALU.mult),
                  reads=[x1T, rstd, gv2], writes=[tc_])
            P.add("act", lambda e, c=c, tc_=tc_, tsl=tsl: e.activation(out=h2T[:, c, tsl], in_=tc_[:], func=AF.Identity, bias=sh2[:, c:c + 1], scale=1.0),
                  reads=[tc_, modT], writes=[h2T])
    P.release(mC3)

    mC4 = P.mark()
    actT = P.sb("actT", [128, 22, 1024], BF16)
    sil = [P.sb("sil0", [128, NT], F32), P.sb("sil1", [128, NT], F32)]
    for half in range(2):
        for f in range(22):
            wgu = next_w()
            wguv = wgu[:, 0:2048].rearrange("p (k g c) -> p k g c", k=8, g=2)
            load_w(wgu, wguv[:, :, 0, :], w_gate, f * 128, [[DFF, 128], [128 * DFF, 8], [1, 128]])
            load_w(wgu, wguv[:, :, 1, :], w_up, f * 128, [[DFF, 128], [128 * DFF, 8], [1, 128]])
            for i in range(2):
                tsl = slice(half * 1024 + i * NT, half * 1024 + (i + 1) * NT)
                pg = next_mm()
                for k in range(8):
                    mm(pg[:, :], wguv[:, k, 0, :], h2T[:, k, tsl], k == 0, k == 7, [wgu, h2T], [pg])
                sl_ = sil[i]
                P.add("act", lambda e, sl_=sl_, pg=pg: e.activation(out=sl_[:], in_=pg[:], func=AF.Silu), reads=[pg], writes=[sl_])
                pu = next_mm()
                for k in range(8):
                    mm(pu[:, :], wguv[:, k, 1, :], h2T[:, k, tsl], k == 0, k == 7, [wgu, h2T], [pu])
                P.add("dve", lambda e, sl_=sl_, pu=pu, f=f, i=i: e.tensor_tensor(out=actT[:, f, i * NT:(i + 1) * NT], in0=pu[:], in1=sl_[:], op=ALU.mult),
                      reads=[pu, sl_], writes=[actT])
        for m in range(8):
            wd = next_w()
            wdv = wd[:, 0:2816].rearrange("p (k c) -> p k c", k=22)
            load_w(wd, wdv, w_down, m * 128, [[D, 128], [128 * D, 22], [1, 128]])
            for i in range(2):
                tsl = slice(half * 1024 + i * NT, half * 1024 + (i + 1) * NT)
                pd = next_mm()
                for k in range(22):
                    mm(pd[:, :], wdv[:, k, :], actT[:, k, i * NT:(i + 1) * NT], k == 0, k == 21, [wd, actT], [pd])
                P.add("dve", lambda e, pd=pd, m=m, tsl=tsl: e.scalar_tensor_tensor(out=x1T[:, m, tsl], in0=pd[:], scalar=gt2[:, m:m + 1], in1=x1T[:, m, tsl], op0=ALU.mult, op1=ALU.add),
                      reads=[pd, x1T, modT], writes=[x1T])

    P.release(mC4)
    P.top += 32768
    sqb = P.sb("sq", [128, 8, NT], BF16)
    rstd = P.sb("rstd", [128, NT], F32)
    ob = [P.sb("ob0", [128, 8, NT], F32), P.sb("ob1", [128, 8, NT], F32)]
    for i in range(4):
        tsl = slice(i * NT, (i + 1) * NT)
        o_ = ob[i % 2]
        P.add("act", lambda e, tsl=tsl, sqb=sqb: e.activation(out=sqb[:], in_=x1T[:, :, tsl], func=AF.Square), reads=[x1T], writes=[sqb])
        rms_rstd(sqb, 8, rstd, float(D), None)
        for c in range(8):
            P.add("dve", lambda e, c=c, o_=o_, tsl=tsl, rstd=rstd: e.scalar_tensor_tensor(out=o_[:, c, :], in0=x1T[:, c, tsl], scalar=gfin[:, c:c + 1], in1=rstd[:], op0=ALU.mult, op1=ALU.mult),
                  reads=[x1T, rstd, gfin], writes=[o_])
        P.dma("sp", [lambda e, o_=o_, i=i: e.dma_start(out=dap(outT, i * NT, [[TQ, 128], [128 * TQ, 8], [1, NT]]), in_=o_[:])], o_, reads=[o_])
    stats = P.emit(final_wait_bufs=ob + dumped)
    return nc, stats, dbg


def _t5_bucket_np(dist):
    max_exact = 16
    d = np.maximum(dist, 1).astype(np.float32)
    log_b = max_exact + (np.log(d / np.float32(max_exact)) / np.float32(math.log(2048 / max_exact)) * np.float32(32 - max_exact)).astype(np.int32)
    log_b = np.minimum(log_b, 31)
    return np.where(dist < max_exact, dist, log_b)


def _constants():
    OH = np.zeros((128, ROWW), np.float32)
    e = np.arange(ROWW)
    delta = e - 511
    mult = np.zeros(ROWW, np.int64)
    for (w, d) in ((128, 1), (512, 4), (2048, 16)):
        mult += ((delta >= 0) & (delta <= w) & (delta % d == 0)).astype(np.int64)
    valid = mult > 0
    bucket = _t5_bucket_np(np.clip(delta, 0, 2048).astype(np.int32))
    OH[bucket[valid], e[valid]] = 1.0
    OH[32, :] = NEG
    OH[32, valid] = np.log(mult[valid].astype(np.float64)).astype(np.float32)
    ident = np.eye(128, dtype=np.float32)
    anti = np.ascontiguousarray(ident[::-1])
    kk = np.arange(128)[:, None]
    cc = np.arange(896)[None, :]
    MT = np.where(cc - 384 - kk >= 0, 0.0, NEG).astype(np.float32)
    misc = np.zeros((128, 4), np.float32)
    half = 16
    freqs = (10000.0 ** (-np.arange(half, dtype=np.float32) / np.float32(half))).astype(np.float32)
    for r in range(32):
        misc[64 + r, 0] = freqs[r % 16]
        misc[64 + r, 1] = -1.0 if r < 16 else 1.0
    misc[:, 2] = EPS
    return OH, ident, anti, MT, misc


def _colT(v, n):
    return np.ascontiguousarray(v.reshape(n, 128).T)


def make_in_maps(x, c, positions, rel_bias, w_ada, b_ada, g_mix, w_in, g_q_lora, w_uq, g_kv_lora, w_ukv,
                 w_up_a, w_up_b, w_o, g_ffn, w_gate, w_up, w_down, g_final):
    OH, ident, anti, MT, misc = _constants()
    w_in0 = np.ascontiguousarray(w_in[0])
    kr = w_in0[:, C_KR:C_KR + 32]
    w_kr = np.zeros((D, 2, 96), np.float32)
    w_kr[:, 0, 64:96] = kr
    w_kr[:, 1, 64:96] = np.concatenate([kr[:, 16:32], kr[:, 0:16]], axis=1)
    wuq = w_uq[0]
    wuqs = wuq.copy()
    wuqs[:, :, 64:80] = wuq[:, :, 80:96]
    wuqs[:, :, 80:96] = wuq[:, :, 64:80]
    rbT = np.zeros((128, 128), np.float32)
    rbT[0:32, 0:8] = rel_bias.T
    rbT[32, 0:8] = 1.0
    shared = {
        "rbT": rbT, "OH": OH, "w_ada": np.ascontiguousarray(w_ada[0]), "b_adaT": _colT(b_ada[0], 48),
        "g_mixT": _colT(g_mix[0], 8), "w_in": w_in0, "w_kr": w_kr, "g_qT": _colT(g_q_lora[0], 6),
        "w_uq": np.ascontiguousarray(wuq.reshape(768, 768)), "w_uqs": np.ascontiguousarray(wuqs.reshape(768, 768)),
        "g_kvT": _colT(g_kv_lora[0], 2), "w_ukv": np.ascontiguousarray(w_ukv[0].reshape(256, 1024)),
        "w_up_a": np.ascontiguousarray(w_up_a[0]), "w_up_b": np.ascontiguousarray(w_up_b[0]), "w_o": np.ascontiguousarray(w_o[0]),
        "g_ffnT": _colT(g_ffn[0], 8), "w_gate": np.ascontiguousarray(w_gate[0]), "w_up": np.ascontiguousarray(w_up[0]),
        "w_down": np.ascontiguousarray(w_down[0]), "g_finT": _colT(g_final, 8),
        "ident": ident, "anti": anti, "MT": MT, "misc": misc,
    }
    in_maps = []
    for core in range(8):
        b, j = core // 4, core % 4
        order = [(j - s) % 4 for s in range(4)]
        xb = x[b]
        xs = np.concatenate([xb[ch * TQ:(ch + 1) * TQ] for ch in order], axis=0)
        pos = np.concatenate([positions[b, ch * TQ:(ch + 1) * TQ] for ch in order]).astype(np.int32).reshape(1, S)
        pmv = np.zeros((128, 4), np.float32)
        for s in range(1, 4):
            if j - s < 0:
                pmv[:, s] = NEG
        m = dict(shared)
        m["xT"] = np.ascontiguousarray(xs.T)
        m["posr"] = pos
        m["cT"] = _colT(c[b], 8)
        m["pm"] = pmv
        in_maps.append(m)
    return in_maps


_CACHE = {}


def kernel(**inputs):
    inputs = {k: np.asarray(v) for k, v in inputs.items()}
    if "nc" not in _CACHE:
        _CACHE["nc"] = build(False)[0]
    nc = _CACHE["nc"]
    in_maps = make_in_maps(**inputs)
    res = run_bass_kernel_spmd(nc, in_maps, core_ids=list(range(8)))
    out = np.empty((2, S, D), np.float32)
    for core in range(8):
        b, j = core // 4, core % 4
        out[b, j * TQ:(j + 1) * TQ, :] = res.results[core]["outT"].T
    return out
```

```python
import math
import numpy as np
import concourse.bass as bass
import concourse.mybir as mybir
from concourse.bass_utils import run_bass_kernel_spmd

F32 = mybir.dt.float32
BF16 = mybir.dt.bfloat16
I32 = mybir.dt.int32
ALU = mybir.AluOpType
AF = mybir.ActivationFunctionType

ENGS = ("pe", "act", "dve", "pool", "sp")
NEG = -30000.0
MAXW = 1
S = 8192
D = 1024
TQ = 2048
NT = 512
DFF = 2816
EPS = 1e-6
C_AQ, C_AK, C_AV, C_CQ, C_CKV, C_KR, C_GA, C_GB = 0, 512, 1024, 1536, 2304, 2560, 2592, 3616
INW = 4640
TW = 2944
ROWW = 3072


class Buf:
    __slots__ = ("name", "t", "lastw", "readers", "dsem", "dcount", "off")

    def __init__(self, name, t=None, off=None):
        self.name = name
        self.t = t
        self.off = off
        self.lastw = None
        self.readers = {}
        self.dsem = None
        self.dcount = 0

    def __getitem__(self, idx):
        if self.off is None:
            return self.t[idx]
        if not isinstance(idx, tuple):
            idx = (idx, slice(None))
        p, c = idx
        if isinstance(c, slice):
            a = 0 if c.start is None else c.start
            b = 512 if c.stop is None else c.stop
            c = slice(a + self.off, b + self.off)
        else:
            c = c + self.off
        return self.t[p, c]


class Op:
    __slots__ = ("eng", "fn", "deps", "signal", "ms", "is_dma", "sem", "val")

    def __init__(self, eng, fn):
        self.eng = eng
        self.fn = fn
        self.deps = []
        self.signal = False
        self.ms = 0
        self.is_dma = False
        self.sem = None
        self.val = 0


class Prog:
    def __init__(self, nc, base=18432, top=229376):
        self.nc = nc
        self.q = {e: [] for e in ENGS}
        self.esem = {}
        self.dma_bufs = []
        self.bar = []
        self.ptr = base
        self.top = top
        self.nalloc = 0

    def sb(self, name, shape, dtype):
        per = int(np.prod(shape[1:])) * mybir.dt.size(dtype)
        per = (per + 63) // 64 * 64
        assert self.ptr + per <= self.top, f"SBUF arena overflow at {name}: {self.ptr}+{per} > {self.top}"
        self.nalloc += 1
        t = self.nc.alloc_sbuf_tensor_at(f"{name}_{self.nalloc}", list(shape), dtype, offset=self.ptr)
        self.ptr += per
        return Buf(name, t)

    def sb_top(self, name, shape, dtype):
        per = int(np.prod(shape[1:])) * mybir.dt.size(dtype)
        per = (per + 63) // 64 * 64
        assert self.top - per >= self.ptr, f"SBUF arena overflow (top) at {name}"
        self.nalloc += 1
        self.top -= per
        t = self.nc.alloc_sbuf_tensor_at(f"{name}_{self.nalloc}", list(shape), dtype, offset=self.top)
        return Buf(name, t)

    def mark(self):
        return self.ptr

    def release(self, m):
        self.ptr = m
        self.barrier()

    def ps(self, name):
        return Buf(name, self.nc.alloc_psum_tensor(name, [128, 512], F32))

    def barrier(self):
        deps = []
        for e in ENGS:
            if self.q[e]:
                deps.append(self.q[e][-1])
        for b in self.dma_bufs:
            if b.lastw is not None and b.lastw.is_dma:
                deps.append(b.lastw)
            for r in b.readers.values():
                if r.is_dma:
                    deps.append(r)
        self.bar = deps

    def _track(self, op, reads, writes):
        deps = {}
        for d in self.bar:
            deps[id(d)] = d
        for b in reads:
            d = b.lastw
            if d is not None:
                deps[id(d)] = d
        for b in writes:
            d = b.lastw
            if d is not None:
                deps[id(d)] = d
            for r in b.readers.values():
                deps[id(r)] = r
        deps.pop(id(op), None)
        op.deps = list(deps.values())
        for b in reads:
            b.readers[op.eng] = op
        for b in writes:
            b.lastw = op
            b.readers = {}

    def add(self, eng, fn, reads=(), writes=()):
        op = Op(eng, fn)
        self._track(op, reads, writes)
        self.q[eng].append(op)
        return op

    def dma(self, queue, fns, sem_buf, reads=(), writes=()):
        op = Op(queue, fns)
        op.is_dma = True
        if sem_buf.dsem is None:
            sem_buf.dsem = self.nc.alloc_semaphore("d%d_%s" % (len(self.dma_bufs), sem_buf.name))
            self.dma_bufs.append(sem_buf)
        sem_buf.dcount += 16 * len(fns)
        op.sem = sem_buf.dsem
        op.val = sem_buf.dcount
        self._track(op, reads, writes)
        self.q[queue].append(op)
        return op

    def emit(self, final_wait_bufs=()):
        nc = self.nc
        for e in ENGS:
            self.esem[e] = nc.alloc_semaphore("e_" + e)
        for e in ENGS:
            for op in self.q[e]:
                for d in op.deps:
                    if d.is_dma:
                        continue
                    if d.eng == "pe" and e == "pe" and not op.is_dma:
                        continue
                    d.signal = True
        for e in ENGS:
            c = 0
            for op in self.q[e]:
                if op.signal and not op.is_dma:
                    c += 1
                    op.ms = c
        stats = {}

        def run(e, eng):
            seen = {}
            nw = 0
            for op in self.q[e]:
                waits = []
                for d in op.deps:
                    if d.is_dma:
                        s, v = d.sem, d.val
                    else:
                        if d.eng == "pe" and e == "pe" and not op.is_dma:
                            continue
                        s, v = self.esem[d.eng], d.ms
                    k = id(s)
                    if seen.get(k, 0) >= v:
                        continue
                    seen[k] = v
                    waits.append((s, v))
                    nw += 1
                emb = waits[-MAXW:] if MAXW > 0 else []
                for (s, v) in waits[:len(waits) - len(emb)]:
                    eng.wait_ge(s, v)
                if op.is_dma:
                    first = True
                    for f in op.fn:
                        ins = f(eng)
                        if first:
                            for (s, v) in emb:
                                ins._wait_ge(s, v)
                            first = False
                        ins.then_inc(op.sem, 16)
                else:
                    ins = op.fn(eng)
                    for (s, v) in emb:
                        ins._wait_ge(s, v)
                    if op.signal:
                        ins.then_inc(self.esem[e], 1)
            if e == "sp":
                for b in final_wait_bufs:
                    eng.wait_ge(b.dsem, b.dcount)
            stats[e] = (len(self.q[e]), nw)

        with nc.Block() as block:
            @block.tensor
            def _(eng):
                run("pe", eng)

            @block.scalar
            def _(eng):
                run("act", eng)

            @block.vector
            def _(eng):
                run("dve", eng)

            @block.gpsimd
            def _(eng):
                run("pool", eng)

            @block.sync
            def _(eng):
                run("sp", eng)
        return stats


def dap(t, off, dims):
    return bass.AP(t, off, [list(d) for d in dims])


def build(debug=False):
    nc = bass.Bass("TRN2", target_bir_lowering=False)
    P = Prog(nc)

    def din(name, shape, dt=F32):
        return nc.dram_tensor(name, list(shape), dt, kind="ExternalInput")

    xT = din("xT", [D, S])
    posr = din("posr", [1, S], I32)
    cT = din("cT", [128, 8])
    pm_d = din("pm", [128, 4])
    rbT_d = din("rbT", [128, 128])
    OH_d = din("OH", [128, ROWW])
    w_ada = din("w_ada", [D, 6 * D])
    b_adaT = din("b_adaT", [128, 48])
    g_mixT = din("g_mixT", [128, 8])
    w_in = din("w_in", [D, INW])
    w_kr = din("w_kr", [D, 2, 96])
    g_qT = din("g_qT", [128, 6])
    w_uq = din("w_uq", [768, 8 * 96])
    w_uqs = din("w_uqs", [768, 8 * 96])
    g_kvT = din("g_kvT", [128, 2])
    w_ukv = din("w_ukv", [256, 8 * 128])
    w_up_a = din("w_up_a", [512, D])
    w_up_b = din("w_up_b", [512, D])
    w_o = din("w_o", [D, D])
    g_ffnT = din("g_ffnT", [128, 8])
    w_gate = din("w_gate", [D, DFF])
    w_up = din("w_up", [D, DFF])
    w_down = din("w_down", [DFF, D])
    g_finT = din("g_finT", [128, 8])
    ident_d = din("ident", [128, 128])
    anti_d = din("anti", [128, 128])
    MT_d = din("MT", [128, 896])
    misc_d = din("misc", [128, 4])
    outT = nc.dram_tensor("outT", [D, TQ], F32, kind="ExternalOutput")
    vA_d = nc.dram_tensor("vA_scr", [8, ROWW], F32)
    dbg = {}

    def dbg_out(name, shape, dt=F32):
        dbg[name] = nc.dram_tensor("dbg_" + name, list(shape), dt, kind="ExternalOutput")
        return dbg[name]

    pmm = [P.ps("pmm0"), P.ps("pmm1")]
    scP = [nc.alloc_psum_tensor("scP0", [128, 1024], F32), nc.alloc_psum_tensor("scP1", [128, 1024], F32)]
    hb = [Buf("h0", scP[0], 0), Buf("h1", scP[0], 512), Buf("h2", scP[1], 0), Buf("h3", scP[1], 512)]
    psc = [hb[0], hb[1]]
    pss = hb[2]
    pmisc = hb[3]
    pov = [P.ps("pov0"), P.ps("pov1")]
    cnt = {"mm": 0, "sc": 0, "ov": 0, "ev": 0, "w": 0}

    ones_bf = P.sb("ones_bf", [128, 128], BF16)
    ones_f = P.sb("ones_f", [128, 128], F32)
    ident = P.sb("ident", [128, 128], BF16)
    anti = P.sb("anti", [128, 128], BF16)
    MT = P.sb("MT", [128, 896], BF16)
    misc = P.sb("misc", [128, 4], F32)
    pm = P.sb("pm", [128, 4], F32)
    modT = P.sb("modT", [128, 48], F32)
    gv1 = P.sb("gv1", [128, 8], F32)
    gv2 = P.sb("gv2", [128, 8], F32)
    gfin = P.sb("gfin", [128, 8], F32)
    gq = P.sb("gq", [128, 6], F32)
    gkv = P.sb("gkv", [128, 2], F32)
    attnAT = P.sb("attnAT", [128, 4, TQ], BF16)
    dump_stage = P.sb("dump_stage", [128, 512], F32) if debug else None

    P.add("dve", lambda e: e.memset(ones_bf[:], 1.0), writes=[ones_bf])
    P.add("dve", lambda e: e.memset(ones_f[:], 0.0), writes=[ones_f])
    P.add("dve", lambda e: e.memset(ones_f[64:65, :], 1.0), writes=[ones_f])
    P.dma("pool", [lambda e: e.dma_start(out=ident[:], in_=ident_d[:]),
                   lambda e: e.dma_start(out=anti[:], in_=anti_d[:]),
                   lambda e: e.dma_start(out=MT[:], in_=MT_d[:])], ident, writes=[ident, anti, MT])
    P.dma("sp", [lambda e: e.dma_start(out=misc[:], in_=misc_d[:]),
                 lambda e: e.dma_start(out=pm[:], in_=pm_d[:]),
                 lambda e: e.dma_start(out=gfin[:], in_=g_finT[:]),
                 lambda e: e.dma_start(out=gq[:], in_=g_qT[:]),
                 lambda e: e.dma_start(out=gkv[:], in_=g_kvT[:])], misc, writes=[misc, pm, gfin, gq, gkv])
    eps_ap = misc[:, 2:3]
    zero_ap = misc[:, 3:4]

    def evac(out_ap, in_ap, reads, writes, eng=None, scale=None):
        if eng is None:
            eng = "act" if cnt["ev"] % 2 == 0 else "dve"
            cnt["ev"] += 1
        if eng == "act":
            if scale is None:
                P.add("act", lambda e: e.activation(out=out_ap, in_=in_ap, func=AF.Copy), reads=reads, writes=writes)
            else:
                P.add("act", lambda e: e.activation(out=out_ap, in_=in_ap, func=AF.Copy, scale=scale), reads=reads, writes=writes)
        else:
            if scale is None:
                P.add(eng, lambda e: e.tensor_copy(out=out_ap, in_=in_ap), reads=reads, writes=writes)
            else:
                P.add(eng, lambda e: e.tensor_scalar(out=out_ap, in0=in_ap, scalar1=scale, scalar2=None, op0=ALU.mult), reads=reads, writes=writes)

    def mm(out_ap, lhsT, rhs, start, stop, reads, writes):
        P.add("pe", lambda e: e.matmul(out_ap, lhsT=lhsT, rhs=rhs, start=start, stop=stop), reads=reads, writes=writes)

    def next_mm():
        b = pmm[cnt["mm"] % 2]
        cnt["mm"] += 1
        return b

    def load_w(buf, out_ap, dram, off, dims):
        P.dma("pool", [lambda e: e.dma_start(out=out_ap, in_=dap(dram, off, dims))], buf, writes=[buf])

    def dump(name, src_ap, src_buf, shape, dt):
        if not debug:
            return
        o = dbg_out(name, shape, dt)
        P.dma("sp", [lambda e: e.dma_start(out=o[:], in_=src_ap)], src_buf, reads=[src_buf])
        dumped.append(src_buf)

    dumped = []

    tmp48 = P.sb("tmp48", [128, 48], F32)
    gtmp = P.sb("gtmp", [128, 16], F32)
    condT = P.sb("condT", [128, 8], F32)
    condB = P.sb("condB", [128, 8], BF16)
    m0 = P.mark()
    P.dma("sp", [lambda e: e.dma_start(out=condT[:], in_=cT[:]),
                 lambda e: e.dma_start(out=tmp48[:], in_=b_adaT[:]),
                 lambda e: e.dma_start(out=gtmp[:, 0:8], in_=g_mixT[:]),
                 lambda e: e.dma_start(out=gtmp[:, 8:16], in_=g_ffnT[:])], condT, writes=[condT, tmp48, gtmp])
    P.add("act", lambda e: e.activation(out=condB[:], in_=condT[:], func=AF.Silu), reads=[condT], writes=[condB])

    def ada_slab(sl, wb):
        load_w(wb, wb[:], w_ada, sl * 512, [[6 * D, 128], [128 * 6 * D, 8], [1, 512]])
        for cc in range(4):
            ci = sl * 4 + cc
            for kc in range(8):
                mm(pmisc[:, ci:ci + 1], wb[:, kc, cc * 128:(cc + 1) * 128], condB[:, kc:kc + 1], kc == 0, kc == 7,
                   [wb, condB], [pmisc])

    wst0 = [P.sb("wst0", [128, 8, 512], BF16), P.sb("wst1", [128, 8, 512], BF16)]
    for sl in range(4):
        ada_slab(sl, wst0[sl % 2])
    P.add("dve", lambda e: e.tensor_tensor(out=modT[:, 0:16], in0=pmisc[:, 0:16], in1=tmp48[:, 0:16], op=ALU.add), reads=[pmisc, tmp48], writes=[modT])
    P.add("dve", lambda e: e.scalar_tensor_tensor(out=gv1[:], in0=modT[:, 8:16], scalar=1.0, in1=gtmp[:, 0:8], op0=ALU.add, op1=ALU.mult),
          reads=[modT, gtmp], writes=[gv1])
    sh1, gt1, sh2, gt2 = modT[:, 0:8], modT[:, 16:24], modT[:, 24:32], modT[:, 40:48]

    rbT = P.sb("rbT", [128, 128], F32)
    OH = P.sb("OH", [128, ROWW], F32)
    vA_sb = P.sb("vA_sb", [8, ROWW], F32)
    P.dma("sp", [lambda e: e.dma_start(out=rbT[:], in_=rbT_d[:]), lambda e: e.dma_start(out=OH[:], in_=OH_d[:])], rbT, writes=[rbT, OH])
    for ch in range(6):
        pb_ = next_mm()
        mm(pb_[:, :], rbT[:, :], OH[:, ch * 512:(ch + 1) * 512], True, True, [rbT, OH], [pb_])
        evac(vA_sb[:, ch * 512:(ch + 1) * 512], pb_[0:8, :], [pb_], [vA_sb], eng="dve")
    vAd = Buf("vAd")
    P.dma("sp", [lambda e: e.dma_start(out=vA_d[:], in_=vA_sb[:])], vA_sb, reads=[vA_sb], writes=[vAd])
    P.release(m0)

    def alloc_tilework(double_h=True, double_x=True):
        w = {}
        x0 = P.sb("xbuf0", [128, 8, NT], F32)
        w["xbuf"] = [x0, P.sb("xbuf1", [128, 8, NT], F32) if double_x else x0]
        h0 = P.sb("hT0", [128, 8, NT], BF16)
        w["hT"] = [h0, P.sb("hT1", [128, 8, NT], BF16) if double_h else h0]
        w["sq"] = P.sb("sq", [128, 8, NT], BF16)
        w["rstd"] = P.sb("rstd", [128, NT], F32)
        w["tmpc"] = [P.sb("tmpc0", [128, NT], F32), P.sb("tmpc1", [128, NT], F32)]
        return w

    def load_x(w, i, t0):
        xb = w["xbuf"][i % 2]
        P.dma("sp", [lambda e: e.dma_start(out=xb[:], in_=dap(xT, t0, [[S, 128], [128 * S, 8], [1, NT]]))], xb, writes=[xb])
        return xb

    def rms_rstd(sqb, nch, rstd, nfeat, src_reads):
        for c in range(nch):
            mm(pss[:, :], ones_bf[:, :], sqb[:, c, :], c == 0, c == nch - 1, [ones_bf, sqb], [pss])
        P.add("act", lambda e: e.activation(out=rstd[:], in_=pss[:], func=AF.Ln, bias=eps_ap, scale=1.0 / nfeat), reads=[pss, misc], writes=[rstd])
        P.add("act", lambda e: e.activation(out=rstd[:], in_=rstd[:], func=AF.Exp, scale=-0.5), reads=[rstd], writes=[rstd])

    def make_h(w, xb, i, gv, sh, gvbuf):
        hT = w["hT"][i % 2]
        sqb, rstd = w["sq"], w["rstd"]
        P.add("act", lambda e: e.activation(out=sqb[:], in_=xb[:], func=AF.Square), reads=[xb], writes=[sqb])
        rms_rstd(sqb, 8, rstd, float(D), None)
        for c in range(8):
            tc_ = w["tmpc"][c % 2]
            P.add("dve", lambda e, c=c, tc_=tc_: e.scalar_tensor_tensor(out=tc_[:], in0=xb[:, c, :], scalar=gv[:, c:c + 1], in1=rstd[:], op0=ALU.mult, op1=ALU.mult),
                  reads=[xb, rstd, gvbuf], writes=[tc_])
            P.add("act", lambda e, c=c, tc_=tc_: e.activation(out=hT[:, c, :], in_=tc_[:], func=AF.Identity, bias=sh[:, c:c + 1], scale=1.0),
                  reads=[tc_, modT], writes=[hT])
        return hT

    def proj_fm(hT, wbuf, wap_fn, nk, out_fn, M=128, rhs_fn=None, ev_eng=None, scale=None, extra_reads=()):
        pb_ = next_mm()
        for k in range(nk):
            rhs = hT[:, k, :] if rhs_fn is None else rhs_fn(k)
            mm(pb_[0:M, :], wap_fn(k), rhs, k == 0, k == nk - 1, [hT, wbuf] + list(extra_reads), [pb_])
        out_fn(pb_)

    mA = P.mark()
    KT_A = P.sb("KT_A", [128, 4, 4096], BF16)
    V_A = P.sb("V_A", [128, 32, 8, 65], BF16)
    QT_A = P.sb("QT_A", [128, 4, TQ], BF16)
    QT_A1 = P.sb("QT_A1", [128, 4, TQ], BF16)
    P.add("pool", lambda e: e.memset(QT_A[64:128, :, :], 0.0), writes=[QT_A])
    P.add("pool", lambda e: e.memset(QT_A1[0:64, :, :], 0.0), writes=[QT_A1])
    mA1 = P.mark()
    wA = P.sb("wA", [128, 8, 1536], BF16)
    W = alloc_tilework(double_h=True, double_x=False)
    wst = [P.sb("wst0", [128, 8, 512], BF16), P.sb("wst1", [128, 8, 512], BF16)]
    for j3 in range(3):
        load_w(wA, wA[:, :, j3 * 512:(j3 + 1) * 512], w_in, j3 * 512, [[INW, 128], [128 * INW, 8], [1, 512]])
    P.add("pool", lambda e: e.memset(V_A[:, :, :, 64:65], 1.0), writes=[V_A])
    tiles = [(1, i) for i in range(4)] + [(0, i) for i in range(4)]
    for n, (s, i) in enumerate(tiles):
        xb = load_x(W, n, s * TQ + i * NT)
        hT = make_h(W, xb, n, gv1, sh1, gv1)
        if debug and n == 4:
            dump("hT0", hT[:, 0, :], hT, [128, 512], BF16)
        kp0 = (1 - s) * TQ + i * NT
        for hp in range(4):
            proj_fm(hT, wA, lambda k, hp=hp: wA[:, k, C_AK + hp * 128:C_AK + (hp + 1) * 128], 8,
                    lambda pb_, hp=hp: evac(KT_A[:, hp, kp0:kp0 + NT], pb_[:, :], [pb_], [KT_A]))
        for sub in range(4):
            pb_ = next_mm()
            for k in range(8):
                mm(pb_[:, :], hT[:, k, sub * 128:(sub + 1) * 128], wA[:, k, C_AV:C_AV + 512], k == 0, k == 7, [hT, wA], [pb_])
            kt = kp0 // 128 + sub
            evac(V_A[:, kt, :, 0:64], pb_[:, :].rearrange("p (h d) -> p h d", h=8), [pb_], [V_A])
        ada_slab(4 + n, wst[n % 2])
        if s == 0:
            for hp in range(4):
                proj_fm(hT, wA, lambda k, hp=hp: wA[:, k, C_AQ + hp * 128:C_AQ + (hp + 1) * 128], 8,
                        lambda pb_, hp=hp: (evac(QT_A[0:64, hp, i * NT:(i + 1) * NT], pb_[0:64, :], [pb_], [QT_A], scale=0.125),
                                            evac(QT_A1[64:128, hp, i * NT:(i + 1) * NT], pb_[64:128, :], [pb_], [QT_A1], scale=0.125)))
    P.add("dve", lambda e: e.tensor_tensor(out=modT[:, 16:48], in0=pmisc[:, 16:48], in1=tmp48[:, 16:48], op=ALU.add), reads=[pmisc, tmp48], writes=[modT])
    P.add("dve", lambda e: e.scalar_tensor_tensor(out=gv2[:], in0=modT[:, 32:40], scalar=1.0, in1=gtmp[:, 8:16], op0=ALU.add, op1=ALU.mult),
          reads=[modT, gtmp], writes=[gv2])
    dump("modT", modT[:], modT, [128, 48], F32)
    dump("KT_A", KT_A[:, 0, :], KT_A, [128, 4096], BF16)
    dump("QT_A", QT_A[:, 0, :], QT_A, [128, TQ], BF16)
    dump("V_A", V_A[:, :, 0, :], V_A, [128, 32, 65], BF16)

    P.release(mA1)
    TB = [P.sb("TB0", [128, TW], BF16), P.sb("TB1", [128, TW], BF16)]
    PT = [P.sb("PT0", [128, NT], BF16), P.sb("PT1", [128, NT], BF16), P.sb("PT2", [128, NT], BF16)]
    recs = P.sb("recs", [128, NT], F32)
    P.add("dve", lambda e: e.memset(recs[:], 0.0), writes=[recs])
    bcs = P.sb("bcs", [64, NT], F32)
    otmp = [P.sb("otmp0", [64, NT], BF16), P.sb("otmp1", [64, NT], BF16)]

    def finalize(ov, h, qb, dstT, n_odd, recs, bcs, otmp, pmisc=pmisc):
        hp, odd = h // 2, h % 2
        P.add("dve", lambda e: e.tensor_copy(out=recs[0:65, :], in_=ov[0:65, :]), reads=[ov], writes=[recs])
        mm(pmisc[:, :], ones_f[:, :], recs[:, :], True, True, [ones_f, recs], [pmisc])
        P.add("act", lambda e: e.activation(out=bcs[:], in_=pmisc[0:64, :], func=AF.Ln), reads=[pmisc], writes=[bcs])
        P.add("act", lambda e: e.activation(out=bcs[:], in_=bcs[:], func=AF.Exp, scale=-1.0), reads=[bcs], writes=[bcs])
        if not odd:
            P.add("dve", lambda e: e.tensor_tensor(out=dstT[0:64, hp, qb * NT:(qb + 1) * NT], in0=recs[0:64, :], in1=bcs[:], op=ALU.mult),
                  reads=[recs, bcs], writes=[dstT])
        else:
            ot = otmp[n_odd % 2]
            P.add("dve", lambda e: e.tensor_tensor(out=ot[:], in0=recs[0:64, :], in1=bcs[:], op=ALU.mult), reads=[recs, bcs], writes=[ot])
            P.dma("sp", [lambda e: e.dma_start(out=dstT[64:128, hp, qb * NT:(qb + 1) * NT], in_=ot[:])], ot, reads=[ot], writes=[dstT])


    class AttnPipe:
        def __init__(self, PTs, depth=2):
            self.scs = [psc[0], psc[1], pss]
            self.PTs = PTs
            self.depth = depth
            self.pend = []
            self.n = 0
            self.deferred = []

        def defer(self, fn, after=8):
            self.deferred.append([after, fn])

        def _tick(self):
            for d in self.deferred:
                d[0] -= 1
            while self.deferred and self.deferred[0][0] <= 0:
                self.deferred.pop(0)[1]()

        def tile(self, qk_fn, bias_ap, scale, pv_fn, bias_reads, mul_ap=None, mul_buf=None, PT2=None):
            self._tick()
            sc = self.scs[self.n % 3]
            pt = self.PTs[self.n % 3]
            qk_fn(sc)
            P.add("act", lambda e, sc=sc, pt=pt: e.activation(out=pt[:], in_=sc[:], func=AF.Exp, bias=bias_ap, scale=scale),
                  reads=[sc] + list(bias_reads), writes=[pt])
            if mul_ap is not None:
                pt2 = PT2[self.n % len(PT2)]
                P.add("dve", lambda e, pt=pt, pt2=pt2: e.tensor_tensor(out=pt2[:], in0=pt[:], in1=mul_ap, op=ALU.mult),
                      reads=[pt, mul_buf], writes=[pt2])
                pt = pt2
            self.n += 1
            self.pend.append((pv_fn, pt))
            assert self.depth < (len(PT2) if mul_ap is not None else len(self.PTs))
            if len(self.pend) > self.depth:
                f, p_ = self.pend.pop(0)
                f(p_)

        def flush(self):
            while self.pend:
                f, p_ = self.pend.pop(0)
                f(p_)
            while self.deferred:
                self.deferred.pop(0)[1]()

    n_odd = 0
    pipe = AttnPipe(PT, depth=3)
    PT2 = [P.sb("PTb%d" % i_, [128, NT], BF16) for i_ in range(4)]
    TE = [P.sb("TE0", [128, TW], BF16), P.sb("TE1", [128, TW], BF16)]
    for h in range(8):
        hp, pb0 = h // 2, 64 * (h % 2)
        tb = TB[h % 2]
        te = TE[h % 2]
        P.dma("pool", [lambda e, tb=tb, h=h: e.dma_start(out=tb[:], in_=dap(vA_d, h * ROWW, [[1, 128], [1, TW]]))], tb, reads=[vAd], writes=[tb])
        for c0 in range(0, TW, NT):
            cw = min(NT, TW - c0)
            pb_ = next_mm()
            mm(pb_[:, 0:cw], anti[:, :], tb[:, c0:c0 + cw], True, True, [anti, tb], [pb_])
            P.add("act", lambda e, pb_=pb_, te=te, c0=c0, cw=cw: e.activation(out=te[:, c0:c0 + cw], in_=pb_[:, 0:cw], func=AF.Exp),
                  reads=[pb_], writes=[te])
        for qb in range(4):
            ov = pov[cnt["ov"] % 2]
            cnt["ov"] += 1
            kts = list(range(4 * qb, 4 * qb + 20))
            for n_, kt in enumerate(kts):
                u = 16 + 4 * qb - kt
                qsel = QT_A if pb0 == 0 else QT_A1

                def qk(sc, kt=kt, qsel=qsel, hp=hp, qb=qb):
                    mm(sc[:, :], KT_A[:, hp, kt * 128:(kt + 1) * 128], qsel[:, hp, qb * NT:(qb + 1) * NT], True, True,
                       [KT_A, qsel], [sc])

                def pv(pt, kt=kt, h=h, qb=qb, ov=ov, first=(n_ == 0), last=(n_ == len(kts) - 1), n_odd=n_odd):
                    mm(ov[0:65, :], V_A[:, kt, h, 0:65], pt[:, :], first, last, [V_A, pt], [ov])
                    if last:
                        pipe.defer(lambda ov=ov, h=h, qb=qb, n_odd=n_odd: finalize(ov, h, qb, attnAT, n_odd, recs, bcs, otmp))

                bias_ap = pm[:, 1:2] if kt < 16 else zero_ap
                pipe.tile(qk, bias_ap, 1.0, pv, [pm, misc], mul_ap=te[:, 128 * (u + 3):128 * (u + 3) + NT], mul_buf=te, PT2=PT2)
            n_odd += h % 2
    pipe.flush()
    dump("attnAT", attnAT[:, 0, :], attnAT, [128, TQ], BF16)
    P.release(mA)

    mlaT = P.sb("mlaT", [128, 4, TQ], BF16)
    mB = P.mark()
    qT_all = P.sb("qT_all", [96, 8, TQ], BF16)

    def rope_tables(R, t0):
        posi, posf, ang, t1, ti, Ct, St = R["posi"], R["posf"], R["ang"], R["t1"], R["ti"], R["C"], R["S"]
        P.dma("sp", [lambda e: e.dma_start(out=posi[0:96, :], in_=dap(posr, t0, [[0, 96], [1, NT]]))], posi, writes=[posi])
        P.add("dve", lambda e: e.tensor_copy(out=posf[0:96, :], in_=posi[0:96, :]), reads=[posi], writes=[posf])
        P.add("dve", lambda e: e.tensor_scalar(out=ang[0:96, :], in0=posf[0:96, :], scalar1=misc[0:96, 0:1], scalar2=None, op0=ALU.mult),
              reads=[posf, misc], writes=[ang])
        for which, dst in ((0, St), (1, Ct)):
            if which == 1:
                P.add("dve", lambda e: e.tensor_scalar(out=ang[0:96, :], in0=ang[0:96, :], scalar1=math.pi / 2, scalar2=None, op0=ALU.add),
                      reads=[ang], writes=[ang])
            P.add("dve", lambda e: e.tensor_scalar(out=t1[0:96, :], in0=ang[0:96, :], scalar1=1.0 / (2 * math.pi), scalar2=None, op0=ALU.mult),
                  reads=[ang], writes=[t1])
            P.add("dve", lambda e: e.tensor_copy(out=ti[0:96, :], in_=t1[0:96, :]), reads=[t1], writes=[ti])
            P.add("dve", lambda e: e.tensor_copy(out=t1[0:96, :], in_=ti[0:96, :]), reads=[ti], writes=[t1])
            P.add("dve", lambda e: e.scalar_tensor_tensor(out=t1[0:96, :], in0=t1[0:96, :], scalar=-2 * math.pi, in1=ang[0:96, :], op0=ALU.mult, op1=ALU.add),
                  reads=[t1, ang], writes=[t1])
            P.add("dve", lambda e: e.tensor_scalar(out=t1[0:96, :], in0=t1[0:96, :], scalar1=-math.pi, scalar2=math.pi, op0=ALU.max, op1=ALU.min),
                  reads=[t1], writes=[t1])
            P.add("act", lambda e, dst=dst: e.activation(out=dst[0:96, :], in_=t1[0:96, :], func=AF.Sin), reads=[t1], writes=[dst])
        P.add("dve", lambda e: e.tensor_scalar(out=St[0:96, :], in0=St[0:96, :], scalar1=misc[0:96, 1:2], scalar2=None, op0=ALU.mult),
              reads=[St, misc], writes=[St])

    def alloc_rope():
        R = {}
        R["posi"] = P.sb("posi", [128, NT], I32)
        R["ti"] = R["posi"]
        for k in ("ang", "t1", "C", "S", "ra", "rb"):
            R[k] = P.sb(k, [128, NT], F32)
        R["posf"] = R["ra"]
        return R

    def apply_rope(R, pa, pb_, lo, hi, out_ap, out_buf):
        ra, rb = R["ra"], R["rb"]
        P.add("dve", lambda e: e.tensor_tensor(out=ra[lo:hi, :], in0=pa[lo:hi, :], in1=R["C"][lo:hi, :], op=ALU.mult), reads=[pa, R["C"]], writes=[ra])
        P.add("dve", lambda e: e.tensor_tensor(out=rb[lo:hi, :], in0=pb_[lo:hi, :], in1=R["S"][lo:hi, :], op=ALU.mult), reads=[pb_, R["S"]], writes=[rb])
        P.add("pool", lambda e: e.tensor_tensor(out=out_ap, in0=ra[lo:hi, :], in1=rb[lo:hi, :], op=ALU.add), reads=[ra, rb], writes=[out_buf])

    mB2 = P.mark()
    W = alloc_tilework(double_h=True, double_x=True)
    R = alloc_rope()
    wcq = P.sb("wcq", [128, 8, 768], BF16)
    wuq = P.sb("wuq", [128, 6, 768], BF16)
    wuqs = P.sb("wuqs", [128, 6, 768], BF16)
    cqf = P.sb("cqf", [128, 6, NT], F32)
    cqsq = P.sb("cqsq", [128, 6, NT], BF16)
    cqn = P.sb("cqn", [128, 6, NT], BF16)
    rstq = P.sb("rstq", [128, NT], F32)
    load_w(wcq, wcq[:, :, 0:384], w_in, C_CQ, [[INW, 128], [128 * INW, 8], [1, 384]])
    load_w(wcq, wcq[:, :, 384:768], w_in, C_CQ + 384, [[INW, 128], [128 * INW, 8], [1, 384]])
    load_w(wuq, wuq[:, :, :], w_uq, 0, [[768, 128], [128 * 768, 6], [1, 768]])
    load_w(wuqs, wuqs[:, :, :], w_uqs, 0, [[768, 128], [128 * 768, 6], [1, 768]])
    for i in range(4):
        xb = load_x(W, i, i * NT)
        hT = make_h(W, xb, i, gv1, sh1, gv1)
        rope_tables(R, i * NT)
        for m in range(6):
            def ev(pb_, m=m):
                P.add("act", lambda e: e.activation(out=cqf[:, m, :], in_=pb_[:, :], func=AF.Copy), reads=[pb_], writes=[cqf])
                P.add("act", lambda e: e.activation(out=cqsq[:, m, :], in_=pb_[:, :], func=AF.Square), reads=[pb_], writes=[cqsq])
            proj_fm(hT, wcq, lambda k, m=m: wcq[:, k, m * 128:(m + 1) * 128], 8, ev)
        rms_rstd(cqsq, 6, rstq, 768.0, None)
        for m in range(6):
            P.add("dve", lambda e, m=m: e.scalar_tensor_tensor(out=cqn[:, m, :], in0=cqf[:, m, :], scalar=gq[:, m:m + 1], in1=rstq[:], op0=ALU.mult, op1=ALU.mult),
                  reads=[cqf, rstq, gq], writes=[cqn])
        for h in range(8):
            pa, pb_ = pmm[0], pmm[1]
            for k in range(6):
                mm(pa[0:96, :], wuq[:, k, h * 96:(h + 1) * 96], cqn[:, k, :], k == 0, k == 5, [wuq, cqn], [pa])
            for k in range(6):
                mm(pb_[0:96, :], wuqs[:, k, h * 96:(h + 1) * 96], cqn[:, k, :], k == 0, k == 5, [wuqs, cqn], [pb_])
            apply_rope(R, pa, pb_, 0, 96, qT_all[0:96, h, i * NT:(i + 1) * NT], qT_all)
    dump("qT0", qT_all[:, 0, :], qT_all, [96, TQ], BF16)
    P.release(mB2)

    latT = P.sb("latT", [128, 2, S], BF16)
    KT = P.sb("KT", [96, S], BF16)
    mB1 = P.mark()
    W = alloc_tilework(double_h=False, double_x=True)
    R = alloc_rope()
    wkv = P.sb("wkv", [128, 8, 256], BF16)
    wkr = P.sb("wkr", [128, 8, 192], BF16)
    ckf = P.sb("ckf", [128, 2, NT], F32)
    cksq = P.sb("cksq", [128, 2, NT], BF16)
    rstk = P.sb("rstk", [128, NT], F32)
    load_w(wkv, wkv[:, :, :], w_in, C_CKV, [[INW, 128], [128 * INW, 8], [1, 256]])
    load_w(wkr, wkr[:, :, :], w_kr, 0, [[192, 128], [128 * 192, 8], [1, 192]])
    for n in range(16):
        t0 = n * NT
        xb = load_x(W, n, t0)
        hT = make_h(W, xb, n, gv1, sh1, gv1)
        rope_tables(R, t0)
        for m in range(2):
            def ev(pb_, m=m):
                P.add("act", lambda e: e.activation(out=ckf[:, m, :], in_=pb_[:, :], func=AF.Copy), reads=[pb_], writes=[ckf])
                P.add("act", lambda e: e.activation(out=cksq[:, m, :], in_=pb_[:, :], func=AF.Square), reads=[pb_], writes=[cksq])
            proj_fm(hT, wkv, lambda k, m=m: wkv[:, k, m * 128:(m + 1) * 128], 8, ev)
        rms_rstd(cksq, 2, rstk, 256.0, None)
        for m in range(2):
            P.add("dve", lambda e, m=m, t0=t0: e.scalar_tensor_tensor(out=latT[:, m, t0:t0 + NT], in0=ckf[:, m, :], scalar=gkv[:, m:m + 1], in1=rstk[:], op0=ALU.mult, op1=ALU.mult),
                  reads=[ckf, rstk, gkv], writes=[latT])
        pa, pb_ = pmm[0], pmm[1]
        for k in range(8):
            mm(pa[0:96, :], wkr[:, k, 0:96], hT[:, k, :], k == 0, k == 7, [wkr, hT], [pa])
        for k in range(8):
            mm(pb_[0:96, :], wkr[:, k, 96:192], hT[:, k, :], k == 0, k == 7, [wkr, hT], [pb_])
        apply_rope(R, pa, pb_, 64, 96, KT[64:96, t0:t0 + NT], KT)
    dump("latT", latT[:, 0, 0:TQ], latT, [128, TQ], BF16)
    P.release(mB1)

    mB3 = P.mark()
    wukv = P.sb("wukv", [128, 2, 1024], BF16)
    Vh = [P.sb("Vh0", [128, 64, 128], BF16), P.sb("Vh1", [128, 64, 128], BF16)]
    PT = [P.sb("PT0", [128, NT], BF16), P.sb("PT1", [128, NT], BF16), P.sb("PT2", [128, NT], BF16)]
    recs = P.sb("recs", [128, NT], F32)
    P.add("dve", lambda e: e.memset(recs[:], 0.0), writes=[recs])
    bcs = P.sb("bcs", [64, NT], F32)
    otmp = [P.sb("otmp0", [64, NT], BF16), P.sb("otmp1", [64, NT], BF16)]
    load_w(wukv, wukv[:, :, :], w_ukv, 0, [[1024, 128], [128 * 1024, 2], [1, 1024]])
    for vb in Vh:
        P.add("pool", lambda e, vb=vb: e.memset(vb[:, :, :], 0.0), writes=[vb])
        P.add("pool", lambda e, vb=vb: e.memset(vb[:, :, 64:65], 1.0), writes=[vb])
    n_odd = 0
    PTp = [P.sb("PTp%d" % i_, [128, 2 * NT], BF16) for i_ in range(3)]
    stages = [(hb[0], hb[1], scP[0]), (hb[2], hb[3], scP[1])]
    pend = []
    deferred = []
    npair = 0

    def tick():
        for d_ in deferred:
            d_[0] -= 1
        while deferred and deferred[0][0] <= 0:
            deferred.pop(0)[1]()

    for h in range(8):
        vb = Vh[h % 2]
        for n in range(16):
            pb_ = pmm[0]
            for k in range(2):
                mm(pb_[:, :], wukv[:, k, h * 128:h * 128 + 128], latT[:, k, n * NT:(n + 1) * NT], k == 0, k == 1, [wukv, latT], [pb_])
            evac(KT[0:64, n * NT:(n + 1) * NT], pb_[0:64, :], [pb_], [KT], eng="dve")
        for g in range(8):
            pb_ = pmm[0]
            for tt in range(8):
                kt = g * 8 + tt
                for k in range(2):
                    mm(pb_[:, tt * 64:(tt + 1) * 64], latT[:, k, kt * 128:(kt + 1) * 128], wukv[:, k, h * 128 + 64:h * 128 + 128], k == 0, k == 1,
                       [wukv, latT], [pb_])
            evac(vb[:, g * 8:(g + 1) * 8, 0:64], pb_[:, :].rearrange("p (t d) -> p t d", t=8), [pb_], [vb], eng="dve")
        if debug and h == 0:
            dump("KT0", KT[:, 0:TQ], KT, [96, TQ], BF16)
            dump("Vh0", vb[:, :, 0:65], vb, [128, 64, 65], BF16)
        for qb in range(4):
            ov = pov[cnt["ov"] % 2]
            cnt["ov"] += 1
            klist = [(0, kt) for kt in range(4 * (qb + 1))] + [(s, kt) for s in (1, 2, 3) for kt in range(16)]
            assert len(klist) % 2 == 0
            for pi in range(len(klist) // 2):
                tick()
                two = klist[2 * pi:2 * pi + 2]
                assert two[0][0] == two[1][0]
                s_ = two[0][0]
                ha, hb_, scT = stages[npair % 2]
                ptp = PTp[npair % 3]
                npair += 1
                for j_, (s, kt) in enumerate(two):
                    g = s * 16 + kt
                    diag = (s == 0 and kt >= 4 * qb)
                    sc = (ha, hb_)[j_]
                    mm(sc[:, :], KT[0:96, g * 128:(g + 1) * 128], qT_all[0:96, h, qb * NT:(qb + 1) * NT], True, not diag, [KT, qT_all], [sc])
                    if diag:
                        wv = kt - 4 * qb
                        mm(sc[:, :], ident[:, :], MT[:, 128 * (3 - wv):128 * (3 - wv) + NT], False, True, [ident, MT], [sc])
                P.add("act", lambda e, scT=scT, ptp=ptp, s_=s_: e.activation(out=ptp[:, :], in_=scT[:, 0:2 * NT], func=AF.Exp, bias=pm[:, s_:s_ + 1], scale=96.0 ** -0.5),
                      reads=[ha, hb_, pm], writes=[ptp])

                def pv(two=two, h=h, qb=qb, ov=ov, vb=vb, ptp=ptp, pi=pi, npairs=len(klist) // 2, n_odd=n_odd):
                    for j_, (s, kt) in enumerate(two):
                        g = s * 16 + kt
                        first = (pi == 0 and j_ == 0)
                        last = (pi == npairs - 1 and j_ == 1)
                        mm(ov[:, :], vb[:, g, :], ptp[:, j_ * NT:(j_ + 1) * NT], first, last, [vb, ptp], [ov])
                    if pi == npairs - 1:
                        deferred.append([5, lambda: finalize(ov, h, qb, mlaT, n_odd, recs, bcs, otmp, pmisc=pmm[1])])

                pend.append(pv)
                if len(pend) > 2:
                    pend.pop(0)()
            n_odd += h % 2
    while pend:
        pend.pop(0)()
    while deferred:
        deferred.pop(0)[1]()
    dump("mlaT", mlaT[:, 0, :], mlaT, [128, TQ], BF16)
    P.release(mB)

    top0 = P.top
    wbufs = [P.sb_top("wb0", [128, 4096], BF16), P.sb_top("wb1", [128, 4096], BF16), P.sb_top("wb2", [128, 4096], BF16)]
    mCm = P.mark()
    mergedT = P.sb("mergedT", [128, 8, TQ], BF16)
    mCh = P.mark()
    hTo = P.sb("hTo", [128, 8, TQ], BF16)
    mC0 = P.mark()
    W = alloc_tilework()
    for i in range(4):
        xb = load_x(W, i, i * NT)
        hT = make_h(W, xb, i, gv1, sh1, gv1)
        evac(hTo[:, :, i * NT:(i + 1) * NT], hT[:], [hT], [hTo], eng="pool")
    P.release(mC0)

    def next_w():
        b = wbufs[cnt["w"] % 3]
        cnt["w"] += 1
        return b

    mC1 = P.mark()
    sig = [P.sb("sig0", [128, NT], F32), P.sb("sig1", [128, NT], F32)]
    prod = [P.sb("prod0", [128, NT], F32), P.sb("prod1", [128, NT], F32)]
    for m in range(8):
        wg = next_w()
        wgv = wg[:, 0:2048].rearrange("p (k g c) -> p k g c", k=8, g=2)
        load_w(wg, wgv[:, :, 0, :], w_in, C_GA + m * 128, [[INW, 128], [128 * INW, 8], [1, 128]])
        load_w(wg, wgv[:, :, 1, :], w_in, C_GB + m * 128, [[INW, 128], [128 * INW, 8], [1, 128]])
        wu = next_w()
        wuv = wu[:, 0:1024].rearrange("p (k g c) -> p k g c", k=4, g=2)
        load_w(wu, wuv[:, :, 0, :], w_up_a, m * 128, [[D, 128], [128 * D, 4], [1, 128]])
        load_w(wu, wuv[:, :, 1, :], w_up_b, m * 128, [[D, 128], [128 * D, 4], [1, 128]])
        for i in range(4):
            tsl = slice(i * NT, (i + 1) * NT)
            for gidx, srcT in ((0, attnAT), (1, mlaT)):
                pg = next_mm()
                for k in range(8):
                    mm(pg[:, :], wgv[:, k, gidx, :], hTo[:, k, tsl], k == 0, k == 7, [wg, hTo], [pg])
                sg = sig[gidx]
                P.add("act", lambda e, sg=sg, pg=pg: e.activation(out=sg[:], in_=pg[:], func=AF.Sigmoid), reads=[pg], writes=[sg])
                py = next_mm()
                for k in range(4):
                    mm(py[:, :], wuv[:, k, gidx, :], srcT[:, k, tsl], k == 0, k == 3, [wu, srcT], [py])
                pr = prod[gidx]
                P.add("dve", lambda e, pr=pr, py=py, sg=sg: e.tensor_tensor(out=pr[:], in0=py[:], in1=sg[:], op=ALU.mult), reads=[py, sg], writes=[pr])
            P.add("pool", lambda e, m=m, tsl=tsl: e.tensor_tensor(out=mergedT[:, m, tsl], in0=prod[0][:], in1=prod[1][:], op=ALU.add),
                  reads=[prod[0], prod[1]], writes=[mergedT])
    dump("mergedT", mergedT[:, 0, :], mergedT, [128, TQ], BF16)
    P.release(mCh)

    x1T = P.sb_top("x1T", [128, 8, TQ], F32)
    mC2 = P.mark()
    xres = [P.sb("xres0", [128, NT], F32), P.sb("xres1", [128, NT], F32)]
    nx = 0
    for m in range(8):
        wo = next_w()
        wov = wo[:, 0:1024].rearrange("p (k c) -> p k c", k=8)
        load_w(wo, wov, w_o, m * 128, [[D, 128], [128 * D, 8], [1, 128]])
        for i in range(4):
            tsl = slice(i * NT, (i + 1) * NT)
            xr = xres[nx % 2]
            nx += 1
            P.dma("sp", [lambda e, xr=xr, m=m, i=i: e.dma_start(out=xr[:], in_=dap(xT, m * 128 * S + i * NT, [[S, 128], [1, NT]]))], xr, writes=[xr])
            po = next_mm()
            for k in range(8):
                mm(po[:, :], wov[:, k, :], mergedT[:, k, tsl], k == 0, k == 7, [wo, mergedT], [po])
            P.add("dve", lambda e, po=po, xr=xr, m=m, tsl=tsl: e.scalar_tensor_tensor(out=x1T[:, m, tsl], in0=po[:], scalar=gt1[:, m:m + 1], in1=xr[:], op0=ALU.mult, op1=ALU.add),
                  reads=[po, xr, modT], writes=[x1T])
    dump("x1T", x1T[:, 0, :], x1T, [128, TQ], F32)
    P.release(mA)

    h2T = P.sb_top("h2T", [128, 8, TQ], BF16)
    mC3 = P.mark()
    sqb = P.sb("sq", [128, 8, NT], BF16)
    rstd = P.sb("rstd", [128, NT], F32)
    tmpc = [P.sb("tmpc0", [128, NT], F32), P.sb("tmpc1", [128, NT], F32)]
    for i in range(4):
        tsl = slice(i * NT, (i + 1) * NT)
        P.add("act", lambda e, tsl=tsl, sqb=sqb: e.activation(out=sqb[:], in_=x1T[:, :, tsl], func=AF.Square), reads=[x1T], writes=[sqb])
        rms_rstd(sqb, 8, rstd, float(D), None)
        for c in range(8):
            tc_ = tmpc[c % 2]
            P.add("dve", lambda e, c=c, tc_=tc_, tsl=tsl, rstd=rstd: e.scalar_tensor_tensor(out=tc_[:], in0=x1T[:, c, tsl], scalar=gv2[:, c:c + 1], in1=rstd[:], op0=ALU.mult, op1=ALU.mult),
                  reads=[x1T, rstd, gv2], writes=[tc_])
            P.add("act", lambda e, c=c, tc_=tc_, tsl=tsl: e.activation(out=h2T[:, c, tsl], in_=tc_[:], func=AF.Identity, bias=sh2[:, c:c + 1], scale=1.0),
                  reads=[tc_, modT], writes=[h2T])
    P.release(mC3)

    mC4 = P.mark()
    actT = P.sb("actT", [128, 22, 1024], BF16)
    sil = [P.sb("sil0", [128, NT], F32), P.sb("sil1", [128, NT], F32)]
    for half in range(2):
        for f in range(22):
            wgu = next_w()
            wguv = wgu[:, 0:2048].rearrange("p (k g c) -> p k g c", k=8, g=2)
            load_w(wgu, wguv[:, :, 0, :], w_gate, f * 128, [[DFF, 128], [128 * DFF, 8], [1, 128]])
            load_w(wgu, wguv[:, :, 1, :], w_up, f * 128, [[DFF, 128], [128 * DFF, 8], [1, 128]])
            for i in range(2):
                tsl = slice(half * 1024 + i * NT, half * 1024 + (i + 1) * NT)
                pg = next_mm()
                for k in range(8):
                    mm(pg[:, :], wguv[:, k, 0, :], h2T[:, k, tsl], k == 0, k == 7, [wgu, h2T], [pg])
                sl_ = sil[i]
                P.add("act", lambda e, sl_=sl_, pg=pg: e.activation(out=sl_[:], in_=pg[:], func=AF.Silu), reads=[pg], writes=[sl_])
                pu = next_mm()
                for k in range(8):
                    mm(pu[:, :], wguv[:, k, 1, :], h2T[:, k, tsl], k == 0, k == 7, [wgu, h2T], [pu])
                P.add("dve", lambda e, sl_=sl_, pu=pu, f=f, i=i: e.tensor_tensor(out=actT[:, f, i * NT:(i + 1) * NT], in0=pu[:], in1=sl_[:], op=ALU.mult),
                      reads=[pu, sl_], writes=[actT])
        for m in range(8):
            wd = next_w()
            wdv = wd[:, 0:2816].rearrange("p (k c) -> p k c", k=22)
            load_w(wd, wdv, w_down, m * 128, [[D, 128], [128 * D, 22], [1, 128]])
            for i in range(2):
                tsl = slice(half * 1024 + i * NT, half * 1024 + (i + 1) * NT)
                pd = next_mm()
                for k in range(22):
                    mm(pd[:, :], wdv[:, k, :], actT[:, k, i * NT:(i + 1) * NT], k == 0, k == 21, [wd, actT], [pd])
                P.add("dve", lambda e, pd=pd, m=m, tsl=tsl: e.scalar_tensor_tensor(out=x1T[:, m, tsl], in0=pd[:], scalar=gt2[:, m:m + 1], in1=x1T[:, m, tsl], op0=ALU.mult, op1=ALU.add),
                      reads=[pd, x1T, modT], writes=[x1T])

    P.release(mC4)
    P.top += 32768
    sqb = P.sb("sq", [128, 8, NT], BF16)
    rstd = P.sb("rstd", [128, NT], F32)
    ob = [P.sb("ob0", [128, 8, NT], F32), P.sb("ob1", [128, 8, NT], F32)]
    for i in range(4):
        tsl = slice(i * NT, (i + 1) * NT)
        o_ = ob[i % 2]
        P.add("act", lambda e, tsl=tsl, sqb=sqb: e.activation(out=sqb[:], in_=x1T[:, :, tsl], func=AF.Square), reads=[x1T], writes=[sqb])
        rms_rstd(sqb, 8, rstd, float(D), None)
        for c in range(8):
            P.add("dve", lambda e, c=c, o_=o_, tsl=tsl, rstd=rstd: e.scalar_tensor_tensor(out=o_[:, c, :], in0=x1T[:, c, tsl], scalar=gfin[:, c:c + 1], in1=rstd[:], op0=ALU.mult, op1=ALU.mult),
                  reads=[x1T, rstd, gfin], writes=[o_])
        P.dma("sp", [lambda e, o_=o_, i=i: e.dma_start(out=dap(outT, i * NT, [[TQ, 128], [128 * TQ, 8], [1, NT]]), in_=o_[:])], o_, reads=[o_])
    stats = P.emit(final_wait_bufs=ob + dumped)
    return nc, stats, dbg


def _t5_bucket_np(dist):
    max_exact = 16
    d = np.maximum(dist, 1).astype(np.float32)
    log_b = max_exact + (np.log(d / np.float32(max_exact)) / np.float32(math.log(2048 / max_exact)) * np.float32(32 - max_exact)).astype(np.int32)
    log_b = np.minimum(log_b, 31)
    return np.where(dist < max_exact, dist, log_b)


def _constants():
    OH = np.zeros((128, ROWW), np.float32)
    e = np.arange(ROWW)
    delta = e - 511
    mult = np.zeros(ROWW, np.int64)
    for (w, d) in ((128, 1), (512, 4), (2048, 16)):
        mult += ((delta >= 0) & (delta <= w) & (delta % d == 0)).astype(np.int64)
    valid = mult > 0
    bucket = _t5_bucket_np(np.clip(delta, 0, 2048).astype(np.int32))
    OH[bucket[valid], e[valid]] = 1.0
    OH[32, :] = NEG
    OH[32, valid] = np.log(mult[valid].astype(np.float64)).astype(np.float32)
    ident = np.eye(128, dtype=np.float32)
    anti = np.ascontiguousarray(ident[::-1])
    kk = np.arange(128)[:, None]
    cc = np.arange(896)[None, :]
    MT = np.where(cc - 384 - kk >= 0, 0.0, NEG).astype(np.float32)
    misc = np.zeros((128, 4), np.float32)
    half = 16
    freqs = (10000.0 ** (-np.arange(half, dtype=np.float32) / np.float32(half))).astype(np.float32)
    for r in range(32):
        misc[64 + r, 0] = freqs[r % 16]
        misc[64 + r, 1] = -1.0 if r < 16 else 1.0
    misc[:, 2] = EPS
    return OH, ident, anti, MT, misc


def _colT(v, n):
    return np.ascontiguousarray(v.reshape(n, 128).T)


def make_in_maps(x, c, positions, rel_bias, w_ada, b_ada, g_mix, w_in, g_q_lora, w_uq, g_kv_lora, w_ukv,
                 w_up_a, w_up_b, w_o, g_ffn, w_gate, w_up, w_down, g_final):
    OH, ident, anti, MT, misc = _constants()
    w_in0 = np.ascontiguousarray(w_in[0])
    kr = w_in0[:, C_KR:C_KR + 32]
    w_kr = np.zeros((D, 2, 96), np.float32)
    w_kr[:, 0, 64:96] = kr
    w_kr[:, 1, 64:96] = np.concatenate([kr[:, 16:32], kr[:, 0:16]], axis=1)
    wuq = w_uq[0]
    wuqs = wuq.copy()
    wuqs[:, :, 64:80] = wuq[:, :, 80:96]
    wuqs[:, :, 80:96] = wuq[:, :, 64:80]
    rbT = np.zeros((128, 128), np.float32)
    rbT[0:32, 0:8] = rel_bias.T
    rbT[32, 0:8] = 1.0
    shared = {
        "rbT": rbT, "OH": OH, "w_ada": np.ascontiguousarray(w_ada[0]), "b_adaT": _colT(b_ada[0], 48),
        "g_mixT": _colT(g_mix[0], 8), "w_in": w_in0, "w_kr": w_kr, "g_qT": _colT(g_q_lora[0], 6),
        "w_uq": np.ascontiguousarray(wuq.reshape(768, 768)), "w_uqs": np.ascontiguousarray(wuqs.reshape(768, 768)),
        "g_kvT": _colT(g_kv_lora[0], 2), "w_ukv": np.ascontiguousarray(w_ukv[0].reshape(256, 1024)),
        "w_up_a": np.ascontiguousarray(w_up_a[0]), "w_up_b": np.ascontiguousarray(w_up_b[0]), "w_o": np.ascontiguousarray(w_o[0]),
        "g_ffnT": _colT(g_ffn[0], 8), "w_gate": np.ascontiguousarray(w_gate[0]), "w_up": np.ascontiguousarray(w_up[0]),
        "w_down": np.ascontiguousarray(w_down[0]), "g_finT": _colT(g_final, 8),
        "ident": ident, "anti": anti, "MT": MT, "misc": misc,
    }
    in_maps = []
    for core in range(8):
        b, j = core // 4, core % 4
        order = [(j - s) % 4 for s in range(4)]
        xb = x[b]
        xs = np.concatenate([xb[ch * TQ:(ch + 1) * TQ] for ch in order], axis=0)
        pos = np.concatenate([positions[b, ch * TQ:(ch + 1) * TQ] for ch in order]).astype(np.int32).reshape(1, S)
        pmv = np.zeros((128, 4), np.float32)
        for s in range(1, 4):
            if j - s < 0:
                pmv[:, s] = NEG
        m = dict(shared)
        m["xT"] = np.ascontiguousarray(xs.T)
        m["posr"] = pos
        m["cT"] = _colT(c[b], 8)
        m["pm"] = pmv
        in_maps.append(m)
    return in_maps


_CACHE = {}


def kernel(**inputs):
    inputs = {k: np.asarray(v) for k, v in inputs.items()}
    if "nc" not in _CACHE:
        _CACHE["nc"] = build(False)[0]
    nc = _CACHE["nc"]
    in_maps = make_in_maps(**inputs)
    res = run_bass_kernel_spmd(nc, in_maps, core_ids=list(range(8)))
    out = np.empty((2, S, D), np.float32)
    for core in range(8):
        b, j = core // 4, core % 4
        out[b, j * TQ:(j + 1) * TQ, :] = res.results[core]["outT"].T
    return out
```

```python
import math
import numpy as np
import concourse.bass as bass
import concourse.mybir as mybir
from concourse.bass_utils import run_bass_kernel_spmd

F32 = mybir.dt.float32
BF16 = mybir.dt.bfloat16
I32 = mybir.dt.int32
ALU = mybir.AluOpType
AF = mybir.ActivationFunctionType

ENGS = ("pe", "act", "dve", "pool", "sp")
NEG = -30000.0
MAXW = 1
S = 8192
D = 1024
TQ = 2048
NT = 512
DFF = 2816
EPS = 1e-6
C_AQ, C_AK, C_AV, C_CQ, C_CKV, C_KR, C_GA, C_GB = 0, 512, 1024, 1536, 2304, 2560, 2592, 3616
INW = 4640
TW = 2944
ROWW = 3072


class Buf:
    __slots__ = ("name", "t", "lastw", "readers", "dsem", "dcount")

    def __init__(self, name, t=None):
        self.name = name
        self.t = t
        self.lastw = None
        self.readers = {}
        self.dsem = None
        self.dcount = 0

    def __getitem__(self, idx):
        return self.t[idx]


class Op:
    __slots__ = ("eng", "fn", "deps", "signal", "ms", "is_dma", "sem", "val")

    def __init__(self, eng, fn):
        self.eng = eng
        self.fn = fn
        self.deps = []
        self.signal = False
        self.ms = 0
        self.is_dma = False
        self.sem = None
        self.val = 0


class Prog:
    def __init__(self, nc, base=18432, top=229376):
        self.nc = nc
        self.q = {e: [] for e in ENGS}
        self.esem = {}
        self.dma_bufs = []
        self.bar = []
        self.ptr = base
        self.top = top
        self.nalloc = 0

    def sb(self, name, shape, dtype):
        per = int(np.prod(shape[1:])) * mybir.dt.size(dtype)
        per = (per + 63) // 64 * 64
        assert self.ptr + per <= self.top, f"SBUF arena overflow at {name}: {self.ptr}+{per} > {self.top}"
        self.nalloc += 1
        t = self.nc.alloc_sbuf_tensor_at(f"{name}_{self.nalloc}", list(shape), dtype, offset=self.ptr)
        self.ptr += per
        return Buf(name, t)

    def sb_top(self, name, shape, dtype):
        per = int(np.prod(shape[1:])) * mybir.dt.size(dtype)
        per = (per + 63) // 64 * 64
        assert self.top - per >= self.ptr, f"SBUF arena overflow (top) at {name}"
        self.nalloc += 1
        self.top -= per
        t = self.nc.alloc_sbuf_tensor_at(f"{name}_{self.nalloc}", list(shape), dtype, offset=self.top)
        return Buf(name, t)

    def mark(self):
        return self.ptr

    def release(self, m):
        self.ptr = m
        self.barrier()

    def ps(self, name):
        return Buf(name, self.nc.alloc_psum_tensor(name, [128, 512], F32))

    def barrier(self):
        deps = []
        for e in ENGS:
            if self.q[e]:
                deps.append(self.q[e][-1])
        for b in self.dma_bufs:
            if b.lastw is not None and b.lastw.is_dma:
                deps.append(b.lastw)
            for r in b.readers.values():
                if r.is_dma:
                    deps.append(r)
        self.bar = deps

    def _track(self, op, reads, writes):
        deps = {}
        for d in self.bar:
            deps[id(d)] = d
        for b in reads:
            d = b.lastw
            if d is not None:
                deps[id(d)] = d
        for b in writes:
            d = b.lastw
            if d is not None:
                deps[id(d)] = d
            for r in b.readers.values():
                deps[id(r)] = r
        deps.pop(id(op), None)
        op.deps = list(deps.values())
        for b in reads:
            b.readers[op.eng] = op
        for b in writes:
            b.lastw = op
            b.readers = {}

    def add(self, eng, fn, reads=(), writes=()):
        op = Op(eng, fn)
        self._track(op, reads, writes)
        self.q[eng].append(op)
        return op

    def dma(self, queue, fns, sem_buf, reads=(), writes=()):
        op = Op(queue, fns)
        op.is_dma = True
        if sem_buf.dsem is None:
            sem_buf.dsem = self.nc.alloc_semaphore("d%d_%s" % (len(self.dma_bufs), sem_buf.name))
            self.dma_bufs.append(sem_buf)
        sem_buf.dcount += 16 * len(fns)
        op.sem = sem_buf.dsem
        op.val = sem_buf.dcount
        self._track(op, reads, writes)
        self.q[queue].append(op)
        return op

    def emit(self, final_wait_bufs=()):
        nc = self.nc
        for e in ENGS:
            self.esem[e] = nc.alloc_semaphore("e_" + e)
        for e in ENGS:
            for op in self.q[e]:
                for d in op.deps:
                    if d.is_dma:
                        continue
                    if d.eng == "pe" and e == "pe" and not op.is_dma:
                        continue
                    d.signal = True
        for e in ENGS:
            c = 0
            for op in self.q[e]:
                if op.signal and not op.is_dma:
                    c += 1
                    op.ms = c
        stats = {}

        def run(e, eng):
            seen = {}
            nw = 0
            for op in self.q[e]:
                waits = []
                for d in op.deps:
                    if d.is_dma:
                        s, v = d.sem, d.val
                    else:
                        if d.eng == "pe" and e == "pe" and not op.is_dma:
                            continue
                        s, v = self.esem[d.eng], d.ms
                    k = id(s)
                    if seen.get(k, 0) >= v:
                        continue
                    seen[k] = v
                    waits.append((s, v))
                    nw += 1
                emb = waits[-MAXW:] if MAXW > 0 else []
                for (s, v) in waits[:len(waits) - len(emb)]:
                    eng.wait_ge(s, v)
                if op.is_dma:
                    first = True
                    for f in op.fn:
                        ins = f(eng)
                        if first:
                            for (s, v) in emb:
                                ins._wait_ge(s, v)
                            first = False
                        ins.then_inc(op.sem, 16)
                else:
                    ins = op.fn(eng)
                    for (s, v) in emb:
                        ins._wait_ge(s, v)
                    if op.signal:
                        ins.then_inc(self.esem[e], 1)
            if e == "sp":
                for b in final_wait_bufs:
                    eng.wait_ge(b.dsem, b.dcount)
            stats[e] = (len(self.q[e]), nw)

        with nc.Block() as block:
            @block.tensor
            def _(eng):
                run("pe", eng)

            @block.scalar
            def _(eng):
                run("act", eng)

            @block.vector
            def _(eng):
                run("dve", eng)

            @block.gpsimd
            def _(eng):
                run("pool", eng)

            @block.sync
            def _(eng):
                run("sp", eng)
        return stats


def dap(t, off, dims):
    return bass.AP(t, off, [list(d) for d in dims])


def build(debug=False):
    nc = bass.Bass("TRN2", target_bir_lowering=False)
    P = Prog(nc)

    def din(name, shape, dt=F32):
        return nc.dram_tensor(name, list(shape), dt, kind="ExternalInput")

    xT = din("xT", [D, S])
    posr = din("posr", [1, S], I32)
    cT = din("cT", [128, 8])
    pm_d = din("pm", [128, 4])
    rbT_d = din("rbT", [128, 128])
    OH_d = din("OH", [128, ROWW])
    w_ada = din("w_ada", [D, 6 * D])
    b_adaT = din("b_adaT", [128, 48])
    g_mixT = din("g_mixT", [128, 8])
    w_in = din("w_in", [D, INW])
    w_kr = din("w_kr", [D, 2, 96])
    g_qT = din("g_qT", [128, 6])
    w_uq = din("w_uq", [768, 8 * 96])
    w_uqs = din("w_uqs", [768, 8 * 96])
    g_kvT = din("g_kvT", [128, 2])
    w_ukv = din("w_ukv", [256, 8 * 128])
    w_up_a = din("w_up_a", [512, D])
    w_up_b = din("w_up_b", [512, D])
    w_o = din("w_o", [D, D])
    g_ffnT = din("g_ffnT", [128, 8])
    w_gate = din("w_gate", [D, DFF])
    w_up = din("w_up", [D, DFF])
    w_down = din("w_down", [DFF, D])
    g_finT = din("g_finT", [128, 8])
    ident_d = din("ident", [128, 128])
    anti_d = din("anti", [128, 128])
    MT_d = din("MT", [128, 896])
    misc_d = din("misc", [128, 4])
    outT = nc.dram_tensor("outT", [D, TQ], F32, kind="ExternalOutput")
    vA_d = nc.dram_tensor("vA_scr", [8, ROWW], F32)
    dbg = {}

    def dbg_out(name, shape, dt=F32):
        dbg[name] = nc.dram_tensor("dbg_" + name, list(shape), dt, kind="ExternalOutput")
        return dbg[name]

    pmm = [P.ps("pmm0"), P.ps("pmm1")]
    pss = P.ps("pss")
    psc = [P.ps("psc0"), P.ps("psc1")]
    pov = [P.ps("pov0"), P.ps("pov1")]
    pmisc = P.ps("pmisc")
    cnt = {"mm": 0, "sc": 0, "ov": 0, "ev": 0, "w": 0}

    ones_bf = P.sb("ones_bf", [128, 128], BF16)
    ones_f = P.sb("ones_f", [128, 128], F32)
    ident = P.sb("ident", [128, 128], BF16)
    anti = P.sb("anti", [128, 128], BF16)
    MT = P.sb("MT", [128, 896], BF16)
    misc = P.sb("misc", [128, 4], F32)
    pm = P.sb("pm", [128, 4], F32)
    modT = P.sb("modT", [128, 48], F32)
    gv1 = P.sb("gv1", [128, 8], F32)
    gv2 = P.sb("gv2", [128, 8], F32)
    gfin = P.sb("gfin", [128, 8], F32)
    gq = P.sb("gq", [128, 6], F32)
    gkv = P.sb("gkv", [128, 2], F32)
    attnAT = P.sb("attnAT", [128, 4, TQ], BF16)
    dump_stage = P.sb("dump_stage", [128, 512], F32) if debug else None

    P.add("dve", lambda e: e.memset(ones_bf[:], 1.0), writes=[ones_bf])
    P.add("dve", lambda e: e.memset(ones_f[:], 0.0), writes=[ones_f])
    P.add("dve", lambda e: e.memset(ones_f[64:65, :], 1.0), writes=[ones_f])
    P.dma("pool", [lambda e: e.dma_start(out=ident[:], in_=ident_d[:]),
                   lambda e: e.dma_start(out=anti[:], in_=anti_d[:]),
                   lambda e: e.dma_start(out=MT[:], in_=MT_d[:])], ident, writes=[ident, anti, MT])
    P.dma("sp", [lambda e: e.dma_start(out=misc[:], in_=misc_d[:]),
                 lambda e: e.dma_start(out=pm[:], in_=pm_d[:]),
                 lambda e: e.dma_start(out=gfin[:], in_=g_finT[:]),
                 lambda e: e.dma_start(out=gq[:], in_=g_qT[:]),
                 lambda e: e.dma_start(out=gkv[:], in_=g_kvT[:])], misc, writes=[misc, pm, gfin, gq, gkv])
    eps_ap = misc[:, 2:3]
    zero_ap = misc[:, 3:4]

    def evac(out_ap, in_ap, reads, writes, eng=None, scale=None):
        if eng is None:
            eng = "act" if cnt["ev"] % 2 == 0 else "dve"
            cnt["ev"] += 1
        if eng == "act":
            if scale is None:
                P.add("act", lambda e: e.activation(out=out_ap, in_=in_ap, func=AF.Copy), reads=reads, writes=writes)
            else:
                P.add("act", lambda e: e.activation(out=out_ap, in_=in_ap, func=AF.Copy, scale=scale), reads=reads, writes=writes)
        else:
            if scale is None:
                P.add(eng, lambda e: e.tensor_copy(out=out_ap, in_=in_ap), reads=reads, writes=writes)
            else:
                P.add(eng, lambda e: e.tensor_scalar(out=out_ap, in0=in_ap, scalar1=scale, scalar2=None, op0=ALU.mult), reads=reads, writes=writes)

    def mm(out_ap, lhsT, rhs, start, stop, reads, writes):
        P.add("pe", lambda e: e.matmul(out_ap, lhsT=lhsT, rhs=rhs, start=start, stop=stop), reads=reads, writes=writes)

    mmpool = {"banks": pmm}

    def next_mm():
        bl = mmpool["banks"]
        b = bl[cnt["mm"] % len(bl)]
        cnt["mm"] += 1
        return b

    def load_w(buf, out_ap, dram, off, dims):
        P.dma("pool", [lambda e: e.dma_start(out=out_ap, in_=dap(dram, off, dims))], buf, writes=[buf])

    def dump(name, src_ap, src_buf, shape, dt):
        if not debug:
            return
        o = dbg_out(name, shape, dt)
        P.dma("sp", [lambda e: e.dma_start(out=o[:], in_=src_ap)], src_buf, reads=[src_buf])
        dumped.append(src_buf)

    dumped = []

    tmp48 = P.sb("tmp48", [128, 48], F32)
    gtmp = P.sb("gtmp", [128, 16], F32)
    condT = P.sb("condT", [128, 8], F32)
    condB = P.sb("condB", [128, 8], BF16)
    m0 = P.mark()
    P.dma("sp", [lambda e: e.dma_start(out=condT[:], in_=cT[:]),
                 lambda e: e.dma_start(out=tmp48[:], in_=b_adaT[:]),
                 lambda e: e.dma_start(out=gtmp[:, 0:8], in_=g_mixT[:]),
                 lambda e: e.dma_start(out=gtmp[:, 8:16], in_=g_ffnT[:])], condT, writes=[condT, tmp48, gtmp])
    P.add("act", lambda e: e.activation(out=condB[:], in_=condT[:], func=AF.Silu), reads=[condT], writes=[condB])

    def ada_slab(sl, wb):
        load_w(wb, wb[:], w_ada, sl * 512, [[6 * D, 128], [128 * 6 * D, 8], [1, 512]])
        for cc in range(4):
            ci = sl * 4 + cc
            for kc in range(8):
                mm(pmisc[:, ci:ci + 1], wb[:, kc, cc * 128:(cc + 1) * 128], condB[:, kc:kc + 1], kc == 0, kc == 7,
                   [wb, condB], [pmisc])

    wst0 = [P.sb("wst0", [128, 8, 512], BF16), P.sb("wst1", [128, 8, 512], BF16)]
    for sl in range(4):
        ada_slab(sl, wst0[sl % 2])
    P.add("dve", lambda e: e.tensor_tensor(out=modT[:, 0:16], in0=pmisc[:, 0:16], in1=tmp48[:, 0:16], op=ALU.add), reads=[pmisc, tmp48], writes=[modT])
    P.add("dve", lambda e: e.scalar_tensor_tensor(out=gv1[:], in0=modT[:, 8:16], scalar=1.0, in1=gtmp[:, 0:8], op0=ALU.add, op1=ALU.mult),
          reads=[modT, gtmp], writes=[gv1])
    sh1, gt1, sh2, gt2 = modT[:, 0:8], modT[:, 16:24], modT[:, 24:32], modT[:, 40:48]

    rbT = P.sb("rbT", [128, 128], F32)
    OH = P.sb("OH", [128, ROWW], F32)
    vA_sb = P.sb("vA_sb", [8, ROWW], F32)
    P.dma("sp", [lambda e: e.dma_start(out=rbT[:], in_=rbT_d[:]), lambda e: e.dma_start(out=OH[:], in_=OH_d[:])], rbT, writes=[rbT, OH])
    for ch in range(6):
        pb_ = next_mm()
        mm(pb_[:, :], rbT[:, :], OH[:, ch * 512:(ch + 1) * 512], True, True, [rbT, OH], [pb_])
        evac(vA_sb[:, ch * 512:(ch + 1) * 512], pb_[0:8, :], [pb_], [vA_sb], eng="dve")
    vAd = Buf("vAd")
    P.dma("sp", [lambda e: e.dma_start(out=vA_d[:], in_=vA_sb[:])], vA_sb, reads=[vA_sb], writes=[vAd])
    P.release(m0)

    def alloc_tilework(double_h=True, double_x=True):
        w = {}
        x0 = P.sb("xbuf0", [128, 8, NT], F32)
        w["xbuf"] = [x0, P.sb("xbuf1", [128, 8, NT], F32) if double_x else x0]
        h0 = P.sb("hT0", [128, 8, NT], BF16)
        w["hT"] = [h0, P.sb("hT1", [128, 8, NT], BF16) if double_h else h0]
        w["sq"] = P.sb("sq", [128, 8, NT], BF16)
        w["rstd"] = P.sb("rstd", [128, NT], F32)
        w["tmpc"] = [P.sb("tmpc0", [128, NT], F32), P.sb("tmpc1", [128, NT], F32)]
        return w

    def load_x(w, i, t0):
        xb = w["xbuf"][i % 2]
        P.dma("sp", [lambda e: e.dma_start(out=xb[:], in_=dap(xT, t0, [[S, 128], [128 * S, 8], [1, NT]]))], xb, writes=[xb])
        return xb

    def rms_rstd(sqb, nch, rstd, nfeat, src_reads):
        for c in range(nch):
            mm(pss[:, :], ones_bf[:, :], sqb[:, c, :], c == 0, c == nch - 1, [ones_bf, sqb], [pss])
        P.add("act", lambda e: e.activation(out=rstd[:], in_=pss[:], func=AF.Ln, bias=eps_ap, scale=1.0 / nfeat), reads=[pss, misc], writes=[rstd])
        P.add("act", lambda e: e.activation(out=rstd[:], in_=rstd[:], func=AF.Exp, scale=-0.5), reads=[rstd], writes=[rstd])

    def make_h(w, xb, i, gv, sh, gvbuf):
        hT = w["hT"][i % 2]
        sqb, rstd = w["sq"], w["rstd"]
        P.add("act", lambda e: e.activation(out=sqb[:], in_=xb[:], func=AF.Square), reads=[xb], writes=[sqb])
        rms_rstd(sqb, 8, rstd, float(D), None)
        for c in range(8):
            tc_ = w["tmpc"][c % 2]
            P.add("dve", lambda e, c=c, tc_=tc_: e.scalar_tensor_tensor(out=tc_[:], in0=xb[:, c, :], scalar=gv[:, c:c + 1], in1=rstd[:], op0=ALU.mult, op1=ALU.mult),
                  reads=[xb, rstd, gvbuf], writes=[tc_])
            P.add("act", lambda e, c=c, tc_=tc_: e.activation(out=hT[:, c, :], in_=tc_[:], func=AF.Identity, bias=sh[:, c:c + 1], scale=1.0),
                  reads=[tc_, modT], writes=[hT])
        return hT

    def proj_fm(hT, wbuf, wap_fn, nk, out_fn, M=128, rhs_fn=None, ev_eng=None, scale=None, extra_reads=()):
        pb_ = next_mm()
        for k in range(nk):
            rhs = hT[:, k, :] if rhs_fn is None else rhs_fn(k)
            mm(pb_[0:M, :], wap_fn(k), rhs, k == 0, k == nk - 1, [hT, wbuf] + list(extra_reads), [pb_])
        out_fn(pb_)

    mA = P.mark()
    KT_A = P.sb("KT_A", [128, 4, 4096], BF16)
    V_A = P.sb("V_A", [128, 32, 8, 65], BF16)
    QT_A = P.sb("QT_A", [128, 4, TQ], BF16)
    QT_A1 = P.sb("QT_A1", [128, 4, TQ], BF16)
    P.add("pool", lambda e: e.memset(QT_A[64:128, :, :], 0.0), writes=[QT_A])
    P.add("pool", lambda e: e.memset(QT_A1[0:64, :, :], 0.0), writes=[QT_A1])
    mA1 = P.mark()
    wA = P.sb("wA", [128, 8, 1536], BF16)
    W = alloc_tilework(double_h=True, double_x=False)
    wst = [P.sb("wst0", [128, 8, 512], BF16), P.sb("wst1", [128, 8, 512], BF16)]
    for j3 in range(3):
        load_w(wA, wA[:, :, j3 * 512:(j3 + 1) * 512], w_in, j3 * 512, [[INW, 128], [128 * INW, 8], [1, 512]])
    P.add("pool", lambda e: e.memset(V_A[:, :, :, 64:65], 1.0), writes=[V_A])
    tiles = [(1, i) for i in range(4)] + [(0, i) for i in range(4)]
    mmpool["banks"] = pmm + psc + pov
    for n, (s, i) in enumerate(tiles):
        xb = load_x(W, n, s * TQ + i * NT)
        hT = make_h(W, xb, n, gv1, sh1, gv1)
        if debug and n == 4:
            dump("hT0", hT[:, 0, :], hT, [128, 512], BF16)
        kp0 = (1 - s) * TQ + i * NT
        for hp in range(4):
            proj_fm(hT, wA, lambda k, hp=hp: wA[:, k, C_AK + hp * 128:C_AK + (hp + 1) * 128], 8,
                    lambda pb_, hp=hp: evac(KT_A[:, hp, kp0:kp0 + NT], pb_[:, :], [pb_], [KT_A]))
        for sub in range(4):
            pb_ = next_mm()
            for k in range(8):
                mm(pb_[:, :], hT[:, k, sub * 128:(sub + 1) * 128], wA[:, k, C_AV:C_AV + 512], k == 0, k == 7, [hT, wA], [pb_])
            kt = kp0 // 128 + sub
            evac(V_A[:, kt, :, 0:64], pb_[:, :].rearrange("p (h d) -> p h d", h=8), [pb_], [V_A])
        ada_slab(4 + n, wst[n % 2])
        if s == 0:
            for hp in range(4):
                proj_fm(hT, wA, lambda k, hp=hp: wA[:, k, C_AQ + hp * 128:C_AQ + (hp + 1) * 128], 8,
                        lambda pb_, hp=hp: (evac(QT_A[0:64, hp, i * NT:(i + 1) * NT], pb_[0:64, :], [pb_], [QT_A], scale=0.125),
                                            evac(QT_A1[64:128, hp, i * NT:(i + 1) * NT], pb_[64:128, :], [pb_], [QT_A1], scale=0.125)))
    P.add("dve", lambda e: e.tensor_tensor(out=modT[:, 16:48], in0=pmisc[:, 16:48], in1=tmp48[:, 16:48], op=ALU.add), reads=[pmisc, tmp48], writes=[modT])
    P.add("dve", lambda e: e.scalar_tensor_tensor(out=gv2[:], in0=modT[:, 32:40], scalar=1.0, in1=gtmp[:, 8:16], op0=ALU.add, op1=ALU.mult),
          reads=[modT, gtmp], writes=[gv2])
    dump("modT", modT[:], modT, [128, 48], F32)
    dump("KT_A", KT_A[:, 0, :], KT_A, [128, 4096], BF16)
    dump("QT_A", QT_A[:, 0, :], QT_A, [128, TQ], BF16)
    dump("V_A", V_A[:, :, 0, :], V_A, [128, 32, 65], BF16)

    mmpool["banks"] = pmm
    P.release(mA1)
    TB = [P.sb("TB0", [128, TW], BF16), P.sb("TB1", [128, TW], BF16)]
    PT = [P.sb("PT0", [128, NT], BF16), P.sb("PT1", [128, NT], BF16), P.sb("PT2", [128, NT], BF16)]
    recs = P.sb("recs", [128, NT], F32)
    P.add("dve", lambda e: e.memset(recs[:], 0.0), writes=[recs])
    bcs = P.sb("bcs", [64, NT], F32)
    otmp = [P.sb("otmp0", [64, NT], BF16), P.sb("otmp1", [64, NT], BF16)]

    def finalize(ov, h, qb, dstT, n_odd, recs, bcs, otmp):
        hp, odd = h // 2, h % 2
        P.add("dve", lambda e: e.tensor_copy(out=recs[0:65, :], in_=ov[0:65, :]), reads=[ov], writes=[recs])
        mm(pmisc[:, :], ones_f[:, :], recs[:, :], True, True, [ones_f, recs], [pmisc])
        P.add("act", lambda e: e.activation(out=bcs[:], in_=pmisc[0:64, :], func=AF.Ln), reads=[pmisc], writes=[bcs])
        P.add("act", lambda e: e.activation(out=bcs[:], in_=bcs[:], func=AF.Exp, scale=-1.0), reads=[bcs], writes=[bcs])
        if not odd:
            P.add("dve", lambda e: e.tensor_tensor(out=dstT[0:64, hp, qb * NT:(qb + 1) * NT], in0=recs[0:64, :], in1=bcs[:], op=ALU.mult),
                  reads=[recs, bcs], writes=[dstT])
        else:
            ot = otmp[n_odd % 2]
            P.add("dve", lambda e: e.tensor_tensor(out=ot[:], in0=recs[0:64, :], in1=bcs[:], op=ALU.mult), reads=[recs, bcs], writes=[ot])
            P.dma("sp", [lambda e: e.dma_start(out=dstT[64:128, hp, qb * NT:(qb + 1) * NT], in_=ot[:])], ot, reads=[ot], writes=[dstT])


    class AttnPipe:
        def __init__(self, PTs, depth=2):
            self.scs = [psc[0], psc[1], pss]
            self.PTs = PTs
            self.depth = depth
            self.pend = []
            self.n = 0
            self.deferred = []

        def defer(self, fn, after=8):
            self.deferred.append([after, fn])

        def _tick(self):
            for d in self.deferred:
                d[0] -= 1
            while self.deferred and self.deferred[0][0] <= 0:
                self.deferred.pop(0)[1]()

        def tile(self, qk_fn, bias_ap, scale, pv_fn, bias_reads, mul_ap=None, mul_buf=None, PT2=None):
            self._tick()
            sc = self.scs[self.n % 3]
            pt = self.PTs[self.n % 3]
            qk_fn(sc)
            P.add("act", lambda e, sc=sc, pt=pt: e.activation(out=pt[:], in_=sc[:], func=AF.Exp, bias=bias_ap, scale=scale),
                  reads=[sc] + list(bias_reads), writes=[pt])
            if mul_ap is not None:
                pt2 = PT2[self.n % len(PT2)]
                P.add("dve", lambda e, pt=pt, pt2=pt2: e.tensor_tensor(out=pt2[:], in0=pt[:], in1=mul_ap, op=ALU.mult),
                      reads=[pt, mul_buf], writes=[pt2])
                pt = pt2
            self.n += 1
            self.pend.append((pv_fn, pt))
            assert self.depth < (len(PT2) if mul_ap is not None else len(self.PTs))
            if len(self.pend) > self.depth:
                f, p_ = self.pend.pop(0)
                f(p_)

        def flush(self):
            while self.pend:
                f, p_ = self.pend.pop(0)
                f(p_)
            while self.deferred:
                self.deferred.pop(0)[1]()

    n_odd = 0
    pipe = AttnPipe(PT, depth=3)
    PT2 = [P.sb("PTb%d" % i_, [128, NT], BF16) for i_ in range(4)]
    TE = [P.sb("TE0", [128, TW], BF16), P.sb("TE1", [128, TW], BF16)]
    for h in range(8):
        hp, pb0 = h // 2, 64 * (h % 2)
        tb = TB[h % 2]
        te = TE[h % 2]
        P.dma("pool", [lambda e, tb=tb, h=h: e.dma_start(out=tb[:], in_=dap(vA_d, h * ROWW, [[1, 128], [1, TW]]))], tb, reads=[vAd], writes=[tb])
        for c0 in range(0, TW, NT):
            cw = min(NT, TW - c0)
            pb_ = next_mm()
            mm(pb_[:, 0:cw], anti[:, :], tb[:, c0:c0 + cw], True, True, [anti, tb], [pb_])
            P.add("act", lambda e, pb_=pb_, te=te, c0=c0, cw=cw: e.activation(out=te[:, c0:c0 + cw], in_=pb_[:, 0:cw], func=AF.Exp),
                  reads=[pb_], writes=[te])
        for qb in range(4):
            ov = pov[cnt["ov"] % 2]
            cnt["ov"] += 1
            kts = list(range(4 * qb, 4 * qb + 20))
            for n_, kt in enumerate(kts):
                u = 16 + 4 * qb - kt
                qsel = QT_A if pb0 == 0 else QT_A1

                def qk(sc, kt=kt, qsel=qsel, hp=hp, qb=qb):
                    mm(sc[:, :], KT_A[:, hp, kt * 128:(kt + 1) * 128], qsel[:, hp, qb * NT:(qb + 1) * NT], True, True,
                       [KT_A, qsel], [sc])

                def pv(pt, kt=kt, h=h, qb=qb, ov=ov, first=(n_ == 0), last=(n_ == len(kts) - 1), n_odd=n_odd):
                    mm(ov[0:65, :], V_A[:, kt, h, 0:65], pt[:, :], first, last, [V_A, pt], [ov])
                    if last:
                        pipe.defer(lambda ov=ov, h=h, qb=qb, n_odd=n_odd: finalize(ov, h, qb, attnAT, n_odd, recs, bcs, otmp))

                bias_ap = pm[:, 1:2] if kt < 16 else zero_ap
                pipe.tile(qk, bias_ap, 1.0, pv, [pm, misc], mul_ap=te[:, 128 * (u + 3):128 * (u + 3) + NT], mul_buf=te, PT2=PT2)
            n_odd += h % 2
    pipe.flush()
    dump("attnAT", attnAT[:, 0, :], attnAT, [128, TQ], BF16)
    P.release(mA)

    mlaT = P.sb("mlaT", [128, 4, TQ], BF16)
    mB = P.mark()
    qT_all = P.sb("qT_all", [96, 8, TQ], BF16)

    def rope_tables(R, t0):
        posi, posf, ang, t1, ti, Ct, St = R["posi"], R["posf"], R["ang"], R["t1"], R["ti"], R["C"], R["S"]
        P.dma("sp", [lambda e: e.dma_start(out=posi[0:96, :], in_=dap(posr, t0, [[0, 96], [1, NT]]))], posi, writes=[posi])
        P.add("dve", lambda e: e.tensor_copy(out=posf[0:96, :], in_=posi[0:96, :]), reads=[posi], writes=[posf])
        P.add("dve", lambda e: e.tensor_scalar(out=ang[0:96, :], in0=posf[0:96, :], scalar1=misc[0:96, 0:1], scalar2=None, op0=ALU.mult),
              reads=[posf, misc], writes=[ang])
        for which, dst in ((0, St), (1, Ct)):
            if which == 1:
                P.add("dve", lambda e: e.tensor_scalar(out=ang[0:96, :], in0=ang[0:96, :], scalar1=math.pi / 2, scalar2=None, op0=ALU.add),
                      reads=[ang], writes=[ang])
            P.add("dve", lambda e: e.tensor_scalar(out=t1[0:96, :], in0=ang[0:96, :], scalar1=1.0 / (2 * math.pi), scalar2=None, op0=ALU.mult),
                  reads=[ang], writes=[t1])
            P.add("dve", lambda e: e.tensor_copy(out=ti[0:96, :], in_=t1[0:96, :]), reads=[t1], writes=[ti])
            P.add("dve", lambda e: e.tensor_copy(out=t1[0:96, :], in_=ti[0:96, :]), reads=[ti], writes=[t1])
            P.add("dve", lambda e: e.scalar_tensor_tensor(out=t1[0:96, :], in0=t1[0:96, :], scalar=-2 * math.pi, in1=ang[0:96, :], op0=ALU.mult, op1=ALU.add),
                  reads=[t1, ang], writes=[t1])
            P.add("dve", lambda e: e.tensor_scalar(out=t1[0:96, :], in0=t1[0:96, :], scalar1=-math.pi, scalar2=math.pi, op0=ALU.max, op1=ALU.min),
                  reads=[t1], writes=[t1])
            P.add("act", lambda e, dst=dst: e.activation(out=dst[0:96, :], in_=t1[0:96, :], func=AF.Sin), reads=[t1], writes=[dst])
        P.add("dve", lambda e: e.tensor_scalar(out=St[0:96, :], in0=St[0:96, :], scalar1=misc[0:96, 1:2], scalar2=None, op0=ALU.mult),
              reads=[St, misc], writes=[St])

    def alloc_rope():
        R = {}
        R["posi"] = P.sb("posi", [128, NT], I32)
        R["ti"] = R["posi"]
        for k in ("ang", "t1", "C", "S", "ra", "rb"):
            R[k] = P.sb(k, [128, NT], F32)
        R["posf"] = R["ra"]
        return R

    def apply_rope(R, pa, pb_, lo, hi, out_ap, out_buf):
        ra, rb = R["ra"], R["rb"]
        P.add("dve", lambda e: e.tensor_tensor(out=ra[lo:hi, :], in0=pa[lo:hi, :], in1=R["C"][lo:hi, :], op=ALU.mult), reads=[pa, R["C"]], writes=[ra])
        P.add("dve", lambda e: e.tensor_tensor(out=rb[lo:hi, :], in0=pb_[lo:hi, :], in1=R["S"][lo:hi, :], op=ALU.mult), reads=[pb_, R["S"]], writes=[rb])
        P.add("pool", lambda e: e.tensor_tensor(out=out_ap, in0=ra[lo:hi, :], in1=rb[lo:hi, :], op=ALU.add), reads=[ra, rb], writes=[out_buf])

    mmpool["banks"] = pmm + psc + pov + [pmisc]
    mB2 = P.mark()
    W = alloc_tilework(double_h=True, double_x=True)
    R = alloc_rope()
    wcq = P.sb("wcq", [128, 8, 768], BF16)
    wuq = P.sb("wuq", [128, 6, 768], BF16)
    wuqs = P.sb("wuqs", [128, 6, 768], BF16)
    cqf = P.sb("cqf", [128, 6, NT], F32)
    cqsq = P.sb("cqsq", [128, 6, NT], BF16)
    cqn = P.sb("cqn", [128, 6, NT], BF16)
    rstq = P.sb("rstq", [128, NT], F32)
    load_w(wcq, wcq[:, :, 0:384], w_in, C_CQ, [[INW, 128], [128 * INW, 8], [1, 384]])
    load_w(wcq, wcq[:, :, 384:768], w_in, C_CQ + 384, [[INW, 128], [128 * INW, 8], [1, 384]])
    load_w(wuq, wuq[:, :, :], w_uq, 0, [[768, 128], [128 * 768, 6], [1, 768]])
    load_w(wuqs, wuqs[:, :, :], w_uqs, 0, [[768, 128], [128 * 768, 6], [1, 768]])
    for i in range(4):
        xb = load_x(W, i, i * NT)
        hT = make_h(W, xb, i, gv1, sh1, gv1)
        rope_tables(R, i * NT)
        for m in range(6):
            def ev(pb_, m=m):
                P.add("act", lambda e: e.activation(out=cqf[:, m, :], in_=pb_[:, :], func=AF.Copy), reads=[pb_], writes=[cqf])
                P.add("act", lambda e: e.activation(out=cqsq[:, m, :], in_=pb_[:, :], func=AF.Square), reads=[pb_], writes=[cqsq])
            proj_fm(hT, wcq, lambda k, m=m: wcq[:, k, m * 128:(m + 1) * 128], 8, ev)
        rms_rstd(cqsq, 6, rstq, 768.0, None)
        for m in range(6):
            P.add("dve", lambda e, m=m: e.scalar_tensor_tensor(out=cqn[:, m, :], in0=cqf[:, m, :], scalar=gq[:, m:m + 1], in1=rstq[:], op0=ALU.mult, op1=ALU.mult),
                  reads=[cqf, rstq, gq], writes=[cqn])
        for h in range(8):
            pa, pb_ = pmm[0], pmm[1]
            for k in range(6):
                mm(pa[0:96, :], wuq[:, k, h * 96:(h + 1) * 96], cqn[:, k, :], k == 0, k == 5, [wuq, cqn], [pa])
            for k in range(6):
                mm(pb_[0:96, :], wuqs[:, k, h * 96:(h + 1) * 96], cqn[:, k, :], k == 0, k == 5, [wuqs, cqn], [pb_])
            apply_rope(R, pa, pb_, 0, 96, qT_all[0:96, h, i * NT:(i + 1) * NT], qT_all)
    dump("qT0", qT_all[:, 0, :], qT_all, [96, TQ], BF16)
    P.release(mB2)

    latT = P.sb("latT", [128, 2, S], BF16)
    KT = P.sb("KT", [96, S], BF16)
    mB1 = P.mark()
    W = alloc_tilework(double_h=False, double_x=True)
    R = alloc_rope()
    wkv = P.sb("wkv", [128, 8, 256], BF16)
    wkr = P.sb("wkr", [128, 8, 192], BF16)
    ckf = P.sb("ckf", [128, 2, NT], F32)
    cksq = P.sb("cksq", [128, 2, NT], BF16)
    rstk = P.sb("rstk", [128, NT], F32)
    load_w(wkv, wkv[:, :, :], w_in, C_CKV, [[INW, 128], [128 * INW, 8], [1, 256]])
    load_w(wkr, wkr[:, :, :], w_kr, 0, [[192, 128], [128 * 192, 8], [1, 192]])
    for n in range(16):
        t0 = n * NT
        xb = load_x(W, n, t0)
        hT = make_h(W, xb, n, gv1, sh1, gv1)
        rope_tables(R, t0)
        for m in range(2):
            def ev(pb_, m=m):
                P.add("act", lambda e: e.activation(out=ckf[:, m, :], in_=pb_[:, :], func=AF.Copy), reads=[pb_], writes=[ckf])
                P.add("act", lambda e: e.activation(out=cksq[:, m, :], in_=pb_[:, :], func=AF.Square), reads=[pb_], writes=[cksq])
            proj_fm(hT, wkv, lambda k, m=m: wkv[:, k, m * 128:(m + 1) * 128], 8, ev)
        rms_rstd(cksq, 2, rstk, 256.0, None)
        for m in range(2):
            P.add("dve", lambda e, m=m, t0=t0: e.scalar_tensor_tensor(out=latT[:, m, t0:t0 + NT], in0=ckf[:, m, :], scalar=gkv[:, m:m + 1], in1=rstk[:], op0=ALU.mult, op1=ALU.mult),
                  reads=[ckf, rstk, gkv], writes=[latT])
        pa, pb_ = pmm[0], pmm[1]
        for k in range(8):
            mm(pa[0:96, :], wkr[:, k, 0:96], hT[:, k, :], k == 0, k == 7, [wkr, hT], [pa])
        for k in range(8):
            mm(pb_[0:96, :], wkr[:, k, 96:192], hT[:, k, :], k == 0, k == 7, [wkr, hT], [pb_])
        apply_rope(R, pa, pb_, 64, 96, KT[64:96, t0:t0 + NT], KT)
    dump("latT", latT[:, 0, 0:TQ], latT, [128, TQ], BF16)
    P.release(mB1)

    mmpool["banks"] = pmm
    mB3 = P.mark()
    wukv = P.sb("wukv", [128, 2, 1024], BF16)
    Vh = [P.sb("Vh0", [128, 64, 65], BF16), P.sb("Vh1", [128, 64, 65], BF16)]
    PT = [P.sb("PT0", [128, NT], BF16), P.sb("PT1", [128, NT], BF16), P.sb("PT2", [128, NT], BF16)]
    recs = P.sb("recs", [128, NT], F32)
    P.add("dve", lambda e: e.memset(recs[:], 0.0), writes=[recs])
    bcs = P.sb("bcs", [64, NT], F32)
    otmp = [P.sb("otmp0", [64, NT], BF16), P.sb("otmp1", [64, NT], BF16)]
    load_w(wukv, wukv[:, :, :], w_ukv, 0, [[1024, 128], [128 * 1024, 2], [1, 1024]])
    for vb in Vh:
        P.add("pool", lambda e, vb=vb: e.memset(vb[:, :, 64:65], 1.0), writes=[vb])
    n_odd = 0
    pipe = AttnPipe(PT)
    for h in range(8):
        vb = Vh[h % 2]
        for n in range(16):
            pb_ = next_mm()
            for k in range(2):
                mm(pb_[:, :], wukv[:, k, h * 128:h * 128 + 128], latT[:, k, n * NT:(n + 1) * NT], k == 0, k == 1, [wukv, latT], [pb_])
            evac(KT[0:64, n * NT:(n + 1) * NT], pb_[0:64, :], [pb_], [KT])
        for g in range(8):
            pb_ = next_mm()
            for tt in range(8):
                kt = g * 8 + tt
                for k in range(2):
                    mm(pb_[:, tt * 64:(tt + 1) * 64], latT[:, k, kt * 128:(kt + 1) * 128], wukv[:, k, h * 128 + 64:h * 128 + 128], k == 0, k == 1,
                       [wukv, latT], [pb_])
            evac(vb[:, g * 8:(g + 1) * 8, 0:64], pb_[:, :].rearrange("p (t d) -> p t d", t=8), [pb_], [vb])
        if debug and h == 0:
            dump("KT0", KT[:, 0:TQ], KT, [96, TQ], BF16)
            dump("Vh0", vb[:, :, :], vb, [128, 64, 65], BF16)
        for qb in range(4):
            ov = pov[cnt["ov"] % 2]
            cnt["ov"] += 1
            klist = [(0, kt) for kt in range(4 * (qb + 1))] + [(s, kt) for s in (1, 2, 3) for kt in range(16)]
            for n_, (s, kt) in enumerate(klist):
                g = s * 16 + kt
                diag = (s == 0 and kt >= 4 * qb)

                def qk(sc, g=g, diag=diag, kt=kt, qb=qb, h=h):
                    mm(sc[:, :], KT[0:96, g * 128:(g + 1) * 128], qT_all[0:96, h, qb * NT:(qb + 1) * NT], True, not diag, [KT, qT_all], [sc])
                    if diag:
                        wv = kt - 4 * qb
                        mm(sc[:, :], ident[:, :], MT[:, 128 * (3 - wv):128 * (3 - wv) + NT], False, True, [ident, MT], [sc])

                def pv(pt, g=g, h=h, qb=qb, ov=ov, vb=vb, first=(n_ == 0), last=(n_ == len(klist) - 1), n_odd=n_odd):
                    mm(ov[0:65, :], vb[:, g, 0:65], pt[:, :], first, last, [vb, pt], [ov])
                    if last:
                        pipe.defer(lambda ov=ov, h=h, qb=qb, n_odd=n_odd: finalize(ov, h, qb, mlaT, n_odd, recs, bcs, otmp))

                pipe.tile(qk, pm[:, s:s + 1], 96.0 ** -0.5, pv, [pm])
            n_odd += h % 2
    pipe.flush()
    dump("mlaT", mlaT[:, 0, :], mlaT, [128, TQ], BF16)
    P.release(mB)

    mmpool["banks"] = pmm + psc + pov + [pmisc]
    top0 = P.top
    wbufs = [P.sb_top("wb0", [128, 4096], BF16), P.sb_top("wb1", [128, 4096], BF16), P.sb_top("wb2", [128, 4096], BF16)]
    mCm = P.mark()
    mergedT = P.sb("mergedT", [128, 8, TQ], BF16)
    mCh = P.mark()
    hTo = P.sb("hTo", [128, 8, TQ], BF16)
    mC0 = P.mark()
    W = alloc_tilework()
    for i in range(4):
        xb = load_x(W, i, i * NT)
        hT = make_h(W, xb, i, gv1, sh1, gv1)
        evac(hTo[:, :, i * NT:(i + 1) * NT], hT[:], [hT], [hTo], eng="pool")
    P.release(mC0)

    def next_w():
        b = wbufs[cnt["w"] % 3]
        cnt["w"] += 1
        return b

    mC1 = P.mark()
    sig = [P.sb("sig0", [128, NT], F32), P.sb("sig1", [128, NT], F32)]
    prod = [P.sb("prod0", [128, NT], F32), P.sb("prod1", [128, NT], F32)]
    for m in range(8):
        wg = next_w()
        wgv = wg[:, 0:2048].rearrange("p (k g c) -> p k g c", k=8, g=2)
        load_w(wg, wgv[:, :, 0, :], w_in, C_GA + m * 128, [[INW, 128], [128 * INW, 8], [1, 128]])
        load_w(wg, wgv[:, :, 1, :], w_in, C_GB + m * 128, [[INW, 128], [128 * INW, 8], [1, 128]])
        wu = next_w()
        wuv = wu[:, 0:1024].rearrange("p (k g c) -> p k g c", k=4, g=2)
        load_w(wu, wuv[:, :, 0, :], w_up_a, m * 128, [[D, 128], [128 * D, 4], [1, 128]])
        load_w(wu, wuv[:, :, 1, :], w_up_b, m * 128, [[D, 128], [128 * D, 4], [1, 128]])
        for i in range(4):
            tsl = slice(i * NT, (i + 1) * NT)
            for gidx, srcT in ((0, attnAT), (1, mlaT)):
                pg = next_mm()
                for k in range(8):
                    mm(pg[:, :], wgv[:, k, gidx, :], hTo[:, k, tsl], k == 0, k == 7, [wg, hTo], [pg])
                sg = sig[gidx]
                P.add("act", lambda e, sg=sg, pg=pg: e.activation(out=sg[:], in_=pg[:], func=AF.Sigmoid), reads=[pg], writes=[sg])
                py = next_mm()
                for k in range(4):
                    mm(py[:, :], wuv[:, k, gidx, :], srcT[:, k, tsl], k == 0, k == 3, [wu, srcT], [py])
                pr = prod[gidx]
                P.add("dve", lambda e, pr=pr, py=py, sg=sg: e.tensor_tensor(out=pr[:], in0=py[:], in1=sg[:], op=ALU.mult), reads=[py, sg], writes=[pr])
            P.add("pool", lambda e, m=m, tsl=tsl: e.tensor_tensor(out=mergedT[:, m, tsl], in0=prod[0][:], in1=prod[1][:], op=ALU.add),
                  reads=[prod[0], prod[1]], writes=[mergedT])
    dump("mergedT", mergedT[:, 0, :], mergedT, [128, TQ], BF16)
    P.release(mCh)

    x1T = P.sb_top("x1T", [128, 8, TQ], F32)
    mC2 = P.mark()
    xres = [P.sb("xres0", [128, NT], F32), P.sb("xres1", [128, NT], F32)]
    nx = 0
    for m in range(8):
        wo = next_w()
        wov = wo[:, 0:1024].rearrange("p (k c) -> p k c", k=8)
        load_w(wo, wov, w_o, m * 128, [[D, 128], [128 * D, 8], [1, 128]])
        for i in range(4):
            tsl = slice(i * NT, (i + 1) * NT)
            xr = xres[nx % 2]
            nx += 1
            P.dma("sp", [lambda e, xr=xr, m=m, i=i: e.dma_start(out=xr[:], in_=dap(xT, m * 128 * S + i * NT, [[S, 128], [1, NT]]))], xr, writes=[xr])
            po = next_mm()
            for k in range(8):
                mm(po[:, :], wov[:, k, :], mergedT[:, k, tsl], k == 0, k == 7, [wo, mergedT], [po])
            P.add("dve", lambda e, po=po, xr=xr, m=m, tsl=tsl: e.scalar_tensor_tensor(out=x1T[:, m, tsl], in0=po[:], scalar=gt1[:, m:m + 1], in1=xr[:], op0=ALU.mult, op1=ALU.add),
                  reads=[po, xr, modT], writes=[x1T])
    dump("x1T", x1T[:, 0, :], x1T, [128, TQ], F32)
    P.release(mA)

    h2T = P.sb_top("h2T", [128, 8, TQ], BF16)
    mC3 = P.mark()
    sqb = P.sb("sq", [128, 8, NT], BF16)
    rstd = P.sb("rstd", [128, NT], F32)
    tmpc = [P.sb("tmpc0", [128, NT], F32), P.sb("tmpc1", [128, NT], F32)]
    for i in range(4):
        tsl = slice(i * NT, (i + 1) * NT)
        P.add("act", lambda e, tsl=tsl, sqb=sqb: e.activation(out=sqb[:], in_=x1T[:, :, tsl], func=AF.Square), reads=[x1T], writes=[sqb])
        rms_rstd(sqb, 8, rstd, float(D), None)
        for c in range(8):
            tc_ = tmpc[c % 2]
            P.add("dve", lambda e, c=c, tc_=tc_, tsl=tsl, rstd=rstd: e.scalar_tensor_tensor(out=tc_[:], in0=x1T[:, c, tsl], scalar=gv2[:, c:c + 1], in1=rstd[:], op0=ALU.mult, op1=ALU.mult),
                  reads=[x1T, rstd, gv2], writes=[tc_])
            P.add("act", lambda e, c=c, tc_=tc_, tsl=tsl: e.activation(out=h2T[:, c, tsl], in_=tc_[:], func=AF.Identity, bias=sh2[:, c:c + 1], scale=1.0),
                  reads=[tc_, modT], writes=[h2T])
    P.release(mC3)

    mC4 = P.mark()
    actT = P.sb("actT", [128, 22, 1024], BF16)
    sil = [P.sb("sil0", [128, NT], F32), P.sb("sil1", [128, NT], F32)]
    for half in range(2):
        for f in range(22):
            wgu = next_w()
            wguv = wgu[:, 0:2048].rearrange("p (k g c) -> p k g c", k=8, g=2)
            load_w(wgu, wguv[:, :, 0, :], w_gate, f * 128, [[DFF, 128], [128 * DFF, 8], [1, 128]])
            load_w(wgu, wguv[:, :, 1, :], w_up, f * 128, [[DFF, 128], [128 * DFF, 8], [1, 128]])
            for i in range(2):
                tsl = slice(half * 1024 + i * NT, half * 1024 + (i + 1) * NT)
                pg = next_mm()
                for k in range(8):
                    mm(pg[:, :], wguv[:, k, 0, :], h2T[:, k, tsl], k == 0, k == 7, [wgu, h2T], [pg])
                sl_ = sil[i]
                P.add("act", lambda e, sl_=sl_, pg=pg: e.activation(out=sl_[:], in_=pg[:], func=AF.Silu), reads=[pg], writes=[sl_])
                pu = next_mm()
                for k in range(8):
                    mm(pu[:, :], wguv[:, k, 1, :], h2T[:, k, tsl], k == 0, k == 7, [wgu, h2T], [pu])
                P.add("dve", lambda e, sl_=sl_, pu=pu, f=f, i=i: e.tensor_tensor(out=actT[:, f, i * NT:(i + 1) * NT], in0=pu[:], in1=sl_[:], op=ALU.mult),
                      reads=[pu, sl_], writes=[actT])
        for m in range(8):
            wd = next_w()
            wdv = wd[:, 0:2816].rearrange("p (k c) -> p k c", k=22)
            load_w(wd, wdv, w_down, m * 128, [[D, 128], [128 * D, 22], [1, 128]])
            for i in range(2):
                tsl = slice(half * 1024 + i * NT, half * 1024 + (i + 1) * NT)
                pd = next_mm()
                for k in range(22):
                    mm(pd[:, :], wdv[:, k, :], actT[:, k, i * NT:(i + 1) * NT], k == 0, k == 21, [wd, actT], [pd])
                P.add("dve", lambda e, pd=pd, m=m, tsl=tsl: e.scalar_tensor_tensor(out=x1T[:, m, tsl], in0=pd[:], scalar=gt2[:, m:m + 1], in1=x1T[:, m, tsl], op0=ALU.mult, op1=ALU.add),
                      reads=[pd, x1T, modT], writes=[x1T])

    P.release(mC4)
    P.top += 32768
    sqb = P.sb("sq", [128, 8, NT], BF16)
    rstd = P.sb("rstd", [128, NT], F32)
    ob = [P.sb("ob0", [128, 8, NT], F32), P.sb("ob1", [128, 8, NT], F32)]
    for i in range(4):
        tsl = slice(i * NT, (i + 1) * NT)
        o_ = ob[i % 2]
        P.add("act", lambda e, tsl=tsl, sqb=sqb: e.activation(out=sqb[:], in_=x1T[:, :, tsl], func=AF.Square), reads=[x1T], writes=[sqb])
        rms_rstd(sqb, 8, rstd, float(D), None)
        for c in range(8):
            P.add("dve", lambda e, c=c, o_=o_, tsl=tsl, rstd=rstd: e.scalar_tensor_tensor(out=o_[:, c, :], in0=x1T[:, c, tsl], scalar=gfin[:, c:c + 1], in1=rstd[:], op0=ALU.mult, op1=ALU.mult),
                  reads=[x1T, rstd, gfin], writes=[o_])
        P.dma("sp", [lambda e, o_=o_, i=i: e.dma_start(out=dap(outT, i * NT, [[TQ, 128], [128 * TQ, 8], [1, NT]]), in_=o_[:])], o_, reads=[o_])
    stats = P.emit(final_wait_bufs=ob + dumped)
    return nc, stats, dbg


def _t5_bucket_np(dist):
    max_exact = 16
    d = np.maximum(dist, 1).astype(np.float32)
    log_b = max_exact + (np.log(d / np.float32(max_exact)) / np.float32(math.log(2048 / max_exact)) * np.float32(32 - max_exact)).astype(np.int32)
    log_b = np.minimum(log_b, 31)
    return np.where(dist < max_exact, dist, log_b)


def _constants():
    OH = np.zeros((128, ROWW), np.float32)
    e = np.arange(ROWW)
    delta = e - 511
    mult = np.zeros(ROWW, np.int64)
    for (w, d) in ((128, 1), (512, 4), (2048, 16)):
        mult += ((delta >= 0) & (delta <= w) & (delta % d == 0)).astype(np.int64)
    valid = mult > 0
    bucket = _t5_bucket_np(np.clip(delta, 0, 2048).astype(np.int32))
    OH[bucket[valid], e[valid]] = 1.0
    OH[32, :] = NEG
    OH[32, valid] = np.log(mult[valid].astype(np.float64)).astype(np.float32)
    ident = np.eye(128, dtype=np.float32)
    anti = np.ascontiguousarray(ident[::-1])
    kk = np.arange(128)[:, None]
    cc = np.arange(896)[None, :]
    MT = np.where(cc - 384 - kk >= 0, 0.0, NEG).astype(np.float32)
    misc = np.zeros((128, 4), np.float32)
    half = 16
    freqs = (10000.0 ** (-np.arange(half, dtype=np.float32) / np.float32(half))).astype(np.float32)
    for r in range(32):
        misc[64 + r, 0] = freqs[r % 16]
        misc[64 + r, 1] = -1.0 if r < 16 else 1.0
    misc[:, 2] = EPS
    return OH, ident, anti, MT, misc


def _colT(v, n):
    return np.ascontiguousarray(v.reshape(n, 128).T)


def make_in_maps(x, c, positions, rel_bias, w_ada, b_ada, g_mix, w_in, g_q_lora, w_uq, g_kv_lora, w_ukv,
                 w_up_a, w_up_b, w_o, g_ffn, w_gate, w_up, w_down, g_final):
    OH, ident, anti, MT, misc = _constants()
    w_in0 = np.ascontiguousarray(w_in[0])
    kr = w_in0[:, C_KR:C_KR + 32]
    w_kr = np.zeros((D, 2, 96), np.float32)
    w_kr[:, 0, 64:96] = kr
    w_kr[:, 1, 64:96] = np.concatenate([kr[:, 16:32], kr[:, 0:16]], axis=1)
    wuq = w_uq[0]
    wuqs = wuq.copy()
    wuqs[:, :, 64:80] = wuq[:, :, 80:96]
    wuqs[:, :, 80:96] = wuq[:, :, 64:80]
    rbT = np.zeros((128, 128), np.float32)
    rbT[0:32, 0:8] = rel_bias.T
    rbT[32, 0:8] = 1.0
    shared = {
        "rbT": rbT, "OH": OH, "w_ada": np.ascontiguousarray(w_ada[0]), "b_adaT": _colT(b_ada[0], 48),
        "g_mixT": _colT(g_mix[0], 8), "w_in": w_in0, "w_kr": w_kr, "g_qT": _colT(g_q_lora[0], 6),
        "w_uq": np.ascontiguousarray(wuq.reshape(768, 768)), "w_uqs": np.ascontiguousarray(wuqs.reshape(768, 768)),
        "g_kvT": _colT(g_kv_lora[0], 2), "w_ukv": np.ascontiguousarray(w_ukv[0].reshape(256, 1024)),
        "w_up_a": np.ascontiguousarray(w_up_a[0]), "w_up_b": np.ascontiguousarray(w_up_b[0]), "w_o": np.ascontiguousarray(w_o[0]),
        "g_ffnT": _colT(g_ffn[0], 8), "w_gate": np.ascontiguousarray(w_gate[0]), "w_up": np.ascontiguousarray(w_up[0]),
        "w_down": np.ascontiguousarray(w_down[0]), "g_finT": _colT(g_final, 8),
        "ident": ident, "anti": anti, "MT": MT, "misc": misc,
    }
    in_maps = []
    for core in range(8):
        b, j = core // 4, core % 4
        order = [(j - s) % 4 for s in range(4)]
        xb = x[b]
        xs = np.concatenate([xb[ch * TQ:(ch + 1) * TQ] for ch in order], axis=0)
        pos = np.concatenate([positions[b, ch * TQ:(ch + 1) * TQ] for ch in order]).astype(np.int32).reshape(1, S)
        pmv = np.zeros((128, 4), np.float32)
        for s in range(1, 4):
            if j - s < 0:
                pmv[:, s] = NEG
        m = dict(shared)
        m["xT"] = np.ascontiguousarray(xs.T)
        m["posr"] = pos
        m["cT"] = _colT(c[b], 8)
        m["pm"] = pmv
        in_maps.append(m)
    return in_maps


_CACHE = {}


def kernel(**inputs):
    inputs = {k: np.asarray(v) for k, v in inputs.items()}
    if "nc" not in _CACHE:
        _CACHE["nc"] = build(False)[0]
    nc = _CACHE["nc"]
    in_maps = make_in_maps(**inputs)
    res = run_bass_kernel_spmd(nc, in_maps, core_ids=list(range(8)))
    out = np.empty((2, S, D), np.float32)
    for core in range(8):
        b, j = core // 4, core % 4
        out[b, j * TQ:(j + 1) * TQ, :] = res.results[core]["outT"].T
    return out
```

```python
import math
import numpy as np
import concourse.bass as bass
import concourse.mybir as mybir
from concourse.bass_utils import run_bass_kernel_spmd

F32 = mybir.dt.float32
BF16 = mybir.dt.bfloat16
I32 = mybir.dt.int32
ALU = mybir.AluOpType
AF = mybir.ActivationFunctionType

ENGS = ("pe", "act", "dve", "pool", "sp")
NEG = -30000.0
MAXW = 1
S = 8192
D = 1024
TQ = 2048
NT = 512
DFF = 2816
EPS = 1e-6
C_AQ, C_AK, C_AV, C_CQ, C_CKV, C_KR, C_GA, C_GB = 0, 512, 1024, 1536, 2304, 2560, 2592, 3616
INW = 4640
TW = 2944
ROWW = 3072


class Buf:
    __slots__ = ("name", "t", "lastw", "readers", "dsem", "dcount")

    def __init__(self, name, t=None):
        self.name = name
        self.t = t
        self.lastw = None
        self.readers = {}
        self.dsem = None
        self.dcount = 0

    def __getitem__(self, idx):
        return self.t[idx]


class Op:
    __slots__ = ("eng", "fn", "deps", "signal", "ms", "is_dma", "sem", "val")

    def __init__(self, eng, fn):
        self.eng = eng
        self.fn = fn
        self.deps = []
        self.signal = False
        self.ms = 0
        self.is_dma = False
        self.sem = None
        self.val = 0


class Prog:
    def __init__(self, nc, base=18432, top=229376):
        self.nc = nc
        self.q = {e: [] for e in ENGS}
        self.esem = {}
        self.dma_bufs = []
        self.bar = []
        self.ptr = base
        self.top = top
        self.nalloc = 0

    def sb(self, name, shape, dtype):
        per = int(np.prod(shape[1:])) * mybir.dt.size(dtype)
        per = (per + 63) // 64 * 64
        assert self.ptr + per <= self.top, f"SBUF arena overflow at {name}: {self.ptr}+{per} > {self.top}"
        self.nalloc += 1
        t = self.nc.alloc_sbuf_tensor_at(f"{name}_{self.nalloc}", list(shape), dtype, offset=self.ptr)
        self.ptr += per
        return Buf(name, t)

    def sb_top(self, name, shape, dtype):
        per = int(np.prod(shape[1:])) * mybir.dt.size(dtype)
        per = (per + 63) // 64 * 64
        assert self.top - per >= self.ptr, f"SBUF arena overflow (top) at {name}"
        self.nalloc += 1
        self.top -= per
        t = self.nc.alloc_sbuf_tensor_at(f"{name}_{self.nalloc}", list(shape), dtype, offset=self.top)
        return Buf(name, t)

    def mark(self):
        return self.ptr

    def release(self, m):
        self.ptr = m
        self.barrier()

    def ps(self, name):
        return Buf(name, self.nc.alloc_psum_tensor(name, [128, 512], F32))

    def barrier(self):
        deps = []
        for e in ENGS:
            if self.q[e]:
                deps.append(self.q[e][-1])
        for b in self.dma_bufs:
            if b.lastw is not None and b.lastw.is_dma:
                deps.append(b.lastw)
            for r in b.readers.values():
                if r.is_dma:
                    deps.append(r)
        self.bar = deps

    def _track(self, op, reads, writes):
        deps = {}
        for d in self.bar:
            deps[id(d)] = d
        for b in reads:
            d = b.lastw
            if d is not None:
                deps[id(d)] = d
        for b in writes:
            d = b.lastw
            if d is not None:
                deps[id(d)] = d
            for r in b.readers.values():
                deps[id(r)] = r
        deps.pop(id(op), None)
        op.deps = list(deps.values())
        for b in reads:
            b.readers[op.eng] = op
        for b in writes:
            b.lastw = op
            b.readers = {}

    def add(self, eng, fn, reads=(), writes=()):
        op = Op(eng, fn)
        self._track(op, reads, writes)
        self.q[eng].append(op)
        return op

    def dma(self, queue, fns, sem_buf, reads=(), writes=()):
        op = Op(queue, fns)
        op.is_dma = True
        if sem_buf.dsem is None:
            sem_buf.dsem = self.nc.alloc_semaphore("d%d_%s" % (len(self.dma_bufs), sem_buf.name))
            self.dma_bufs.append(sem_buf)
        sem_buf.dcount += 16 * len(fns)
        op.sem = sem_buf.dsem
        op.val = sem_buf.dcount
        self._track(op, reads, writes)
        self.q[queue].append(op)
        return op

    def emit(self, final_wait_bufs=()):
        nc = self.nc
        for e in ENGS:
            self.esem[e] = nc.alloc_semaphore("e_" + e)
        for e in ENGS:
            for op in self.q[e]:
                for d in op.deps:
                    if d.is_dma:
                        continue
                    if d.eng == "pe" and e == "pe" and not op.is_dma:
                        continue
                    d.signal = True
        for e in ENGS:
            c = 0
            for op in self.q[e]:
                if op.signal and not op.is_dma:
                    c += 1
                    op.ms = c
        stats = {}

        def run(e, eng):
            seen = {}
            nw = 0
            for op in self.q[e]:
                waits = []
                for d in op.deps:
                    if d.is_dma:
                        s, v = d.sem, d.val
                    else:
                        if d.eng == "pe" and e == "pe" and not op.is_dma:
                            continue
                        s, v = self.esem[d.eng], d.ms
                    k = id(s)
                    if seen.get(k, 0) >= v:
                        continue
                    seen[k] = v
                    waits.append((s, v))
                    nw += 1
                emb = waits[-MAXW:] if MAXW > 0 else []
                for (s, v) in waits[:len(waits) - len(emb)]:
                    eng.wait_ge(s, v)
                if op.is_dma:
                    first = True
                    for f in op.fn:
                        ins = f(eng)
                        if first:
                            for (s, v) in emb:
                                ins._wait_ge(s, v)
                            first = False
                        ins.then_inc(op.sem, 16)
                else:
                    ins = op.fn(eng)
                    for (s, v) in emb:
                        ins._wait_ge(s, v)
                    if op.signal:
                        ins.then_inc(self.esem[e], 1)
            if e == "sp":
                for b in final_wait_bufs:
                    eng.wait_ge(b.dsem, b.dcount)
            stats[e] = (len(self.q[e]), nw)

        with nc.Block() as block:
            @block.tensor
            def _(eng):
                run("pe", eng)

            @block.scalar
            def _(eng):
                run("act", eng)

            @block.vector
            def _(eng):
                run("dve", eng)

            @block.gpsimd
            def _(eng):
                run("pool", eng)

            @block.sync
            def _(eng):
                run("sp", eng)
        return stats


def dap(t, off, dims):
    return bass.AP(t, off, [list(d) for d in dims])


def build(debug=False):
    nc = bass.Bass("TRN2", target_bir_lowering=False)
    P = Prog(nc)

    def din(name, shape, dt=F32):
        return nc.dram_tensor(name, list(shape), dt, kind="ExternalInput")

    xT = din("xT", [D, S])
    posr = din("posr", [1, S], I32)
    cT = din("cT", [128, 8])
    pm_d = din("pm", [128, 4])
    rbT_d = din("rbT", [128, 128])
    OH_d = din("OH", [128, ROWW])
    w_ada = din("w_ada", [D, 6 * D])
    b_adaT = din("b_adaT", [128, 48])
    g_mixT = din("g_mixT", [128, 8])
    w_in = din("w_in", [D, INW])
    w_kr = din("w_kr", [D, 2, 96])
    g_qT = din("g_qT", [128, 6])
    w_uq = din("w_uq", [768, 8 * 96])
    w_uqs = din("w_uqs", [768, 8 * 96])
    g_kvT = din("g_kvT", [128, 2])
    w_ukv = din("w_ukv", [256, 8 * 128])
    w_up_a = din("w_up_a", [512, D])
    w_up_b = din("w_up_b", [512, D])
    w_o = din("w_o", [D, D])
    g_ffnT = din("g_ffnT", [128, 8])
    w_gate = din("w_gate", [D, DFF])
    w_up = din("w_up", [D, DFF])
    w_down = din("w_down", [DFF, D])
    g_finT = din("g_finT", [128, 8])
    ident_d = din("ident", [128, 128])
    anti_d = din("anti", [128, 128])
    MT_d = din("MT", [128, 896])
    misc_d = din("misc", [128, 4])
    outT = nc.dram_tensor("outT", [D, TQ], F32, kind="ExternalOutput")
    vA_d = nc.dram_tensor("vA_scr", [8, ROWW], F32)
    dbg = {}

    def dbg_out(name, shape, dt=F32):
        dbg[name] = nc.dram_tensor("dbg_" + name, list(shape), dt, kind="ExternalOutput")
        return dbg[name]

    pmm = [P.ps("pmm0"), P.ps("pmm1")]
    pss = P.ps("pss")
    psc = [P.ps("psc0"), P.ps("psc1")]
    pov = [P.ps("pov0"), P.ps("pov1")]
    pmisc = P.ps("pmisc")
    cnt = {"mm": 0, "sc": 0, "ov": 0, "ev": 0, "w": 0}

    ones_bf = P.sb("ones_bf", [128, 128], BF16)
    ones_f = P.sb("ones_f", [128, 128], F32)
    ident = P.sb("ident", [128, 128], BF16)
    anti = P.sb("anti", [128, 128], BF16)
    MT = P.sb("MT", [128, 896], BF16)
    misc = P.sb("misc", [128, 4], F32)
    pm = P.sb("pm", [128, 4], F32)
    modT = P.sb("modT", [128, 48], F32)
    gv1 = P.sb("gv1", [128, 8], F32)
    gv2 = P.sb("gv2", [128, 8], F32)
    gfin = P.sb("gfin", [128, 8], F32)
    gq = P.sb("gq", [128, 6], F32)
    gkv = P.sb("gkv", [128, 2], F32)
    attnAT = P.sb("attnAT", [128, 4, TQ], BF16)
    dump_stage = P.sb("dump_stage", [128, 512], F32) if debug else None

    P.add("dve", lambda e: e.memset(ones_bf[:], 1.0), writes=[ones_bf])
    P.add("dve", lambda e: e.memset(ones_f[:], 0.0), writes=[ones_f])
    P.add("dve", lambda e: e.memset(ones_f[64:65, :], 1.0), writes=[ones_f])
    P.dma("pool", [lambda e: e.dma_start(out=ident[:], in_=ident_d[:]),
                   lambda e: e.dma_start(out=anti[:], in_=anti_d[:]),
                   lambda e: e.dma_start(out=MT[:], in_=MT_d[:])], ident, writes=[ident, anti, MT])
    P.dma("sp", [lambda e: e.dma_start(out=misc[:], in_=misc_d[:]),
                 lambda e: e.dma_start(out=pm[:], in_=pm_d[:]),
                 lambda e: e.dma_start(out=gfin[:], in_=g_finT[:]),
                 lambda e: e.dma_start(out=gq[:], in_=g_qT[:]),
                 lambda e: e.dma_start(out=gkv[:], in_=g_kvT[:])], misc, writes=[misc, pm, gfin, gq, gkv])
    eps_ap = misc[:, 2:3]
    zero_ap = misc[:, 3:4]

    def evac(out_ap, in_ap, reads, writes, eng=None, scale=None):
        if eng is None:
            eng = "act" if cnt["ev"] % 2 == 0 else "dve"
            cnt["ev"] += 1
        if eng == "act":
            if scale is None:
                P.add("act", lambda e: e.activation(out=out_ap, in_=in_ap, func=AF.Copy), reads=reads, writes=writes)
            else:
                P.add("act", lambda e: e.activation(out=out_ap, in_=in_ap, func=AF.Copy, scale=scale), reads=reads, writes=writes)
        else:
            if scale is None:
                P.add(eng, lambda e: e.tensor_copy(out=out_ap, in_=in_ap), reads=reads, writes=writes)
            else:
                P.add(eng, lambda e: e.tensor_scalar(out=out_ap, in0=in_ap, scalar1=scale, scalar2=None, op0=ALU.mult), reads=reads, writes=writes)

    def mm(out_ap, lhsT, rhs, start, stop, reads, writes):
        P.add("pe", lambda e: e.matmul(out_ap, lhsT=lhsT, rhs=rhs, start=start, stop=stop), reads=reads, writes=writes)

    def next_mm():
        b = pmm[cnt["mm"] % 2]
        cnt["mm"] += 1
        return b

    def load_w(buf, out_ap, dram, off, dims):
        P.dma("pool", [lambda e: e.dma_start(out=out_ap, in_=dap(dram, off, dims))], buf, writes=[buf])

    def dump(name, src_ap, src_buf, shape, dt):
        if not debug:
            return
        o = dbg_out(name, shape, dt)
        P.dma("sp", [lambda e: e.dma_start(out=o[:], in_=src_ap)], src_buf, reads=[src_buf])
        dumped.append(src_buf)

    dumped = []

    tmp48 = P.sb("tmp48", [128, 48], F32)
    gtmp = P.sb("gtmp", [128, 16], F32)
    condT = P.sb("condT", [128, 8], F32)
    condB = P.sb("condB", [128, 8], BF16)
    m0 = P.mark()
    P.dma("sp", [lambda e: e.dma_start(out=condT[:], in_=cT[:]),
                 lambda e: e.dma_start(out=tmp48[:], in_=b_adaT[:]),
                 lambda e: e.dma_start(out=gtmp[:, 0:8], in_=g_mixT[:]),
                 lambda e: e.dma_start(out=gtmp[:, 8:16], in_=g_ffnT[:])], condT, writes=[condT, tmp48, gtmp])
    P.add("act", lambda e: e.activation(out=condB[:], in_=condT[:], func=AF.Silu), reads=[condT], writes=[condB])

    def ada_slab(sl, wb):
        load_w(wb, wb[:], w_ada, sl * 512, [[6 * D, 128], [128 * 6 * D, 8], [1, 512]])
        for cc in range(4):
            ci = sl * 4 + cc
            for kc in range(8):
                mm(pmisc[:, ci:ci + 1], wb[:, kc, cc * 128:(cc + 1) * 128], condB[:, kc:kc + 1], kc == 0, kc == 7,
                   [wb, condB], [pmisc])

    wst0 = [P.sb("wst0", [128, 8, 512], BF16), P.sb("wst1", [128, 8, 512], BF16)]
    for sl in range(4):
        ada_slab(sl, wst0[sl % 2])
    P.add("dve", lambda e: e.tensor_tensor(out=modT[:, 0:16], in0=pmisc[:, 0:16], in1=tmp48[:, 0:16], op=ALU.add), reads=[pmisc, tmp48], writes=[modT])
    P.add("dve", lambda e: e.scalar_tensor_tensor(out=gv1[:], in0=modT[:, 8:16], scalar=1.0, in1=gtmp[:, 0:8], op0=ALU.add, op1=ALU.mult),
          reads=[modT, gtmp], writes=[gv1])
    sh1, gt1, sh2, gt2 = modT[:, 0:8], modT[:, 16:24], modT[:, 24:32], modT[:, 40:48]

    rbT = P.sb("rbT", [128, 128], F32)
    OH = P.sb("OH", [128, ROWW], F32)
    vA_sb = P.sb("vA_sb", [8, ROWW], F32)
    P.dma("sp", [lambda e: e.dma_start(out=rbT[:], in_=rbT_d[:]), lambda e: e.dma_start(out=OH[:], in_=OH_d[:])], rbT, writes=[rbT, OH])
    for ch in range(6):
        pb_ = next_mm()
        mm(pb_[:, :], rbT[:, :], OH[:, ch * 512:(ch + 1) * 512], True, True, [rbT, OH], [pb_])
        evac(vA_sb[:, ch * 512:(ch + 1) * 512], pb_[0:8, :], [pb_], [vA_sb], eng="dve")
    vAd = Buf("vAd")
    P.dma("sp", [lambda e: e.dma_start(out=vA_d[:], in_=vA_sb[:])], vA_sb, reads=[vA_sb], writes=[vAd])
    P.release(m0)

    def alloc_tilework(double_h=True, double_x=True):
        w = {}
        x0 = P.sb("xbuf0", [128, 8, NT], F32)
        w["xbuf"] = [x0, P.sb("xbuf1", [128, 8, NT], F32) if double_x else x0]
        h0 = P.sb("hT0", [128, 8, NT], BF16)
        w["hT"] = [h0, P.sb("hT1", [128, 8, NT], BF16) if double_h else h0]
        w["sq"] = P.sb("sq", [128, 8, NT], BF16)
        w["rstd"] = P.sb("rstd", [128, NT], F32)
        w["tmpc"] = [P.sb("tmpc0", [128, NT], F32), P.sb("tmpc1", [128, NT], F32)]
        return w

    def load_x(w, i, t0):
        xb = w["xbuf"][i % 2]
        P.dma("sp", [lambda e: e.dma_start(out=xb[:], in_=dap(xT, t0, [[S, 128], [128 * S, 8], [1, NT]]))], xb, writes=[xb])
        return xb

    def rms_rstd(sqb, nch, rstd, nfeat, src_reads):
        for c in range(nch):
            mm(pss[:, :], ones_bf[:, :], sqb[:, c, :], c == 0, c == nch - 1, [ones_bf, sqb], [pss])
        P.add("act", lambda e: e.activation(out=rstd[:], in_=pss[:], func=AF.Ln, bias=eps_ap, scale=1.0 / nfeat), reads=[pss, misc], writes=[rstd])
        P.add("act", lambda e: e.activation(out=rstd[:], in_=rstd[:], func=AF.Exp, scale=-0.5), reads=[rstd], writes=[rstd])

    def make_h(w, xb, i, gv, sh, gvbuf, out=None):
        hT = w["hT"][i % 2]
        hfn = (lambda c: hT[:, c, :])
        if out is not None:
            hT, hfn = out
        sqb, rstd = w["sq"], w["rstd"]
        P.add("act", lambda e: e.activation(out=sqb[:], in_=xb[:], func=AF.Square), reads=[xb], writes=[sqb])
        rms_rstd(sqb, 8, rstd, float(D), None)
        for c in range(8):
            tc_ = w["tmpc"][c % 2]
            P.add("dve", lambda e, c=c, tc_=tc_: e.scalar_tensor_tensor(out=tc_[:], in0=xb[:, c, :], scalar=gv[:, c:c + 1], in1=rstd[:], op0=ALU.mult, op1=ALU.mult),
                  reads=[xb, rstd, gvbuf], writes=[tc_])
            P.add("act", lambda e, c=c, tc_=tc_: e.activation(out=hfn(c), in_=tc_[:], func=AF.Identity, bias=sh[:, c:c + 1], scale=1.0),
                  reads=[tc_, modT], writes=[hT])
        return hT

    def proj_fm(hT, wbuf, wap_fn, nk, out_fn, M=128, rhs_fn=None, ev_eng=None, scale=None, extra_reads=()):
        pb_ = next_mm()
        for k in range(nk):
            rhs = hT[:, k, :] if rhs_fn is None else rhs_fn(k)
            mm(pb_[0:M, :], wap_fn(k), rhs, k == 0, k == nk - 1, [hT, wbuf] + list(extra_reads), [pb_])
        out_fn(pb_)

    mA = P.mark()
    KT_A = P.sb("KT_A", [128, 4, 4096], BF16)
    V_A = P.sb("V_A", [128, 32, 8, 65], BF16)
    QT_A = P.sb("QT_A", [128, 4, TQ], BF16)
    QT_A1 = P.sb("QT_A1", [128, 4, TQ], BF16)
    P.add("pool", lambda e: e.memset(QT_A[64:128, :, :], 0.0), writes=[QT_A])
    P.add("pool", lambda e: e.memset(QT_A1[0:64, :, :], 0.0), writes=[QT_A1])
    mA1 = P.mark()
    wA = P.sb("wA", [128, 8, 1536], BF16)
    W = alloc_tilework(double_h=True, double_x=False)
    wst = [P.sb("wst0", [128, 8, 512], BF16), P.sb("wst1", [128, 8, 512], BF16)]
    for j3 in range(3):
        load_w(wA, wA[:, :, j3 * 512:(j3 + 1) * 512], w_in, j3 * 512, [[INW, 128], [128 * INW, 8], [1, 512]])
    P.add("pool", lambda e: e.memset(V_A[:, :, :, 64:65], 1.0), writes=[V_A])
    tiles = [(1, i) for i in range(4)] + [(0, i) for i in range(4)]
    for n, (s, i) in enumerate(tiles):
        xb = load_x(W, n, s * TQ + i * NT)
        hT = make_h(W, xb, n, gv1, sh1, gv1)
        if debug and n == 4:
            dump("hT0", hT[:, 0, :], hT, [128, 512], BF16)
        kp0 = (1 - s) * TQ + i * NT
        for hp in range(4):
            proj_fm(hT, wA, lambda k, hp=hp: wA[:, k, C_AK + hp * 128:C_AK + (hp + 1) * 128], 8,
                    lambda pb_, hp=hp: evac(KT_A[:, hp, kp0:kp0 + NT], pb_[:, :], [pb_], [KT_A]))
        for sub in range(4):
            pb_ = next_mm()
            for k in range(8):
                mm(pb_[:, :], hT[:, k, sub * 128:(sub + 1) * 128], wA[:, k, C_AV:C_AV + 512], k == 0, k == 7, [hT, wA], [pb_])
            kt = kp0 // 128 + sub
            evac(V_A[:, kt, :, 0:64], pb_[:, :].rearrange("p (h d) -> p h d", h=8), [pb_], [V_A])
        ada_slab(4 + n, wst[n % 2])
        if s == 0:
            for hp in range(4):
                proj_fm(hT, wA, lambda k, hp=hp: wA[:, k, C_AQ + hp * 128:C_AQ + (hp + 1) * 128], 8,
                        lambda pb_, hp=hp: (evac(QT_A[0:64, hp, i * NT:(i + 1) * NT], pb_[0:64, :], [pb_], [QT_A], scale=0.125),
                                            evac(QT_A1[64:128, hp, i * NT:(i + 1) * NT], pb_[64:128, :], [pb_], [QT_A1], scale=0.125)))
    P.add("dve", lambda e: e.tensor_tensor(out=modT[:, 16:48], in0=pmisc[:, 16:48], in1=tmp48[:, 16:48], op=ALU.add), reads=[pmisc, tmp48], writes=[modT])
    P.add("dve", lambda e: e.scalar_tensor_tensor(out=gv2[:], in0=modT[:, 32:40], scalar=1.0, in1=gtmp[:, 8:16], op0=ALU.add, op1=ALU.mult),
          reads=[modT, gtmp], writes=[gv2])
    dump("modT", modT[:], modT, [128, 48], F32)
    dump("KT_A", KT_A[:, 0, :], KT_A, [128, 4096], BF16)
    dump("QT_A", QT_A[:, 0, :], QT_A, [128, TQ], BF16)
    dump("V_A", V_A[:, :, 0, :], V_A, [128, 32, 65], BF16)

    P.release(mA1)
    TB = [P.sb("TB0", [128, TW], BF16), P.sb("TB1", [128, TW], BF16)]
    PT = [P.sb("PT0", [128, NT], BF16), P.sb("PT1", [128, NT], BF16), P.sb("PT2", [128, NT], BF16)]
    recs = P.sb("recs", [128, NT], F32)
    P.add("dve", lambda e: e.memset(recs[:], 0.0), writes=[recs])
    bcs = P.sb("bcs", [64, NT], F32)
    otmp = [P.sb("otmp0", [64, NT], BF16), P.sb("otmp1", [64, NT], BF16)]

    def finalize(ov, h, qb, dstT, n_odd, recs, bcs, otmp):
        hp, odd = h // 2, h % 2
        P.add("dve", lambda e: e.tensor_copy(out=recs[0:65, :], in_=ov[0:65, :]), reads=[ov], writes=[recs])
        mm(pmisc[:, :], ones_f[:, :], recs[:, :], True, True, [ones_f, recs], [pmisc])
        P.add("act", lambda e: e.activation(out=bcs[:], in_=pmisc[0:64, :], func=AF.Ln), reads=[pmisc], writes=[bcs])
        P.add("act", lambda e: e.activation(out=bcs[:], in_=bcs[:], func=AF.Exp, scale=-1.0), reads=[bcs], writes=[bcs])
        if not odd:
            P.add("dve", lambda e: e.tensor_tensor(out=dstT[0:64, hp, qb * NT:(qb + 1) * NT], in0=recs[0:64, :], in1=bcs[:], op=ALU.mult),
                  reads=[recs, bcs], writes=[dstT])
        else:
            ot = otmp[n_odd % 2]
            P.add("dve", lambda e: e.tensor_tensor(out=ot[:], in0=recs[0:64, :], in1=bcs[:], op=ALU.mult), reads=[recs, bcs], writes=[ot])
            P.dma("sp", [lambda e: e.dma_start(out=dstT[64:128, hp, qb * NT:(qb + 1) * NT], in_=ot[:])], ot, reads=[ot], writes=[dstT])


    class AttnPipe:
        def __init__(self, PTs, depth=2):
            self.scs = [psc[0], psc[1], pss]
            self.PTs = PTs
            self.depth = depth
            self.pend = []
            self.n = 0
            self.deferred = []

        def defer(self, fn, after=8):
            self.deferred.append([after, fn])

        def _tick(self):
            for d in self.deferred:
                d[0] -= 1
            while self.deferred and self.deferred[0][0] <= 0:
                self.deferred.pop(0)[1]()

        def tile(self, qk_fn, bias_ap, scale, pv_fn, bias_reads, mul_ap=None, mul_buf=None, PT2=None):
            self._tick()
            sc = self.scs[self.n % 3]
            pt = self.PTs[self.n % 3]
            qk_fn(sc)
            P.add("act", lambda e, sc=sc, pt=pt: e.activation(out=pt[:], in_=sc[:], func=AF.Exp, bias=bias_ap, scale=scale),
                  reads=[sc] + list(bias_reads), writes=[pt])
            if mul_ap is not None:
                pt2 = PT2[self.n % len(PT2)]
                P.add("dve", lambda e, pt=pt, pt2=pt2: e.tensor_tensor(out=pt2[:], in0=pt[:], in1=mul_ap, op=ALU.mult),
                      reads=[pt, mul_buf], writes=[pt2])
                pt = pt2
            self.n += 1
            self.pend.append((pv_fn, pt))
            assert self.depth < (len(PT2) if mul_ap is not None else len(self.PTs))
            if len(self.pend) > self.depth:
                f, p_ = self.pend.pop(0)
                f(p_)

        def flush(self):
            while self.pend:
                f, p_ = self.pend.pop(0)
                f(p_)
            while self.deferred:
                self.deferred.pop(0)[1]()

    n_odd = 0
    pipe = AttnPipe(PT, depth=3)
    PT2 = [P.sb("PTb%d" % i_, [128, NT], BF16) for i_ in range(4)]
    TE = [P.sb("TE0", [128, TW], BF16), P.sb("TE1", [128, TW], BF16)]
    for h in range(8):
        hp, pb0 = h // 2, 64 * (h % 2)
        tb = TB[h % 2]
        te = TE[h % 2]
        P.dma("pool", [lambda e, tb=tb, h=h: e.dma_start(out=tb[:], in_=dap(vA_d, h * ROWW, [[1, 128], [1, TW]]))], tb, reads=[vAd], writes=[tb])
        for c0 in range(0, TW, NT):
            cw = min(NT, TW - c0)
            pb_ = next_mm()
            mm(pb_[:, 0:cw], anti[:, :], tb[:, c0:c0 + cw], True, True, [anti, tb], [pb_])
            P.add("act", lambda e, pb_=pb_, te=te, c0=c0, cw=cw: e.activation(out=te[:, c0:c0 + cw], in_=pb_[:, 0:cw], func=AF.Exp),
                  reads=[pb_], writes=[te])
        for qb in range(4):
            ov = pov[cnt["ov"] % 2]
            cnt["ov"] += 1
            kts = list(range(4 * qb, 4 * qb + 20))
            for n_, kt in enumerate(kts):
                u = 16 + 4 * qb - kt
                qsel = QT_A if pb0 == 0 else QT_A1

                def qk(sc, kt=kt, qsel=qsel, hp=hp, qb=qb):
                    mm(sc[:, :], KT_A[:, hp, kt * 128:(kt + 1) * 128], qsel[:, hp, qb * NT:(qb + 1) * NT], True, True,
                       [KT_A, qsel], [sc])

                def pv(pt, kt=kt, h=h, qb=qb, ov=ov, first=(n_ == 0), last=(n_ == len(kts) - 1), n_odd=n_odd):
                    mm(ov[0:65, :], V_A[:, kt, h, 0:65], pt[:, :], first, last, [V_A, pt], [ov])
                    if last:
                        pipe.defer(lambda ov=ov, h=h, qb=qb, n_odd=n_odd: finalize(ov, h, qb, attnAT, n_odd, recs, bcs, otmp))

                bias_ap = pm[:, 1:2] if kt < 16 else zero_ap
                pipe.tile(qk, bias_ap, 1.0, pv, [pm, misc], mul_ap=te[:, 128 * (u + 3):128 * (u + 3) + NT], mul_buf=te, PT2=PT2)
            n_odd += h % 2
    pipe.flush()
    dump("attnAT", attnAT[:, 0, :], attnAT, [128, TQ], BF16)
    P.release(mA)

    mlaT = P.sb("mlaT", [128, 4, TQ], BF16)
    mB = P.mark()
    qT_all = P.sb("qT_all", [96, 8, TQ], BF16)

    def rope_tables(R, t0):
        posi, posf, ang, t1, ti, Ct, St = R["posi"], R["posf"], R["ang"], R["t1"], R["ti"], R["C"], R["S"]
        P.dma("sp", [lambda e: e.dma_start(out=posi[0:96, :], in_=dap(posr, t0, [[0, 96], [1, NT]]))], posi, writes=[posi])
        P.add("dve", lambda e: e.tensor_copy(out=posf[0:96, :], in_=posi[0:96, :]), reads=[posi], writes=[posf])
        P.add("dve", lambda e: e.tensor_scalar(out=ang[0:96, :], in0=posf[0:96, :], scalar1=misc[0:96, 0:1], scalar2=None, op0=ALU.mult),
              reads=[posf, misc], writes=[ang])
        for which, dst in ((0, St), (1, Ct)):
            if which == 1:
                P.add("dve", lambda e: e.tensor_scalar(out=ang[0:96, :], in0=ang[0:96, :], scalar1=math.pi / 2, scalar2=None, op0=ALU.add),
                      reads=[ang], writes=[ang])
            P.add("dve", lambda e: e.tensor_scalar(out=t1[0:96, :], in0=ang[0:96, :], scalar1=1.0 / (2 * math.pi), scalar2=None, op0=ALU.mult),
                  reads=[ang], writes=[t1])
            P.add("dve", lambda e: e.tensor_copy(out=ti[0:96, :], in_=t1[0:96, :]), reads=[t1], writes=[ti])
            P.add("dve", lambda e: e.tensor_copy(out=t1[0:96, :], in_=ti[0:96, :]), reads=[ti], writes=[t1])
            P.add("dve", lambda e: e.scalar_tensor_tensor(out=t1[0:96, :], in0=t1[0:96, :], scalar=-2 * math.pi, in1=ang[0:96, :], op0=ALU.mult, op1=ALU.add),
                  reads=[t1, ang], writes=[t1])
            P.add("dve", lambda e: e.tensor_scalar(out=t1[0:96, :], in0=t1[0:96, :], scalar1=-math.pi, scalar2=math.pi, op0=ALU.max, op1=ALU.min),
                  reads=[t1], writes=[t1])
            P.add("act", lambda e, dst=dst: e.activation(out=dst[0:96, :], in_=t1[0:96, :], func=AF.Sin), reads=[t1], writes=[dst])
        P.add("dve", lambda e: e.tensor_scalar(out=St[0:96, :], in0=St[0:96, :], scalar1=misc[0:96, 1:2], scalar2=None, op0=ALU.mult),
              reads=[St, misc], writes=[St])

    def alloc_rope():
        R = {}
        R["posi"] = P.sb("posi", [128, NT], I32)
        R["ti"] = R["posi"]
        for k in ("ang", "t1", "C", "S", "ra", "rb"):
            R[k] = P.sb(k, [128, NT], F32)
        R["posf"] = R["ra"]
        return R

    def apply_rope(R, pa, pb_, lo, hi, out_ap, out_buf):
        ra, rb = R["ra"], R["rb"]
        P.add("dve", lambda e: e.tensor_tensor(out=ra[lo:hi, :], in0=pa[lo:hi, :], in1=R["C"][lo:hi, :], op=ALU.mult), reads=[pa, R["C"]], writes=[ra])
        P.add("dve", lambda e: e.tensor_tensor(out=rb[lo:hi, :], in0=pb_[lo:hi, :], in1=R["S"][lo:hi, :], op=ALU.mult), reads=[pb_, R["S"]], writes=[rb])
        P.add("pool", lambda e: e.tensor_tensor(out=out_ap, in0=ra[lo:hi, :], in1=rb[lo:hi, :], op=ALU.add), reads=[ra, rb], writes=[out_buf])

    mB2 = P.mark()
    W = alloc_tilework(double_h=True, double_x=True)
    R = alloc_rope()
    wcq = P.sb("wcq", [128, 8, 768], BF16)
    wuq = P.sb("wuq", [128, 6, 768], BF16)
    wuqs = P.sb("wuqs", [128, 6, 768], BF16)
    cqf = P.sb("cqf", [128, 6, NT], F32)
    cqsq = P.sb("cqsq", [128, 6, NT], BF16)
    cqn = P.sb("cqn", [128, 6, NT], BF16)
    rstq = P.sb("rstq", [128, NT], F32)
    load_w(wcq, wcq[:, :, 0:384], w_in, C_CQ, [[INW, 128], [128 * INW, 8], [1, 384]])
    load_w(wcq, wcq[:, :, 384:768], w_in, C_CQ + 384, [[INW, 128], [128 * INW, 8], [1, 384]])
    load_w(wuq, wuq[:, :, :], w_uq, 0, [[768, 128], [128 * 768, 6], [1, 768]])
    load_w(wuqs, wuqs[:, :, :], w_uqs, 0, [[768, 128], [128 * 768, 6], [1, 768]])
    for i in range(4):
        xb = load_x(W, i, i * NT)
        hT = make_h(W, xb, i, gv1, sh1, gv1)
        rope_tables(R, i * NT)
        for m in range(6):
            def ev(pb_, m=m):
                P.add("act", lambda e: e.activation(out=cqf[:, m, :], in_=pb_[:, :], func=AF.Copy), reads=[pb_], writes=[cqf])
                P.add("act", lambda e: e.activation(out=cqsq[:, m, :], in_=pb_[:, :], func=AF.Square), reads=[pb_], writes=[cqsq])
            proj_fm(hT, wcq, lambda k, m=m: wcq[:, k, m * 128:(m + 1) * 128], 8, ev)
        rms_rstd(cqsq, 6, rstq, 768.0, None)
        for m in range(6):
            P.add("dve", lambda e, m=m: e.scalar_tensor_tensor(out=cqn[:, m, :], in0=cqf[:, m, :], scalar=gq[:, m:m + 1], in1=rstq[:], op0=ALU.mult, op1=ALU.mult),
                  reads=[cqf, rstq, gq], writes=[cqn])
        for h in range(8):
            pa, pb_ = pmm[0], pmm[1]
            for k in range(6):
                mm(pa[0:96, :], wuq[:, k, h * 96:(h + 1) * 96], cqn[:, k, :], k == 0, k == 5, [wuq, cqn], [pa])
            for k in range(6):
                mm(pb_[0:96, :], wuqs[:, k, h * 96:(h + 1) * 96], cqn[:, k, :], k == 0, k == 5, [wuqs, cqn], [pb_])
            apply_rope(R, pa, pb_, 0, 96, qT_all[0:96, h, i * NT:(i + 1) * NT], qT_all)
    dump("qT0", qT_all[:, 0, :], qT_all, [96, TQ], BF16)
    P.release(mB2)

    latT = P.sb("latT", [128, 2, S], BF16)
    KT = P.sb("KT", [96, S], BF16)
    mB1 = P.mark()
    W = alloc_tilework(double_h=False, double_x=True)
    R = alloc_rope()
    wkv = P.sb("wkv", [128, 8, 256], BF16)
    wkr = P.sb("wkr", [128, 8, 192], BF16)
    ckf = P.sb("ckf", [128, 2, NT], F32)
    cksq = P.sb("cksq", [128, 2, NT], BF16)
    rstk = P.sb("rstk", [128, NT], F32)
    load_w(wkv, wkv[:, :, :], w_in, C_CKV, [[INW, 128], [128 * INW, 8], [1, 256]])
    load_w(wkr, wkr[:, :, :], w_kr, 0, [[192, 128], [128 * 192, 8], [1, 192]])
    for n in range(16):
        t0 = n * NT
        xb = load_x(W, n, t0)
        hT = make_h(W, xb, n, gv1, sh1, gv1)
        rope_tables(R, t0)
        for m in range(2):
            def ev(pb_, m=m):
                P.add("act", lambda e: e.activation(out=ckf[:, m, :], in_=pb_[:, :], func=AF.Copy), reads=[pb_], writes=[ckf])
                P.add("act", lambda e: e.activation(out=cksq[:, m, :], in_=pb_[:, :], func=AF.Square), reads=[pb_], writes=[cksq])
            proj_fm(hT, wkv, lambda k, m=m: wkv[:, k, m * 128:(m + 1) * 128], 8, ev)
        rms_rstd(cksq, 2, rstk, 256.0, None)
        for m in range(2):
            P.add("dve", lambda e, m=m, t0=t0: e.scalar_tensor_tensor(out=latT[:, m, t0:t0 + NT], in0=ckf[:, m, :], scalar=gkv[:, m:m + 1], in1=rstk[:], op0=ALU.mult, op1=ALU.mult),
                  reads=[ckf, rstk, gkv], writes=[latT])
        pa, pb_ = pmm[0], pmm[1]
        for k in range(8):
            mm(pa[0:96, :], wkr[:, k, 0:96], hT[:, k, :], k == 0, k == 7, [wkr, hT], [pa])
        for k in range(8):
            mm(pb_[0:96, :], wkr[:, k, 96:192], hT[:, k, :], k == 0, k == 7, [wkr, hT], [pb_])
        apply_rope(R, pa, pb_, 64, 96, KT[64:96, t0:t0 + NT], KT)
    dump("latT", latT[:, 0, 0:TQ], latT, [128, TQ], BF16)
    P.release(mB1)

    mB3 = P.mark()
    wukv = P.sb("wukv", [128, 2, 1024], BF16)
    Vh = [P.sb("Vh0", [128, 64, 65], BF16), P.sb("Vh1", [128, 64, 65], BF16)]
    PT = [P.sb("PT0", [128, NT], BF16), P.sb("PT1", [128, NT], BF16), P.sb("PT2", [128, NT], BF16)]
    recs = P.sb("recs", [128, NT], F32)
    P.add("dve", lambda e: e.memset(recs[:], 0.0), writes=[recs])
    bcs = P.sb("bcs", [64, NT], F32)
    otmp = [P.sb("otmp0", [64, NT], BF16), P.sb("otmp1", [64, NT], BF16)]
    load_w(wukv, wukv[:, :, :], w_ukv, 0, [[1024, 128], [128 * 1024, 2], [1, 1024]])
    for vb in Vh:
        P.add("pool", lambda e, vb=vb: e.memset(vb[:, :, 64:65], 1.0), writes=[vb])
    n_odd = 0
    pipe = AttnPipe(PT)
    for h in range(8):
        vb = Vh[h % 2]
        for n in range(16):
            pb_ = next_mm()
            for k in range(2):
                mm(pb_[:, :], wukv[:, k, h * 128:h * 128 + 128], latT[:, k, n * NT:(n + 1) * NT], k == 0, k == 1, [wukv, latT], [pb_])
            evac(KT[0:64, n * NT:(n + 1) * NT], pb_[0:64, :], [pb_], [KT])
        for g in range(8):
            pb_ = next_mm()
            for tt in range(8):
                kt = g * 8 + tt
                for k in range(2):
                    mm(pb_[:, tt * 64:(tt + 1) * 64], latT[:, k, kt * 128:(kt + 1) * 128], wukv[:, k, h * 128 + 64:h * 128 + 128], k == 0, k == 1,
                       [wukv, latT], [pb_])
            evac(vb[:, g * 8:(g + 1) * 8, 0:64], pb_[:, :].rearrange("p (t d) -> p t d", t=8), [pb_], [vb])
        if debug and h == 0:
            dump("KT0", KT[:, 0:TQ], KT, [96, TQ], BF16)
            dump("Vh0", vb[:, :, :], vb, [128, 64, 65], BF16)
        for qb in range(4):
            ov = pov[cnt["ov"] % 2]
            cnt["ov"] += 1
            klist = [(0, kt) for kt in range(4 * (qb + 1))] + [(s, kt) for s in (1, 2, 3) for kt in range(16)]
            for n_, (s, kt) in enumerate(klist):
                g = s * 16 + kt
                diag = (s == 0 and kt >= 4 * qb)

                def qk(sc, g=g, diag=diag, kt=kt, qb=qb, h=h):
                    mm(sc[:, :], KT[0:96, g * 128:(g + 1) * 128], qT_all[0:96, h, qb * NT:(qb + 1) * NT], True, not diag, [KT, qT_all], [sc])
                    if diag:
                        wv = kt - 4 * qb
                        mm(sc[:, :], ident[:, :], MT[:, 128 * (3 - wv):128 * (3 - wv) + NT], False, True, [ident, MT], [sc])

                def pv(pt, g=g, h=h, qb=qb, ov=ov, vb=vb, first=(n_ == 0), last=(n_ == len(klist) - 1), n_odd=n_odd):
                    mm(ov[0:65, :], vb[:, g, 0:65], pt[:, :], first, last, [vb, pt], [ov])
                    if last:
                        pipe.defer(lambda ov=ov, h=h, qb=qb, n_odd=n_odd: finalize(ov, h, qb, mlaT, n_odd, recs, bcs, otmp))

                pipe.tile(qk, pm[:, s:s + 1], 96.0 ** -0.5, pv, [pm])
            n_odd += h % 2
    pipe.flush()
    dump("mlaT", mlaT[:, 0, :], mlaT, [128, TQ], BF16)
    P.release(mB)

    top0 = P.top
    wbufs = [P.sb_top("wb0", [128, 4096], BF16), P.sb_top("wb1", [128, 4096], BF16), P.sb_top("wb2", [128, 4096], BF16)]
    mCm = P.mark()
    mergedT = P.sb("mergedT", [128, 8, TQ], BF16)
    mCh = P.mark()
    hTo = P.sb("hTo", [128, 8, TQ], BF16)
    mC0 = P.mark()
    W = alloc_tilework()
    for i in range(4):
        xb = load_x(W, i, i * NT)
        make_h(W, xb, i, gv1, sh1, gv1, out=(hTo, lambda c, i=i: hTo[:, c, i * NT:(i + 1) * NT]))
    P.release(mC0)

    def next_w():
        b = wbufs[cnt["w"] % 3]
        cnt["w"] += 1
        return b

    class Prefetch:
        def __init__(self, thunks, depth=2):
            self.th, self.h, self.nxt, self.depth = thunks, {}, 0, depth

        def get(self, k):
            while self.nxt <= min(k + self.depth, len(self.th) - 1):
                self.h[self.nxt] = self.th[self.nxt]()
                self.nxt += 1
            return self.h.pop(k)

    mC1 = P.mark()
    sig = [P.sb("sig0", [128, NT], F32), P.sb("sig1", [128, NT], F32)]
    prod = [P.sb("prod0", [128, NT], F32), P.sb("prod1", [128, NT], F32)]
    def c1_load(m):
        wb = next_w()
        wgv = wb[:, 0:2048].rearrange("p (k g c) -> p k g c", k=8, g=2)
        wuv = wb[:, 2048:3072].rearrange("p (k g c) -> p k g c", k=4, g=2)
        load_w(wb, wgv[:, :, 0, :], w_in, C_GA + m * 128, [[INW, 128], [128 * INW, 8], [1, 128]])
        load_w(wb, wgv[:, :, 1, :], w_in, C_GB + m * 128, [[INW, 128], [128 * INW, 8], [1, 128]])
        load_w(wb, wuv[:, :, 0, :], w_up_a, m * 128, [[D, 128], [128 * D, 4], [1, 128]])
        load_w(wb, wuv[:, :, 1, :], w_up_b, m * 128, [[D, 128], [128 * D, 4], [1, 128]])
        return wb, wgv, wuv

    pf = Prefetch([lambda m=m: c1_load(m) for m in range(8)], depth=2)
    for m in range(8):
        wg, wgv, wuv = pf.get(m)
        wu = wg
        for i in range(4):
            tsl = slice(i * NT, (i + 1) * NT)
            for gidx, srcT in ((0, attnAT), (1, mlaT)):
                pg = next_mm()
                for k in range(8):
                    mm(pg[:, :], wgv[:, k, gidx, :], hTo[:, k, tsl], k == 0, k == 7, [wg, hTo], [pg])
                sg = sig[gidx]
                P.add("act", lambda e, sg=sg, pg=pg: e.activation(out=sg[:], in_=pg[:], func=AF.Sigmoid), reads=[pg], writes=[sg])
                py = next_mm()
                for k in range(4):
                    mm(py[:, :], wuv[:, k, gidx, :], srcT[:, k, tsl], k == 0, k == 3, [wu, srcT], [py])
                pr = prod[gidx]
                P.add("dve", lambda e, pr=pr, py=py, sg=sg: e.tensor_tensor(out=pr[:], in0=py[:], in1=sg[:], op=ALU.mult), reads=[py, sg], writes=[pr])
            P.add("pool", lambda e, m=m, tsl=tsl: e.tensor_tensor(out=mergedT[:, m, tsl], in0=prod[0][:], in1=prod[1][:], op=ALU.add),
                  reads=[prod[0], prod[1]], writes=[mergedT])
    dump("mergedT", mergedT[:, 0, :], mergedT, [128, TQ], BF16)
    P.release(mCh)

    x1T = P.sb_top("x1T", [128, 8, TQ], F32)
    mC2 = P.mark()
    xres = [P.sb("xres0", [128, NT], F32), P.sb("xres1", [128, NT], F32)]
    nx = 0
    def c2_load(m):
        wo = next_w()
        wov = wo[:, 0:1024].rearrange("p (k c) -> p k c", k=8)
        load_w(wo, wov, w_o, m * 128, [[D, 128], [128 * D, 8], [1, 128]])
        return wo, wov

    pf = Prefetch([lambda m=m: c2_load(m) for m in range(8)], depth=2)
    for m in range(8):
        wo, wov = pf.get(m)
        for i in range(4):
            tsl = slice(i * NT, (i + 1) * NT)
            xr = xres[nx % 2]
            nx += 1
            P.dma("sp", [lambda e, xr=xr, m=m, i=i: e.dma_start(out=xr[:], in_=dap(xT, m * 128 * S + i * NT, [[S, 128], [1, NT]]))], xr, writes=[xr])
            po = next_mm()
            for k in range(8):
                mm(po[:, :], wov[:, k, :], mergedT[:, k, tsl], k == 0, k == 7, [wo, mergedT], [po])
            P.add("dve", lambda e, po=po, xr=xr, m=m, tsl=tsl: e.scalar_tensor_tensor(out=x1T[:, m, tsl], in0=po[:], scalar=gt1[:, m:m + 1], in1=xr[:], op0=ALU.mult, op1=ALU.add),
                  reads=[po, xr, modT], writes=[x1T])
    dump("x1T", x1T[:, 0, :], x1T, [128, TQ], F32)
    P.release(mA)

    h2T = P.sb_top("h2T", [128, 8, TQ], BF16)
    mC3 = P.mark()
    sqb = P.sb("sq", [128, 8, NT], BF16)
    rstd = P.sb("rstd", [128, NT], F32)
    tmpc = [P.sb("tmpc0", [128, NT], F32), P.sb("tmpc1", [128, NT], F32)]
    for i in range(4):
        tsl = slice(i * NT, (i + 1) * NT)
        P.add("act", lambda e, tsl=tsl, sqb=sqb: e.activation(out=sqb[:], in_=x1T[:, :, tsl], func=AF.Square), reads=[x1T], writes=[sqb])
        rms_rstd(sqb, 8, rstd, float(D), None)
        for c in range(8):
            tc_ = tmpc[c % 2]
            P.add("dve", lambda e, c=c, tc_=tc_, tsl=tsl, rstd=rstd: e.scalar_tensor_tensor(out=tc_[:], in0=x1T[:, c, tsl], scalar=gv2[:, c:c + 1], in1=rstd[:], op0=ALU.mult, op1=ALU.mult),
                  reads=[x1T, rstd, gv2], writes=[tc_])
            P.add("act", lambda e, c=c, tc_=tc_, tsl=tsl: e.activation(out=h2T[:, c, tsl], in_=tc_[:], func=AF.Identity, bias=sh2[:, c:c + 1], scale=1.0),
                  reads=[tc_, modT], writes=[h2T])
    P.release(mC3)

    mC4 = P.mark()
    actT = P.sb("actT", [128, 22, 1024], BF16)
    sil = [P.sb("sil0", [128, NT], F32), P.sb("sil1", [128, NT], F32)]
    def gu_load(f):
        wgu = next_w()
        wguv = wgu[:, 0:2048].rearrange("p (k g c) -> p k g c", k=8, g=2)
        load_w(wgu, wguv[:, :, 0, :], w_gate, f * 128, [[DFF, 128], [128 * DFF, 8], [1, 128]])
        load_w(wgu, wguv[:, :, 1, :], w_up, f * 128, [[DFF, 128], [128 * DFF, 8], [1, 128]])
        return wgu, wguv

    def d_load(m):
        wd = next_w()
        wdv = wd[:, 0:2816].rearrange("p (k c) -> p k c", k=22)
        load_w(wd, wdv, w_down, m * 128, [[D, 128], [128 * D, 22], [1, 128]])
        return wd, wdv

    seq = []
    for half in range(2):
        seq += [lambda f=f: gu_load(f) for f in range(22)] + [lambda m=m: d_load(m) for m in range(8)]
    pf = Prefetch(seq, depth=2)
    for half in range(2):
        for f in range(22):
            wgu, wguv = pf.get(half * 30 + f)
            for i in range(2):
                tsl = slice(half * 1024 + i * NT, half * 1024 + (i + 1) * NT)
                pg = next_mm()
                for k in range(8):
                    mm(pg[:, :], wguv[:, k, 0, :], h2T[:, k, tsl], k == 0, k == 7, [wgu, h2T], [pg])
                sl_ = sil[i]
                P.add("act", lambda e, sl_=sl_, pg=pg: e.activation(out=sl_[:], in_=pg[:], func=AF.Silu), reads=[pg], writes=[sl_])
                pu = next_mm()
                for k in range(8):
                    mm(pu[:, :], wguv[:, k, 1, :], h2T[:, k, tsl], k == 0, k == 7, [wgu, h2T], [pu])
                P.add("dve", lambda e, sl_=sl_, pu=pu, f=f, i=i: e.tensor_tensor(out=actT[:, f, i * NT:(i + 1) * NT], in0=pu[:], in1=sl_[:], op=ALU.mult),
                      reads=[pu, sl_], writes=[actT])
        for m in range(8):
            wd, wdv = pf.get(half * 30 + 22 + m)
            for i in range(2):
                tsl = slice(half * 1024 + i * NT, half * 1024 + (i + 1) * NT)
                pd = next_mm()
                for k in range(22):
                    mm(pd[:, :], wdv[:, k, :], actT[:, k, i * NT:(i + 1) * NT], k == 0, k == 21, [wd, actT], [pd])
                P.add("dve", lambda e, pd=pd, m=m, tsl=tsl: e.scalar_tensor_tensor(out=x1T[:, m, tsl], in0=pd[:], scalar=gt2[:, m:m + 1], in1=x1T[:, m, tsl], op0=ALU.mult, op1=ALU.add),
                      reads=[pd, x1T, modT], writes=[x1T])

    P.release(mC4)
    P.top += 32768
    sqb = P.sb("sq", [128, 8, NT], BF16)
    rstd = P.sb("rstd", [128, NT], F32)
    ob = [P.sb("ob0", [128, 8, NT], F32), P.sb("ob1", [128, 8, NT], F32)]
    for i in range(4):
        tsl = slice(i * NT, (i + 1) * NT)
        o_ = ob[i % 2]
        P.add("act", lambda e, tsl=tsl, sqb=sqb: e.activation(out=sqb[:], in_=x1T[:, :, tsl], func=AF.Square), reads=[x1T], writes=[sqb])
        rms_rstd(sqb, 8, rstd, float(D), None)
        for c in range(8):
            P.add("dve", lambda e, c=c, o_=o_, tsl=tsl, rstd=rstd: e.scalar_tensor_tensor(out=o_[:, c, :], in0=x1T[:, c, tsl], scalar=gfin[:, c:c + 1], in1=rstd[:], op0=ALU.mult, op1=ALU.mult),
                  reads=[x1T, rstd, gfin], writes=[o_])
        P.dma("sp", [lambda e, o_=o_, i=i: e.dma_start(out=dap(outT, i * NT, [[TQ, 128], [128 * TQ, 8], [1, NT]]), in_=o_[:])], o_, reads=[o_])
    stats = P.emit(final_wait_bufs=ob + dumped)
    return nc, stats, dbg


def _t5_bucket_np(dist):
    max_exact = 16
    d = np.maximum(dist, 1).astype(np.float32)
    log_b = max_exact + (np.log(d / np.float32(max_exact)) / np.float32(math.log(2048 / max_exact)) * np.float32(32 - max_exact)).astype(np.int32)
    log_b = np.minimum(log_b, 31)
    return np.where(dist < max_exact, dist, log_b)


def _constants():
    OH = np.zeros((128, ROWW), np.float32)
    e = np.arange(ROWW)
    delta = e - 511
    mult = np.zeros(ROWW, np.int64)
    for (w, d) in ((128, 1), (512, 4), (2048, 16)):
        mult += ((delta >= 0) & (delta <= w) & (delta % d == 0)).astype(np.int64)
    valid = mult > 0
    bucket = _t5_bucket_np(np.clip(delta, 0, 2048).astype(np.int32))
    OH[bucket[valid], e[valid]] = 1.0
    OH[32, :] = NEG
    OH[32, valid] = np.log(mult[valid].astype(np.float64)).astype(np.float32)
    ident = np.eye(128, dtype=np.float32)
    anti = np.ascontiguousarray(ident[::-1])
    kk = np.arange(128)[:, None]
    cc = np.arange(896)[None, :]
    MT = np.where(cc - 384 - kk >= 0, 0.0, NEG).astype(np.float32)
    misc = np.zeros((128, 4), np.float32)
    half = 16
    freqs = (10000.0 ** (-np.arange(half, dtype=np.float32) / np.float32(half))).astype(np.float32)
    for r in range(32):
        misc[64 + r, 0] = freqs[r % 16]
        misc[64 + r, 1] = -1.0 if r < 16 else 1.0
    misc[:, 2] = EPS
    return OH, ident, anti, MT, misc


def _colT(v, n):
    return np.ascontiguousarray(v.reshape(n, 128).T)


def make_in_maps(x, c, positions, rel_bias, w_ada, b_ada, g_mix, w_in, g_q_lora, w_uq, g_kv_lora, w_ukv,
                 w_up_a, w_up_b, w_o, g_ffn, w_gate, w_up, w_down, g_final):
    OH, ident, anti, MT, misc = _constants()
    w_in0 = np.ascontiguousarray(w_in[0])
    kr = w_in0[:, C_KR:C_KR + 32]
    w_kr = np.zeros((D, 2, 96), np.float32)
    w_kr[:, 0, 64:96] = kr
    w_kr[:, 1, 64:96] = np.concatenate([kr[:, 16:32], kr[:, 0:16]], axis=1)
    wuq = w_uq[0]
    wuqs = wuq.copy()
    wuqs[:, :, 64:80] = wuq[:, :, 80:96]
    wuqs[:, :, 80:96] = wuq[:, :, 64:80]
    rbT = np.zeros((128, 128), np.float32)
    rbT[0:32, 0:8] = rel_bias.T
    rbT[32, 0:8] = 1.0
    shared = {
        "rbT": rbT, "OH": OH, "w_ada": np.ascontiguousarray(w_ada[0]), "b_adaT": _colT(b_ada[0], 48),
        "g_mixT": _colT(g_mix[0], 8), "w_in": w_in0, "w_kr": w_kr, "g_qT": _colT(g_q_lora[0], 6),
        "w_uq": np.ascontiguousarray(wuq.reshape(768, 768)), "w_uqs": np.ascontiguousarray(wuqs.reshape(768, 768)),
        "g_kvT": _colT(g_kv_lora[0], 2), "w_ukv": np.ascontiguousarray(w_ukv[0].reshape(256, 1024)),
        "w_up_a": np.ascontiguousarray(w_up_a[0]), "w_up_b": np.ascontiguousarray(w_up_b[0]), "w_o": np.ascontiguousarray(w_o[0]),
        "g_ffnT": _colT(g_ffn[0], 8), "w_gate": np.ascontiguousarray(w_gate[0]), "w_up": np.ascontiguousarray(w_up[0]),
        "w_down": np.ascontiguousarray(w_down[0]), "g_finT": _colT(g_final, 8),
        "ident": ident, "anti": anti, "MT": MT, "misc": misc,
    }
    in_maps = []
    for core in range(8):
        b, j = core // 4, core % 4
        order = [(j - s) % 4 for s in range(4)]
        xb = x[b]
        xs = np.concatenate([xb[ch * TQ:(ch + 1) * TQ] for ch in order], axis=0)
        pos = np.concatenate([positions[b, ch * TQ:(ch + 1) * TQ] for ch in order]).astype(np.int32).reshape(1, S)
        pmv = np.zeros((128, 4), np.float32)
        for s in range(1, 4):
            if j - s < 0:
                pmv[:, s] = NEG
        m = dict(shared)
        m["xT"] = np.ascontiguousarray(xs.T)
        m["posr"] = pos
        m["cT"] = _colT(c[b], 8)
        m["pm"] = pmv
        in_maps.append(m)
    return in_maps


_CACHE = {}


def kernel(**inputs):
    inputs = {k: np.asarray(v) for k, v in inputs.items()}
    if "nc" not in _CACHE:
        _CACHE["nc"] = build(False)[0]
    nc = _CACHE["nc"]
    in_maps = make_in_maps(**inputs)
    res = run_bass_kernel_spmd(nc, in_maps, core_ids=list(range(8)))
    out = np.empty((2, S, D), np.float32)
    for core in range(8):
        b, j = core // 4, core % 4
        out[b, j * TQ:(j + 1) * TQ, :] = res.results[core]["outT"].T
    return out
```
